# Optimizing a Trainium2 kernel written in Bass

```python
import jax, jax.numpy as jnp
from jax import lax
import numpy as np

D_MODEL = 2048
BATCH = 4
SEQ = 4096
DEPTH = 2

D_MIX = D_MODEL
ATTN_WIDTH = D_MIX // 2
HGRN_WIDTH = D_MIX - ATTN_WIDTH
HEAD_DIM = 64
N_Q_HEADS = ATTN_WIDTH // HEAD_DIM
N_KV_HEADS = 4
Q_PER_KV = N_Q_HEADS // N_KV_HEADS
WINDOW = 128
ATTN_BLOCK = WINDOW
ROPE_THETA = 10000.0
MASK_VALUE = -1e30
HGRN_EXPAND = 128
HGRN_HEADS = HGRN_WIDTH // HGRN_EXPAND
HGRN_VDIM = HGRN_WIDTH // HGRN_HEADS
HGRN_CHUNK = 64
D_FF = ((8 * D_MODEL // 3 + 255) // 256) * 256
D_PLE = 256
RMS_EPS = 1e-6
KV_WIDTH = N_KV_HEADS * HEAD_DIM
IN_SIZES = (ATTN_WIDTH, KV_WIDTH, KV_WIDTH, HGRN_WIDTH, HGRN_WIDTH, HGRN_WIDTH, HGRN_WIDTH)
SPLIT_POINTS = tuple(int(v) for v in np.cumsum(IN_SIZES)[:-1])
D_IN = sum(IN_SIZES)

kernel_name = "hymba_swa_sink_hgrn2_sandwich_ple"


def rms_norm(x, gain):
    xf = x.astype(jnp.float32)
    y = xf * lax.rsqrt(jnp.mean(xf * xf, axis=-1, keepdims=True) + RMS_EPS)
    return (y * gain.astype(jnp.float32)).astype(x.dtype)


def rope(x, positions):
    half = HEAD_DIM // 2
    inv_freq = ROPE_THETA ** (-jnp.arange(half, dtype=jnp.float32) / half)
    ang = positions.astype(jnp.float32)[..., None] * inv_freq
    cos = jnp.cos(ang)[:, :, None, :]
    sin = jnp.sin(ang)[:, :, None, :]
    xf = x.astype(jnp.float32)
    x1, x2 = xf[..., :half], xf[..., half:]
    out = jnp.concatenate([x1 * cos - x2 * sin, x2 * cos + x1 * sin], axis=-1)
    return out.astype(x.dtype)


def sliding_window_attention(q, k, v, sinks):
    B, S = q.shape[0], q.shape[1]
    nb = S // ATTN_BLOCK
    L = ATTN_BLOCK
    qb = q.reshape(B, nb, L, N_KV_HEADS, Q_PER_KV, HEAD_DIM)
    kb = k.reshape(B, nb, L, N_KV_HEADS, HEAD_DIM)
    vb = v.reshape(B, nb, L, N_KV_HEADS, HEAD_DIM)
    prev = lambda t: jnp.concatenate([jnp.zeros_like(t[:, :1]), t[:, :-1]], axis=1)
    kk = jnp.concatenate([prev(kb), kb], axis=2)
    vv = jnp.concatenate([prev(vb), vb], axis=2)
    scores = jnp.einsum('bnqhgd,bnkhd->bnhgqk', qb, kk,
                        preferred_element_type=jnp.float32) * (HEAD_DIM ** -0.5)
    qi = jnp.arange(L)[:, None] + L
    ki = jnp.arange(2 * L)[None, :]
    rel = qi - ki
    band = (rel >= 0) & (rel < WINDOW)
    valid = band[None] & ((jnp.arange(nb)[:, None, None] > 0) | (ki >= L)[None])
    scores = jnp.where(valid[None, :, None, None], scores, MASK_VALUE)
    sink = sinks.astype(jnp.float32).reshape(N_KV_HEADS, Q_PER_KV)[None, None, :, :, None, None]
    m = jnp.maximum(jnp.max(scores, axis=-1, keepdims=True), sink)
    e = jnp.exp(scores - m)
    probs = e / (jnp.sum(e, axis=-1, keepdims=True) + jnp.exp(sink - m))
    out = jnp.einsum('bnhgqk,bnkhd->bnqhgd', probs.astype(vv.dtype), vv)
    return out.reshape(B, S, N_Q_HEADS * HEAD_DIM)


def hgrn2_chunkwise(q, k, v, log_f):
    B, S = q.shape[0], q.shape[1]
    nc = S // HGRN_CHUNK
    C = HGRN_CHUNK

    def to_chunks(t):
        return t.astype(jnp.float32).reshape(B, nc, C, HGRN_HEADS, t.shape[-1]).transpose(1, 0, 3, 2, 4)

    qc, kc, vc, gc = to_chunks(q), to_chunks(k), to_chunks(v), to_chunks(log_f)
    causal = jnp.tril(jnp.ones((C, C), dtype=bool))

    def step(state, inp):
        q_, k_, v_, g_ = inp
        b = jnp.cumsum(g_, axis=2)
        diff = b[:, :, :, None, :] - b[:, :, None, :, :]
        decay = jnp.exp(jnp.where(causal[None, None, :, :, None], diff, MASK_VALUE))
        a = jnp.einsum('bhtk,bhsk,bhtsk->bhts', q_, k_, decay)
        o = jnp.einsum('bhts,bhsv->bhtv', a, v_) + \
            jnp.einsum('bhtk,bhkv->bhtv', q_ * jnp.exp(b), state)
        b_last = b[:, :, -1:, :]
        new_state = state * jnp.exp(b_last)[:, :, 0, :, None] + \
            jnp.einsum('bhsk,bhsv->bhkv', k_ * jnp.exp(b_last - b), v_)
        return new_state, o

    s0 = jnp.zeros((B, HGRN_HEADS, HGRN_EXPAND, HGRN_VDIM), jnp.float32)
    _, o = lax.scan(step, s0, (qc, kc, vc, gc))
    return o.transpose(1, 0, 3, 2, 4).reshape(B, S, HGRN_HEADS, HGRN_VDIM)


def hybrid_mixer(h, positions, w_in, sinks, lb, attn_gain, hgrn_gain, w_out):
    B, S = h.shape[0], h.shape[1]
    proj = h @ w_in
    q, k, v, hq, hf, hi, hg = jnp.split(proj, SPLIT_POINTS, axis=-1)
    q = rope(q.reshape(B, S, N_Q_HEADS, HEAD_DIM), positions)
    k = rope(k.reshape(B, S, N_KV_HEADS, HEAD_DIM), positions)
    v = v.reshape(B, S, N_KV_HEADS, HEAD_DIM)
    attn = rms_norm(sliding_window_attention(q, k, v, sinks), attn_gain)
    z = hf.astype(jnp.float32)
    lbf = lb.astype(jnp.float32)
    f = lbf + (1.0 - lbf) * jax.nn.sigmoid(z)
    log_f = jnp.log(f)
    k_in = (1.0 - lbf) * jax.nn.sigmoid(-z)
    hq_ = jax.nn.silu(hq.astype(jnp.float32))
    shp = (B, S, HGRN_HEADS, HGRN_EXPAND)
    o = hgrn2_chunkwise(hq_.reshape(shp), k_in.reshape(shp),
                        hi.reshape(B, S, HGRN_HEADS, HGRN_VDIM), log_f.reshape(shp))
    o = rms_norm(o, hgrn_gain.reshape(HGRN_HEADS, HGRN_VDIM)).reshape(B, S, HGRN_WIDTH)
    hgrn = (o * jax.nn.silu(hg.astype(jnp.float32))).astype(h.dtype)
    return jnp.concatenate([attn, hgrn], axis=-1) @ w_out


def setup_inputs(seed: int = 0) -> dict:
    key = jax.random.key(seed)
    ks = jax.random.split(key, 24)
    f32 = jnp.float32
    nrm = lambda k, shape, scale: jax.random.normal(k, shape, f32) * scale
    gain = lambda k, shape: 1.0 + 0.05 * jax.random.normal(k, shape, f32)
    offsets = jax.random.randint(ks[2], (BATCH, 1), 0, 1024, dtype=jnp.int32)
    positions = offsets + jnp.arange(SEQ, dtype=jnp.int32)[None, :]
    return {
        "x": nrm(ks[0], (BATCH, SEQ, D_MODEL), 1.0),
        "p": nrm(ks[1], (DEPTH, BATCH, SEQ, D_PLE), 1.0),
        "positions": positions,
        "w_in": nrm(ks[3], (DEPTH, D_MODEL, D_IN), D_MODEL ** -0.5),
        "attn_sinks": nrm(ks[4], (DEPTH, N_Q_HEADS), 1.0),
        "hgrn_lb_logits": nrm(ks[5], (DEPTH, HGRN_WIDTH), 0.5),
        "attn_out_gain": gain(ks[6], (DEPTH, ATTN_WIDTH)),
        "hgrn_out_gain": gain(ks[7], (DEPTH, HGRN_WIDTH)),
        "w_out": nrm(ks[8], (DEPTH, D_MIX, D_MODEL), D_MIX ** -0.5),
        "pre_mix_gain": gain(ks[9], (DEPTH, D_MODEL)),
        "post_mix_gain": gain(ks[10], (DEPTH, D_MODEL)),
        "pre_ffn_gain": gain(ks[11], (DEPTH, D_MODEL)),
        "post_ffn_gain": gain(ks[12], (DEPTH, D_MODEL)),
        "w_ffn_gate": nrm(ks[13], (DEPTH, D_MODEL, D_FF), D_MODEL ** -0.5),
        "w_ffn_up": nrm(ks[14], (DEPTH, D_MODEL, D_FF), D_MODEL ** -0.5),
        "w_ffn_down": nrm(ks[15], (DEPTH, D_FF, D_MODEL), D_FF ** -0.5),
        "ple_gain": gain(ks[16], (DEPTH, D_MODEL)),
        "w_ple_gate": nrm(ks[17], (DEPTH, D_MODEL, D_MODEL), D_MODEL ** -0.5),
        "w_ple_proj": nrm(ks[18], (DEPTH, D_PLE, D_MODEL), 0.5 * D_PLE ** -0.5),
    }


def reference(x, p, positions, w_in, attn_sinks, hgrn_lb_logits, attn_out_gain, hgrn_out_gain,
              w_out, pre_mix_gain, post_mix_gain, pre_ffn_gain, post_ffn_gain,
              w_ffn_gate, w_ffn_up, w_ffn_down, ple_gain, w_ple_gate, w_ple_proj):
    lb_soft = jax.nn.softmax(hgrn_lb_logits.astype(jnp.float32), axis=0)
    lower_bounds = jnp.cumsum(lb_soft, axis=0) - lb_soft[0:1]
    for i in range(DEPTH):
        h = rms_norm(x, pre_mix_gain[i])
        m = hybrid_mixer(h, positions, w_in[i], attn_sinks[i], lower_bounds[i],
                         attn_out_gain[i], hgrn_out_gain[i], w_out[i])
        x = x + rms_norm(m, post_mix_gain[i])
        h = rms_norm(x, pre_ffn_gain[i])
        f = (jax.nn.silu(h @ w_ffn_gate[i]) * (h @ w_ffn_up[i])) @ w_ffn_down[i]
        x = x + rms_norm(f, post_ffn_gain[i])
        gate = jax.nn.sigmoid(rms_norm(x, ple_gain[i]) @ w_ple_gate[i])
        x = x + (p[i] @ w_ple_proj[i]) * gate
    return x
```

```python
import math
from contextlib import ExitStack

import numpy as np
import concourse.bass as bass
import concourse.mybir as mybir
from concourse.bass_utils import run_bass_kernel_spmd

F32 = mybir.dt.float32
BF16 = mybir.dt.bfloat16
I32 = mybir.dt.int32
AF = mybir.ActivationFunctionType
ALU = mybir.AluOpType
AX = mybir.AxisListType

P = 128
D = 2048
TG = 512
NS = 4
DFF = 5632
DIN = 5632
NQH = 16
NKV = 4
HD = 64
NHH = 8
EPS = 1e-6
MASKV = -30000.0
NVEC = 72


class Buf:
    __slots__ = ("name", "w", "r")

    def __init__(self, name):
        self.name = name
        self.w = None
        self.r = {}


class DSem:
    ALL = []

    def __init__(self, nc, name):
        self.sem = nc.alloc_semaphore(name)
        self.cnt = 0
        self.key = name
        DSem.ALL.append(self)


class Ctx:
    def __init__(self, nc):
        self.nc = nc
        self.E = {"pe": nc.tensor, "act": nc.scalar, "dve": nc.vector, "pool": nc.gpsimd, "sp": nc.sync}
        self.sem = {}
        self.cnt = {}
        for e in self.E:
            self.sem[e] = nc.alloc_semaphore("s_" + e)
            self.cnt[e] = 0
        self.waited = {}
        self.nwaits = 0
        self.nops = 0

    def _wait(self, e, sig):
        if sig is None:
            return
        sem, val, key = sig
        k = (e, key)
        if self.waited.get(k, 0) >= val:
            return
        self.E[e].wait_ge(sem, val)
        self.waited[k] = val
        self.nwaits += 1

    def _pre(self, e, reads, writes):
        for b in reads:
            if not (e == "pe" and b.w is not None and b.w[2] == "pe"):
                self._wait(e, b.w)
        for b in writes:
            if not (e == "pe" and b.w is not None and b.w[2] == "pe"):
                self._wait(e, b.w)
            for rk, rs in b.r.items():
                if not (e == "pe" and rk == "pe"):
                    self._wait(e, rs)

    def _post(self, sig, reads, writes):
        for b in reads:
            b.r[sig[2]] = sig
        for b in writes:
            b.w = sig
            b.r = {}

    def op(self, e, fn, reads=(), writes=()):
        self._pre(e, reads, writes)
        inst = fn()
        self.cnt[e] += 1
        inst.then_inc(self.sem[e], 1)
        sig = (self.sem[e], self.cnt[e], e)
        self._post(sig, reads, writes)
        self.nops += 1
        return sig

    def dma(self, q, out, in_, dsem, reads=(), writes=()):
        self._pre(q, reads, writes)
        inst = self.E[q].dma_start(out=out, in_=in_)
        dsem.cnt += 16
        inst.then_inc(dsem.sem, 16)
        sig = (dsem.sem, dsem.cnt, dsem.key)
        self._post(sig, reads, writes)
        return sig

    def barrier(self, engines=("pe", "act", "dve")):
        for e in engines:
            for e2 in engines:
                if self.cnt[e2] > 0:
                    self._wait(e, (self.sem[e2], self.cnt[e2], e2))


class _Stop(Exception):
    pass


def build(NTOK, NL, dbg=None):
    NG = NTOK // TG
    NB = NTOK // P
    nc = bass.Bass("TRN2", target_bir_lowering=False)
    DSem.ALL = []
    V, A, T, G = nc.vector, nc.scalar, nc.tensor, nc.gpsimd

    def din(name, shape, dt=F32):
        return nc.dram_tensor(name, shape, dt, kind="ExternalInput").ap()

    x_d = din("x", [NTOK, D])
    pT_d = din("pT", [2, 256, NTOK])
    pos_d = din("posT", [P, NB], I32)
    w_in_d = din("w_in", [2, D, DIN])
    w_out_d = din("w_out", [2, D, D])
    w_g_d = din("w_gate", [2, D, DFF])
    w_u_d = din("w_up", [2, D, DFF])
    w_d_d = din("w_down", [2, DFF, D])
    w_pg_d = din("w_pg", [2, D, D])
    w_pp_d = din("w_pp", [2, 256, D])
    vec_d = din("vec_fm", [P, 2 * NVEC])
    pmg_d = din("post_mix_gain", [2, D])
    pfg_d = din("post_ffn_gain", [2, D])
    sink_d = din("sinks", [32])
    invf_d = din("invf", [32])
    amask_d = din("amask", [4, P, 256])
    hmask_d = din("hmask", [P, P])
    y_d = nc.dram_tensor("y", [NTOK, D], F32, kind="ExternalOutput").ap()
    dbg_d = None
    if dbg is not None:
        dbg_d = nc.dram_tensor("dbg", [P, 16 * TG], F32, kind="ExternalOutput").ap()

    ctx = Ctx(nc)
    top = ExitStack()
    dsem_dbg = DSem(nc, "d_dbg")

    def dump(stage, ap2d, ncols):
        if dbg is None or dbg != stage:
            return
        ctx.barrier(("pe", "act", "dve", "pool"))
        nc.gpsimd.dma_start(out=dbg_d[:, 0:ncols], in_=ap2d).then_inc(dsem_dbg.sem, 16)
        dsem_dbg.cnt = 16
        for d_ in DSem.ALL:
            if d_.cnt > 0:
                nc.gpsimd.wait_ge(d_.sem, d_.cnt)
        raise _Stop()

    uniq = {"n": 0}

    def sb(name, shape, dt, st=top):
        uniq["n"] += 1
        t = st.enter_context(nc.sbuf_tensor(f"sb_{name}_{uniq['n']}", list(shape), dt))
        return t, Buf(name)

    def ps(name, shape, dt):
        t = top.enter_context(nc.psum_tensor("ps_" + name, list(shape), dt))
        return t, Buf(name)

    x_res, _ = sb("x_res", [P, NS, D], F32)
    xb = [Buf(f"x{s}") for s in range(NS)]
    actT, _ = sb("actT", [P, 16, TG], BF16)
    actb = [Buf(f"actT{k}") for k in range(16)]
    R_hid, _ = sb("hidT", [P, 44, TG], BF16)
    hidb = [Buf(f"hid{k}") for k in range(44)]
    mixT = R_hid
    WS = 3
    wslots = [sb(f"wslot{i}", [P, 16, 512], BF16) for i in range(WS)]
    wsems = [DSem(nc, f"d_w{i}") for i in range(WS)]
    Sst, Sst_b = sb("Sst", [P, 2, NHH, P], F32)
    kTb, kTb_b = sb("kTb", [P, 2, 2, NKV, 2, P], BF16)
    Vb, Vb_b = sb("Vb", [P, 2, 2, NKV, HD], BF16)
    cs_t, cs_b = sb("cs", [P, 2, NS, 32], F32)
    amask, amask_b = sb("amask", [P, 4, 256], F32)
    hmask, hmask_b = sb("hmask", [P, P], F32)
    vec, vec_b = sb("vec", [P, 2 * NVEC], F32)
    ident, ident_b = sb("ident", [P, P], BF16)
    ones_f, ones_b = sb("ones_f", [P, P], F32)
    sinkt, sink_b = sb("sinkt", [P, 2, 32], F32)
    invf, invf_b = sb("invf", [P, 32], F32)
    posi, posi_b = sb("posi", [P, NB], I32)
    posf, posf_b = sb("posf", [P, NB], F32)
    lbt, lbt_b = sb("lbt", [P, 2, 2, NHH], F32)
    smallc, smallc_b = sb("smallc", [P, 8], F32)
    pTt, pT_b = sb("pTt", [P, 2, TG], BF16)

    pbank = [ps(f"pb{i}", [P, 512], F32) for i in range(5)]
    pdS = ps("pdS", [P, 512], F32)
    ptb = [ps(f"ptb{i}", [P, 1024], BF16) for i in range(2)]
    rot = {"pb": 0, "pt": 0}

    def next_bank():
        i = rot["pb"]
        rot["pb"] = (i + 1) % 4
        return pbank[i]

    def next_pt():
        i = rot["pt"]
        rot["pt"] = (i + 1) % 2
        return ptb[i]

    flip = {"e": 0}

    def evac_engine():
        flip["e"] ^= 1
        return "act" if flip["e"] else "dve"

    def copy_op(e, out, in_, reads, writes):
        if e == "act":
            return ctx.op("act", lambda: A.copy(out=out, in_=in_), reads, writes)
        return ctx.op("dve", lambda: V.tensor_copy(out=out, in_=in_), reads, writes)

    def scale_op(e, out, in_, sc_ap, reads, writes):
        if e == "act":
            return ctx.op("act", lambda: A.activation(out=out, in_=in_, func=AF.Identity, scale=sc_ap), reads, writes)
        return ctx.op("dve", lambda: V.tensor_scalar(out=out, in0=in_, scalar1=sc_ap, scalar2=None, op0=ALU.mult), reads, writes)

    def setup_load(name, out, in_, buf, q="sp"):
        ctx.dma(q, out, in_, DSem(nc, "d_" + name), reads=(), writes=(buf,))

    setup_load("vec", vec[:], vec_d[:, :], vec_b)
    setup_load("amask", amask[:], amask_d.rearrange("v p k -> p v k"), amask_b)
    setup_load("hmask", hmask[:], hmask_d[:, :], hmask_b)
    setup_load("sink", sinkt[:, 0, :], sink_d.partition_broadcast(P), sink_b)
    setup_load("invf", invf[:], invf_d.partition_broadcast(P), invf_b)
    setup_load("pos", posi[:], pos_d[:, :], posi_b)

    ctx.op("dve", lambda: V.memset(ones_f[:], 1.0), (), (ones_b,))
    ctx.op("dve", lambda: V.memset(ident[:], 1.0), (), (ident_b,))
    ctx.op("pool", lambda: G.affine_select(out=ident[:], in_=ident[:], pattern=[[-1, P]], compare_op=ALU.is_equal,
                                           fill=0.0, base=0, channel_multiplier=1), (ident_b,), (ident_b,))
    ctx.op("dve", lambda: V.memset(Sst[:], 0.0), (), (Sst_b,))
    ctx.op("dve", lambda: V.memset(kTb[:], 0.0), (), (kTb_b,))
    ctx.op("dve", lambda: V.memset(Vb[:], 0.0), (), (Vb_b,))
    ctx.op("dve", lambda: V.tensor_copy(out=posf[:], in_=posi[:]), (posi_b,), (posf_b,))
    ctx.op("dve", lambda: V.tensor_scalar(out=sinkt[:, 1, :], in0=sinkt[:, 0, :], scalar1=-1.0, scalar2=None, op0=ALU.mult),
           (sink_b,), (sink_b,))
    LB0 = 64
    l0 = vec[:, LB0:LB0 + 8]
    l1 = vec[:, NVEC + LB0:NVEC + LB0 + 8]
    with ExitStack() as st:
        tm, tm_b = sb("lb_m", [P, 8], F32, st)
        e0, e0_b = sb("lb_e0", [P, 8], F32, st)
        e1, e1_b = sb("lb_e1", [P, 8], F32, st)
        ctx.op("dve", lambda: V.tensor_tensor(out=tm[:], in0=l0, in1=l1, op=ALU.max), (vec_b,), (tm_b,))
        ctx.op("dve", lambda: V.tensor_tensor(out=e0[:], in0=l0, in1=tm[:], op=ALU.subtract), (vec_b, tm_b), (e0_b,))
        ctx.op("dve", lambda: V.tensor_tensor(out=e1[:], in0=l1, in1=tm[:], op=ALU.subtract), (vec_b, tm_b), (e1_b,))
        ctx.op("act", lambda: A.activation(out=e0[:], in_=e0[:], func=AF.Exp), (e0_b,), (e0_b,))
        ctx.op("act", lambda: A.activation(out=e1[:], in_=e1[:], func=AF.Exp), (e1_b,), (e1_b,))
        ctx.op("dve", lambda: V.tensor_tensor(out=tm[:], in0=e0[:], in1=e1[:], op=ALU.add), (e0_b, e1_b), (tm_b,))
        ctx.op("dve", lambda: V.reciprocal(out=tm[:], in_=tm[:]), (tm_b,), (tm_b,))
        ctx.op("dve", lambda: V.memset(lbt[:, 0, 0, :], 0.0), (), (lbt_b,))
        ctx.op("dve", lambda: V.tensor_tensor(out=lbt[:, 1, 0, :], in0=e1[:], in1=tm[:], op=ALU.mult), (e1_b, tm_b), (lbt_b,))
        ctx.op("dve", lambda: V.tensor_scalar(out=lbt[:, :, 1, :], in0=lbt[:, :, 0, :], scalar1=-1.0, scalar2=1.0,
                                              op0=ALU.mult, op1=ALU.add), (lbt_b,), (lbt_b,))
        ctx.barrier()

    def weight_plan():
        for g in range(NG):
            for l in range(NL):
                for t in range(11):
                    yield ("in", w_in_d[l, :, t * 512:(t + 1) * 512].rearrange("(kc p) n -> p kc n", p=P), 16)
                for fb in range(4):
                    yield ("out", w_out_d[l, :, fb * 512:(fb + 1) * 512].rearrange("(kc p) n -> p kc n", p=P), 16)
                for t in range(11):
                    yield ("gate", w_g_d[l, :, t * 512:(t + 1) * 512].rearrange("(kc p) n -> p kc n", p=P), 16)
                    yield ("up", w_u_d[l, :, t * 512:(t + 1) * 512].rearrange("(kc p) n -> p kc n", p=P), 16)
                for fb in range(4):
                    for kp in range(4):
                        yield ("down", w_d_d[l, kp * 1408:(kp + 1) * 1408, fb * 512:(fb + 1) * 512]
                               .rearrange("(kc p) n -> p kc n", p=P), 11)
                for fb in range(4):
                    yield ("pg", w_pg_d[l, :, fb * 512:(fb + 1) * 512].rearrange("(kc p) n -> p kc n", p=P), 16)
                    yield ("pp", w_pp_d[l, :, fb * 512:(fb + 1) * 512].rearrange("(kc p) n -> p kc n", p=P), 2)

    wplan = list(weight_plan())
    wst = {"issued": 0, "consumed": 0}

    def w_issue(upto):
        while wst["issued"] < min(upto, len(wplan)):
            i = wst["issued"]
            tag, src, kc = wplan[i]
            t, b = wslots[i % WS]
            ctx.dma("pool", t[:, 0:kc, :], src, wsems[i % WS], reads=(), writes=(b,))
            wst["issued"] += 1

    def w_get(tag, held=0):
        i = wst["consumed"]
        assert wplan[i][0] == tag, (wplan[i][0], tag)
        w_issue(i + WS - held)
        wst["consumed"] += 1
        return wslots[i % WS]

    xsem = DSem(nc, "d_x")
    ysem = DSem(nc, "d_y")
    psem = DSem(nc, "d_p")
    gsem = DSem(nc, "d_g")

    def rstd_from_ss(ss, ss_b, n, width):
        ctx.op("act", lambda: A.activation(out=ss[:, 0:n], in_=ss[:, 0:n], func=AF.Ln, scale=1.0 / width, bias=eps_ap),
               (ss_b, smallc_b), (ss_b,))
        ctx.op("act", lambda: A.activation(out=ss[:, 0:n], in_=ss[:, 0:n], func=AF.Exp, scale=-0.5), (ss_b,), (ss_b,))

    ctx.op("dve", lambda: V.memset(smallc[:, 0:1], EPS), (), (smallc_b,))
    eps_ap = smallc[:, 0:1]

    def norm_to_actT(gcol):
        with ExitStack() as st:
            hb4, hb4_b = sb("hb4", [P, NS, D], BF16, st)
            ss, ss_b = sb("nss", [P, NS], F32, st)
            hbs = [Buf(f"hb{s}") for s in range(NS)]
            for s in range(NS):
                ctx.op("act", lambda: A.activation(out=hb4[:, s, :], in_=x_res[:, s, :], func=AF.Square,
                                                   accum_out=ss[:, s:s + 1]), (xb[s],), (hbs[s], ss_b))
            rstd_from_ss(ss, ss_b, NS, D)
            for s in range(NS):
                scale_op("dve" if s % 2 else "act", hb4[:, s, :], x_res[:, s, :], ss[:, s:s + 1], (xb[s], ss_b), (hbs[s],))
            for kc in range(16):
                pt, pt_b = next_pt()

                def tr():
                    for s in range(NS):
                        i = T.transpose(out=pt[:, s * P:(s + 1) * P], in_=hb4[:, s, kc * P:(kc + 1) * P], identity=ident[:])
                    return i
                ctx.op("pe", tr, hbs + [ident_b], (pt_b,))
                scale_op(evac_engine(), actT[:, kc, :], pt[:, 0:TG], vec[:, gcol + kc:gcol + kc + 1], (pt_b, vec_b), (actb[kc],))
            ctx.barrier()

    def residual_norm_add(ybuf, yb, gain_d_row):
        with ExitStack() as st:
            junk = actT[:, 0:4, :].rearrange("p a b -> p (a b)")
            junk_bs = actb[0:4]
            gbc = actT[:, 4:12, :].rearrange("p a b -> p (a b)").bitcast(F32)
            gbc_bs = actb[4:12]
            ss, ss_b = sb("rss", [P, NS], F32, st)
            ctx.dma("sp", gbc, gain_d_row.partition_broadcast(P), gsem, reads=(), writes=tuple(gbc_bs))
            for s in range(NS):
                ctx.op("act", lambda: A.activation(out=junk, in_=ybuf[:, s, :], func=AF.Square, accum_out=ss[:, s:s + 1]),
                       (yb[s],), tuple(junk_bs) + (ss_b,))
            rstd_from_ss(ss, ss_b, NS, D)
            for s in range(NS):
                ctx.op("dve", lambda: V.scalar_tensor_tensor(out=ybuf[:, s, :], in0=ybuf[:, s, :], scalar=ss[:, s:s + 1], in1=gbc,
                                                             op0=ALU.mult, op1=ALU.mult), (yb[s], ss_b) + tuple(gbc_bs), (yb[s],))
                ctx.op("dve", lambda: V.tensor_tensor(out=x_res[:, s, :], in0=x_res[:, s, :], in1=ybuf[:, s, :], op=ALU.add),
                       (xb[s], yb[s]), (xb[s],))
            ctx.barrier()

    def tokmajor_proj(tag, srcT, srcb, nk, ybuf, yb, fb):
        wt, wb = w_get(tag)
        for s in range(NS):
            bk, bk_b = next_bank()

            def mm():
                for kc in range(nk):
                    i = T.matmul(bk[:], lhsT=srcT[:, kc, s * P:(s + 1) * P], rhs=wt[:, kc, :], start=(kc == 0), stop=(kc == nk - 1))
                return i
            ctx.op("pe", mm, list(srcb[0:nk]) + [wb], (bk_b,))
            copy_op(evac_engine(), ybuf[:, s, fb * 512:(fb + 1) * 512], bk[:], (bk_b,), (yb[s],))

    def main_body(NG):
        for g in range(NG):
            ctx.dma("sp", x_res[:], x_d[g * TG:(g + 1) * TG, :].rearrange("(s p) d -> p s d", p=P), xsem, reads=(), writes=tuple(xb))
            with ExitStack() as st:
                ang, ang_b = sb("ang", [P, NS, 32], F32, st)
                kk, kk_b = sb("kk", [P, NS, 32], F32, st)
                ki, ki_b = sb("ki", [P, NS, 32], I32, st)
                yy, yy_b = sb("yy", [P, NS, 32], F32, st)
                mk, mk_b = sb("mk", [P, NS, 32], F32, st)
                C1 = 6.28125
                C2 = 2.0 * math.pi - C1
                ctx.op("dve", lambda: V.tensor_tensor(out=ang[:], in0=invf[:].unsqueeze(1).to_broadcast([P, NS, 32]),
                                                      in1=posf[:, g * NS:(g + 1) * NS].unsqueeze(2).to_broadcast([P, NS, 32]),
                                                      op=ALU.mult), (invf_b, posf_b), (ang_b,))
                for which, shift in ((1, 0.0), (0, math.pi / 2)):
                    ctx.op("dve", lambda: V.tensor_scalar(out=kk[:], in0=ang[:], scalar1=shift, scalar2=1.0 / (2 * math.pi),
                                                          op0=ALU.add, op1=ALU.mult), (ang_b,), (kk_b,))
                    ctx.op("dve", lambda: V.tensor_copy(out=ki[:], in_=kk[:]), (kk_b,), (ki_b,))
                    ctx.op("dve", lambda: V.tensor_copy(out=kk[:], in_=ki[:]), (ki_b,), (kk_b,))
                    ctx.op("dve", lambda: V.scalar_tensor_tensor(out=yy[:], in0=kk[:], scalar=-C1, in1=ang[:], op0=ALU.mult, op1=ALU.add),
                           (kk_b, ang_b), (yy_b,))
                    ctx.op("dve", lambda: V.scalar_tensor_tensor(out=yy[:], in0=kk[:], scalar=-C2, in1=yy[:], op0=ALU.mult, op1=ALU.add),
                           (kk_b, yy_b), (yy_b,))
                    if shift != 0.0:
                        ctx.op("dve", lambda: V.tensor_scalar(out=yy[:], in0=yy[:], scalar1=shift, scalar2=None, op0=ALU.add), (yy_b,), (yy_b,))
                    ctx.op("dve", lambda: V.tensor_scalar(out=mk[:], in0=yy[:], scalar1=math.pi, scalar2=-2 * math.pi, op0=ALU.is_gt, op1=ALU.mult),
                           (yy_b,), (mk_b,))
                    ctx.op("dve", lambda: V.tensor_tensor(out=yy[:], in0=yy[:], in1=mk[:], op=ALU.add), (yy_b, mk_b), (yy_b,))
                    ctx.op("dve", lambda: V.tensor_scalar(out=mk[:], in0=yy[:], scalar1=-math.pi, scalar2=2 * math.pi, op0=ALU.is_lt, op1=ALU.mult),
                           (yy_b,), (mk_b,))
                    ctx.op("dve", lambda: V.tensor_tensor(out=yy[:], in0=yy[:], in1=mk[:], op=ALU.add), (yy_b, mk_b), (yy_b,))
                    ctx.op("dve", lambda: V.tensor_scalar(out=yy[:], in0=yy[:], scalar1=3.1415925, scalar2=-3.1415925, op0=ALU.min, op1=ALU.max),
                           (yy_b,), (yy_b,))
                    ctx.op("act", lambda: A.activation(out=cs_t[:, which, :, :], in_=yy[:], func=AF.Sin), (yy_b,), (cs_b,))
                ctx.barrier()

            for l in range(NL):
                vc = l * NVEC

                norm_to_actT(vc + 0)
                dump("actT", actT[:].rearrange("p a b -> p (a b)"), 16 * TG)
                with ExitStack() as st:
                    qkv = R_hid[:, 16:40, :].rearrange("p a b -> p (a b)").bitcast(F32).rearrange("p (s c) -> p s c", c=1536)
                    qkvb = [Buf(f"qkv{s}") for s in range(NS)]
                    an4, _ = sb("an4", [P, NS, 1024], BF16, st)
                    an4b = [Buf(f"an4_{s}") for s in range(NS)]
                    for t in range(3):
                        wt, wb = w_get("in")
                        for s in range(NS):
                            bk, bk_b = next_bank()

                            def mm():
                                for kc in range(16):
                                    i = T.matmul(bk[:], lhsT=actT[:, kc, s * P:(s + 1) * P], rhs=wt[:, kc, :], start=(kc == 0), stop=(kc == 15))
                                return i
                            ctx.op("pe", mm, actb + [wb], (bk_b,))
                            copy_op(evac_engine(), qkv[:, s, t * 512:(t + 1) * 512], bk[:], (bk_b,), (qkvb[s],))

                    with ExitStack() as st2:
                        t1, t1_b = sb("rt1", [P, 20, 32], F32, st2)
                        t2, t2_b = sb("rt2", [P, 20, 32], F32, st2)
                        qr, qr_b = sb("qr", [P, NQH, HD], BF16, st2)
                        kdup, kdup_b = sb("kdup", [P, NKV, 2, HD], BF16, st2)
                        qT, qT_b = sb("qT", [P, 8, P], BF16, st2)
                        NR = 3
                        Ssb = [sb(f"Ssb{i}", [P, 2, 256], F32, st2) for i in range(NR)]
                        eb = [sb(f"eb{i}", [P, 2, 256], BF16, st2) for i in range(NR)]
                        eT = [sb(f"eT{i}", [P, 4, P], BF16, st2) for i in range(NR)]
                        stat, stat_b = sb("stat", [P, 8, NQH], F32, st2)
                        atok, atok_b = sb("atok", [P, 1024], F32, st2)
                        ajunk, ajunk_b = sb("ajunk", [P, 1024], BF16, st2)
                        ass, ass_b = sb("ass", [P, NS], F32, st2)

                        for s in range(NS):
                            bi = g * NS + s
                            slot = s % 2
                            first = (bi == 0)
                            mvar = (2 if first else 0) + slot
                            cosb = cs_t[:, 0, s, :].unsqueeze(1).to_broadcast([P, 20, 32])
                            sinb = cs_t[:, 1, s, :].unsqueeze(1).to_broadcast([P, 20, 32])
                            qk = qkv[:, s, 0:1280].rearrange("p (h d) -> p h d", d=HD)
                            x1 = qk[:, :, 0:32]
                            x2 = qk[:, :, 32:64]
                            ctx.op("dve", lambda: V.tensor_tensor(out=t1[:], in0=x1, in1=cosb, op=ALU.mult), (qkvb[s], cs_b), (t1_b,))
                            ctx.op("dve", lambda: V.tensor_tensor(out=t2[:], in0=x2, in1=sinb, op=ALU.mult), (qkvb[s], cs_b), (t2_b,))
                            ctx.op("dve", lambda: V.tensor_tensor(out=qr[:, :, 0:32], in0=t1[:, 0:16, :], in1=t2[:, 0:16, :], op=ALU.subtract),
                                   (t1_b, t2_b), (qr_b,))
                            for dd in range(2):
                                ctx.op("dve", lambda: V.tensor_tensor(out=kdup[:, :, dd, 0:32], in0=t1[:, 16:20, :], in1=t2[:, 16:20, :], op=ALU.subtract),
                                       (t1_b, t2_b), (kdup_b,))
                            ctx.op("dve", lambda: V.tensor_tensor(out=t1[:], in0=x2, in1=cosb, op=ALU.mult), (qkvb[s], cs_b), (t1_b,))
                            ctx.op("dve", lambda: V.tensor_tensor(out=t2[:], in0=x1, in1=sinb, op=ALU.mult), (qkvb[s], cs_b), (t2_b,))
                            ctx.op("dve", lambda: V.tensor_tensor(out=qr[:, :, 32:64], in0=t1[:, 0:16, :], in1=t2[:, 0:16, :], op=ALU.add),
                                   (t1_b, t2_b), (qr_b,))
                            for dd in range(2):
                                ctx.op("dve", lambda: V.tensor_tensor(out=kdup[:, :, dd, 32:64], in0=t1[:, 16:20, :], in1=t2[:, 16:20, :], op=ALU.add),
                                       (t1_b, t2_b), (kdup_b,))
                            ctx.op("act", lambda: A.copy(out=Vb[:, l, slot, :, :], in_=qkv[:, s, 1280:1536].rearrange("p (g d) -> p g d", d=HD)),
                                   (qkvb[s],), (Vb_b,))
                            qrf = qr[:].rearrange("p h d -> p (h d)")
                            for half in range(2):
                                pt, pt_b = next_pt()

                                def trq():
                                    for j in range(4):
                                        jj = half * 4 + j
                                        i = T.transpose(out=pt[:, j * P:(j + 1) * P], in_=qrf[:, jj * P:(jj + 1) * P], identity=ident[:])
                                    return i
                                ctx.op("pe", trq, (qr_b, ident_b), (pt_b,))
                                copy_op(evac_engine(), qT[:, half * 4:(half + 1) * 4, :], pt[:, 0:512].rearrange("p (j t) -> p j t", t=P),
                                        (pt_b,), (qT_b,))
                            pt, pt_b = next_pt()
                            kdf = kdup[:].rearrange("p g t d -> p g (t d)")

                            def trk():
                                for gg in range(NKV):
                                    i = T.transpose(out=pt[:, gg * P:(gg + 1) * P], in_=kdf[:, gg, :], identity=ident[:])
                                return i
                            ctx.op("pe", trk, (kdup_b, ident_b), (pt_b,))
                            copy_op("act", kTb[0:64, l, 0, :, slot, :], pt[0:64, 0:512].rearrange("p (j t) -> p j t", t=P), (pt_b,), (kTb_b,))
                            copy_op("dve", kTb[64:128, l, 1, :, slot, :], pt[64:128, 0:512].rearrange("p (j t) -> p j t", t=P), (pt_b,), (kTb_b,))

                            obk = [pbank[4], pdS]

                            def stage1(j):
                                gq = j // 2
                                bk, bk_b = next_bank()
                                S_t, S_b = Ssb[j % NR]
                                e_t, e_b = eb[j % NR]

                                def mm():
                                    for i2 in range(2):
                                        i = T.matmul(bk[:, i2 * 256:(i2 + 1) * 256], lhsT=qT[:, j, :],
                                                     rhs=kTb[:, l, i2, gq, :, :].rearrange("p s t -> p (s t)"), start=True, stop=True)
                                    return i
                                ctx.op("pe", mm, (qT_b, kTb_b), (bk_b,))
                                for i2 in range(2):
                                    ctx.op("dve", lambda: V.scalar_tensor_tensor(out=S_t[:, i2, :], in0=bk[:, i2 * 256:(i2 + 1) * 256], scalar=0.125,
                                                                                 in1=amask[:, mvar, :], op0=ALU.mult, op1=ALU.add),
                                           (bk_b, amask_b), (S_b,))
                                h0 = 2 * j
                                ctx.op("dve", lambda: V.tensor_reduce(out=stat[:, 0, h0:h0 + 2], in_=S_t[:], axis=AX.X, op=ALU.max, negate=True),
                                       (S_b,), (stat_b,))
                                ctx.op("dve", lambda: V.tensor_tensor(out=stat[:, 0, h0:h0 + 2], in0=stat[:, 0, h0:h0 + 2],
                                                                      in1=sinkt[:, 1, l * 16 + h0:l * 16 + h0 + 2], op=ALU.min), (stat_b, sink_b), (stat_b,))
                                for i2 in range(2):
                                    ctx.op("act", lambda: A.activation(out=e_t[:, i2, :], in_=S_t[:, i2, :], func=AF.Exp,
                                                                       bias=stat[:, 0, h0 + i2:h0 + i2 + 1], scale=1.0,
                                                                       accum_out=stat[:, 1, h0 + i2:h0 + i2 + 1]), (S_b, stat_b), (e_b, stat_b))

                            def stage2(j):
                                e_t, e_b = e_bufs = eb[j % NR]
                                eT_t, eT_b = eT[j % NR]
                                pt, pt_b = next_pt()

                                def tr():
                                    for i2 in range(2):
                                        for sl in range(2):
                                            i = T.transpose(out=pt[:, (i2 * 2 + sl) * P:(i2 * 2 + sl + 1) * P],
                                                            in_=e_t[:, i2, sl * P:(sl + 1) * P], identity=ident[:])
                                    return i
                                ctx.op("pe", tr, (e_b, ident_b), (pt_b,))
                                copy_op(evac_engine(), eT_t[:], pt[:, 0:512].rearrange("p (j t) -> p j t", t=P), (pt_b,), (eT_b,))

                            def stage3(j):
                                gq = j // 2
                                eT_t, eT_b = eT[j % NR]
                                for i2 in range(2):
                                    h = 2 * j + i2
                                    ob, ob_b = obk[h // 8]
                                    col = (h % 8) * HD

                                    def mm():
                                        T.matmul(ob[:, col:col + HD], lhsT=eT_t[:, i2 * 2 + 0, :], rhs=Vb[:, l, 0, gq, :], start=True, stop=False)
                                        return T.matmul(ob[:, col:col + HD], lhsT=eT_t[:, i2 * 2 + 1, :], rhs=Vb[:, l, 1, gq, :], start=False, stop=True)
                                    ctx.op("pe", mm, (eT_b, Vb_b), (ob_b,))

                            for step in range(8 + 2):
                                if step < 8:
                                    stage1(step)
                                if 0 <= step - 1 < 8:
                                    stage2(step - 1)
                                if 0 <= step - 2 < 8:
                                    stage3(step - 2)
                            ctx.op("dve", lambda: V.tensor_tensor(out=stat[:, 2, :], in0=stat[:, 0, :], in1=sinkt[:, 0, l * 16:(l + 1) * 16], op=ALU.add),
                                   (stat_b, sink_b), (stat_b,))
                            ctx.op("act", lambda: A.activation(out=stat[:, 3, :], in_=stat[:, 2, :], func=AF.Exp), (stat_b,), (stat_b,))
                            ctx.op("dve", lambda: V.tensor_tensor(out=stat[:, 3, :], in0=stat[:, 3, :], in1=stat[:, 1, :], op=ALU.add), (stat_b,), (stat_b,))
                            ctx.op("dve", lambda: V.reciprocal(out=stat[:, 4, :], in_=stat[:, 3, :]), (stat_b,), (stat_b,))
                            for hb_ in range(2):
                                ob, ob_b = obk[hb_]
                                ctx.op("dve", lambda: V.tensor_tensor(out=atok[:, hb_ * 512:(hb_ + 1) * 512].rearrange("p (h d) -> p h d", d=HD),
                                                                      in0=ob[:].rearrange("p (h d) -> p h d", d=HD),
                                                                      in1=stat[:, 4, hb_ * 8:(hb_ + 1) * 8].unsqueeze(2).to_broadcast([P, 8, HD]),
                                                                      op=ALU.mult), (ob_b, stat_b), (atok_b,))
                            ctx.op("act", lambda: A.activation(out=ajunk[:], in_=atok[:], func=AF.Square, accum_out=ass[:, s:s + 1]),
                                   (atok_b,), (ajunk_b, ass_b))
                            rstd_from_ss(ass[:, s:s + 1], ass_b, 1, 1024)
                            scale_op("act", an4[:, s, :], atok[:], ass[:, s:s + 1], (atok_b, ass_b), (an4b[s],))
                        ctx.barrier()
                    for j in range(8):
                        pt, pt_b = next_pt()

                        def tr():
                            for s in range(NS):
                                i = T.transpose(out=pt[:, s * P:(s + 1) * P], in_=an4[:, s, j * P:(j + 1) * P], identity=ident[:])
                            return i
                        ctx.op("pe", tr, an4b + [ident_b], (pt_b,))
                        scale_op(evac_engine(), mixT[:, j, :], pt[:, 0:TG], vec[:, vc + 48 + j:vc + 48 + j + 1], (pt_b, vec_b), (hidb[j],))
                    ctx.barrier()

                dump("attn", mixT[:, 0:8, :].rearrange("p a b -> p (a b)"), 8 * TG)
                with ExitStack() as st:
                    def f32t(name, shape=(P, TG)):
                        return sb(name, list(shape), F32, st)
                    sg, sg_b = f32t("h_sg")
                    qs, qs_b = f32t("h_qs")
                    sgate, sgate_b = f32t("h_sgate")
                    vT, vT_b = sb("h_vT", [P, TG], BF16, st)
                    ff, ff_b = f32t("h_f")
                    gg_, gg_b = f32t("h_g")
                    bT, bT_b = f32t("h_b")
                    dd_, dd_b = f32t("h_d")
                    E1, E1_b = f32t("h_E1")
                    E2, E2_b = f32t("h_E2")
                    kin, kin_b = f32t("h_kin")
                    qt, qt_b = sb("h_qt", [P, TG], BF16, st)
                    kt, kt_b = sb("h_kt", [P, TG], BF16, st)
                    vtok, vtok_b = sb("h_vtok", [P, 4, P], BF16, st)
                    ktokA, ktokA_b = sb("h_ktokA", [P, 4, P], BF16, st)
                    ktokB, ktokB_b = sb("h_ktokB", [P, 4, P], BF16, st)
                    AT, AT_b = sb("h_AT", [P, 4, P], BF16, st)
                    ctx.op("dve", lambda: V.memset(ktokA[:], 0.0), (), (ktokA_b,))
                    ctx.op("dve", lambda: V.memset(ktokB[:], 0.0), (), (ktokB_b,))
                    Sr, Sr_b = sb("h_Sr", [P, P], BF16, st)
                    dtmp, dtmp_b = f32t("h_dtmp", (P, P))
                    sq, sq_b = f32t("h_sq")
                    rs_, rs_b = f32t("h_rs")
                    on, on_b = f32t("h_on")
                    sc, sc_b = f32t("h_sc", (P, 4, 8))
                    rmask, rmask_b = f32t("h_rmask")
                    ctx.op("dve", lambda: V.memset(rmask[:], 1.0), (), (rmask_b,))
                    ctx.op("dve", lambda: V.memset(rmask[:].rearrange("p (c t) -> p c t", t=64)[:, :, 0:1], 0.0), (), (rmask_b,))

                    for h in range(NHH):
                        wt, wb = w_get("in")
                        banks = []
                        for c in range(4):
                            bk, bk_b = next_bank()

                            def mm():
                                for kc in range(16):
                                    i = T.matmul(bk[:], lhsT=wt[:, kc, c * P:(c + 1) * P], rhs=actT[:, kc, :], start=(kc == 0), stop=(kc == 15))
                                return i
                            ctx.op("pe", mm, actb + [wb], (bk_b,))
                            banks.append((bk, bk_b))
                        (bq, bq_b), (bf_, bf_b), (bi_, bi_b), (bg, bg_b) = banks
                        ctx.op("act", lambda: A.activation(out=sg[:], in_=bf_[:], func=AF.Sigmoid), (bf_b,), (sg_b,))
                        ctx.op("act", lambda: A.activation(out=qs[:], in_=bq[:], func=AF.Silu), (bq_b,), (qs_b,))
                        ctx.op("act", lambda: A.copy(out=vT[:], in_=bi_[:]), (bi_b,), (vT_b,))
                        ctx.op("act", lambda: A.activation(out=sgate[:], in_=bg[:], func=AF.Silu), (bg_b,), (sgate_b,))
                        ctx.op("dve", lambda: V.tensor_scalar(out=ff[:], in0=sg[:], scalar1=lbt[:, l, 1, h:h + 1], scalar2=lbt[:, l, 0, h:h + 1],
                                                              op0=ALU.mult, op1=ALU.add), (sg_b, lbt_b), (ff_b,))
                        ctx.op("act", lambda: A.activation(out=gg_[:], in_=ff[:], func=AF.Ln), (ff_b,), (gg_b,))
                        ctx.op("dve", lambda: V.tensor_scalar(out=kin[:], in0=ff[:], scalar1=-1.0, scalar2=1.0, op0=ALU.mult, op1=ALU.add),
                               (ff_b,), (kin_b,))
                        ctx.op("dve", lambda: V.tensor_tensor_scan(out=bT[:], data0=rmask[:], data1=gg_[:], initial=0.0, op0=ALU.mult, op1=ALU.add),
                               (rmask_b, gg_b), (bT_b,))
                        b3 = bT[:].rearrange("p (c t) -> p c t", t=64)
                        ctx.op("dve", lambda: V.tensor_tensor(out=dd_[:].rearrange("p (c t) -> p c t", t=64), in0=b3,
                                                              in1=b3[:, :, 31:32].to_broadcast([P, 8, 64]), op=ALU.subtract), (bT_b,), (dd_b,))
                        ctx.op("act", lambda: A.activation(out=E1[:], in_=dd_[:], func=AF.Exp), (dd_b,), (E1_b,))
                        ctx.op("act", lambda: A.activation(out=E2[:], in_=dd_[:], func=AF.Exp, scale=-1.0), (dd_b,), (E2_b,))
                        ctx.op("dve", lambda: V.tensor_tensor(out=qt[:], in0=qs[:], in1=E1[:], op=ALU.mult), (qs_b, E1_b), (qt_b,))
                        ctx.op("dve", lambda: V.tensor_tensor(out=kt[:], in0=kin[:], in1=E2[:], op=ALU.mult), (kin_b, E2_b), (kt_b,))
                        ctx.op("act", lambda: A.activation(out=sc[:, 0, :], in_=b3[:, :, 31], func=AF.Exp), (bT_b,), (sc_b,))
                        ctx.op("act", lambda: A.activation(out=sc[:, 1, :], in_=b3[:, :, 63], func=AF.Exp), (bT_b,), (sc_b,))
                        ctx.op("dve", lambda: V.tensor_tensor(out=sc[:, 3, :], in0=b3[:, :, 63], in1=b3[:, :, 31], op=ALU.subtract), (bT_b,), (sc_b,))
                        ctx.op("act", lambda: A.activation(out=sc[:, 2, :], in_=sc[:, 3, :], func=AF.Exp), (sc_b,), (sc_b,))
                        pt, pt_b = next_pt()

                        def trv():
                            for pc in range(4):
                                i = T.transpose(out=pt[:, pc * P:(pc + 1) * P], in_=vT[:, pc * P:(pc + 1) * P], identity=ident[:])
                            return i
                        ctx.op("pe", trv, (vT_b, ident_b), (pt_b,))
                        copy_op(evac_engine(), vtok[:], pt[:, 0:512].rearrange("p (c k) -> p c k", k=P), (pt_b,), (vtok_b,))
                        pt, pt_b = next_pt()

                        def trk2():
                            for pc in range(4):
                                i = T.transpose(out=pt[:, pc * P:(pc + 1) * P], in_=kt[:, pc * P:(pc + 1) * P], identity=ident[:])
                            return i
                        ctx.op("pe", trk2, (kt_b, ident_b), (pt_b,))
                        copy_op("act", ktokA[0:64, :, :], pt[0:64, 0:512].rearrange("p (c k) -> p c k", k=P), (pt_b,), (ktokA_b,))
                        copy_op("dve", ktokB[64:128, :, :], pt[64:128, 0:512].rearrange("p (c k) -> p c k", k=P), (pt_b,), (ktokB_b,))
                        bk, bk_b = next_bank()

                        def mmA():
                            for pc in range(4):
                                i = T.matmul(bk[:, pc * P:(pc + 1) * P], lhsT=kt[:, pc * P:(pc + 1) * P], rhs=qt[:, pc * P:(pc + 1) * P],
                                             start=True, stop=True)
                            return i
                        ctx.op("pe", mmA, (kt_b, qt_b), (bk_b,))
                        ctx.op("dve", lambda: V.tensor_tensor(out=AT[:], in0=bk[:].rearrange("p (c t) -> p c t", t=P),
                                                              in1=hmask[:].unsqueeze(1).to_broadcast([P, 4, P]), op=ALU.mult),
                               (bk_b, hmask_b), (AT_b,))
                        oT, oT_b = pbank[4]
                        dS, dS_b = pdS
                        Sh = Sst[:, l, h, :]
                        for pc in range(4):
                            ctx.op("pe", lambda: T.matmul(oT[:, pc * P:(pc + 1) * P], lhsT=vtok[:, pc, :], rhs=AT[:, pc, :], start=True, stop=False),
                                   (vtok_b, AT_b), (oT_b,))
                            for cc in range(2):
                                c = 2 * pc + cc
                                ktk, ktk_b = (ktokA, ktokA_b) if cc == 0 else (ktokB, ktokB_b)
                                ctx.op("dve", lambda: V.tensor_scalar(out=Sr[:], in0=Sh, scalar1=sc[:, 0, c:c + 1], scalar2=None, op0=ALU.mult),
                                       (Sst_b, sc_b), (Sr_b,))
                                ctx.op("pe", lambda: T.matmul(oT[:, c * 64:(c + 1) * 64], lhsT=Sr[:], rhs=qt[:, c * 64:(c + 1) * 64],
                                                              start=False, stop=(cc == 1)), (Sr_b, qt_b), (oT_b,))
                                ctx.op("pe", lambda: T.matmul(dS[:, 0:P], lhsT=ktk[:, pc, :], rhs=vtok[:, pc, :], start=True, stop=True),
                                       (ktk_b, vtok_b), (dS_b,))
                                ctx.op("dve", lambda: V.tensor_scalar(out=dtmp[:], in0=dS[:, 0:P], scalar1=sc[:, 2, c:c + 1], scalar2=None, op0=ALU.mult),
                                       (dS_b, sc_b), (dtmp_b,))
                                ctx.op("dve", lambda: V.scalar_tensor_tensor(out=Sh, in0=Sh, scalar=sc[:, 1, c:c + 1], in1=dtmp[:], op0=ALU.mult, op1=ALU.add),
                                       (Sst_b, sc_b, dtmp_b), (Sst_b,))
                        ctx.op("act", lambda: A.activation(out=sq[:], in_=oT[:], func=AF.Square), (oT_b,), (sq_b,))
                        bk, bk_b = next_bank()
                        ctx.op("pe", lambda: T.matmul(bk[:], lhsT=ones_f[:], rhs=sq[:], start=True, stop=True), (ones_b, sq_b), (bk_b,))
                        ctx.op("act", lambda: A.activation(out=rs_[:], in_=bk[:], func=AF.Ln, scale=1.0 / P, bias=eps_ap), (bk_b, smallc_b), (rs_b,))
                        ctx.op("act", lambda: A.activation(out=rs_[:], in_=rs_[:], func=AF.Exp, scale=-0.5), (rs_b,), (rs_b,))
                        ctx.op("dve", lambda: V.tensor_tensor(out=on[:], in0=oT[:], in1=rs_[:], op=ALU.mult), (oT_b, rs_b), (on_b,))
                        ctx.op("dve", lambda: V.scalar_tensor_tensor(out=mixT[:, 8 + h, :], in0=on[:], scalar=vec[:, vc + 56 + h:vc + 56 + h + 1],
                                                                     in1=sgate[:], op0=ALU.mult, op1=ALU.mult), (on_b, vec_b, sgate_b), (hidb[8 + h],))
                    ctx.barrier()

                dump("hgrn", mixT[:, 8:16, :].rearrange("p a b -> p (a b)"), 8 * TG)
                with ExitStack() as st:
                    ybuf, _ = sb("ybuf", [P, NS, D], F32, st)
                    yb = [Buf(f"y{s}") for s in range(NS)]
                    for fb in range(4):
                        tokmajor_proj("out", mixT, hidb, 16, ybuf, yb, fb)
                    residual_norm_add(ybuf, yb, pmg_d[l])

                dump("xmix", x_res[:].rearrange("p a b -> p (a b)"), 4 * D)
                norm_to_actT(vc + 16)
                with ExitStack() as st:
                    sgt = [sb(f"f_sg{i}", [P, TG], F32, st) for i in range(2)]
                    for t in range(11):
                        wg, wg_b = w_get("gate")
                        wu, wu_b = w_get("up", held=1)
                        for c in range(4):
                            bg, bg_b = next_bank()
                            bu, bu_b = next_bank()

                            def mmg():
                                for kc in range(16):
                                    i = T.matmul(bg[:], lhsT=wg[:, kc, c * P:(c + 1) * P], rhs=actT[:, kc, :], start=(kc == 0), stop=(kc == 15))
                                return i

                            def mmu():
                                for kc in range(16):
                                    i = T.matmul(bu[:], lhsT=wu[:, kc, c * P:(c + 1) * P], rhs=actT[:, kc, :], start=(kc == 0), stop=(kc == 15))
                                return i
                            ctx.op("pe", mmg, actb + [wg_b], (bg_b,))
                            ctx.op("pe", mmu, actb + [wu_b], (bu_b,))
                            s_t, s_b = sgt[c % 2]
                            ctx.op("act", lambda: A.activation(out=s_t[:], in_=bg[:], func=AF.Silu), (bg_b,), (s_b,))
                            hc = t * 4 + c
                            ctx.op("dve", lambda: V.tensor_tensor(out=R_hid[:, hc, :], in0=s_t[:], in1=bu[:], op=ALU.mult), (s_b, bu_b), (hidb[hc],))
                    ctx.barrier()
                with ExitStack() as st:
                    ybuf, _ = sb("ybuf2", [P, NS, D], F32, st)
                    yb = [Buf(f"y2{s}") for s in range(NS)]
                    for fb in range(4):
                        for kp in range(4):
                            wt, wb = w_get("down")
                            for s in range(NS):
                                bk, bk_b = pbank[s]

                                def mm():
                                    for kc in range(11):
                                        hc = kp * 11 + kc
                                        i = T.matmul(bk[:], lhsT=R_hid[:, hc, s * P:(s + 1) * P], rhs=wt[:, kc, :],
                                                     start=(hc == 0), stop=(hc == 43))
                                    return i
                                ctx.op("pe", mm, hidb[kp * 11:(kp + 1) * 11] + [wb], (bk_b,))
                        for s in range(NS):
                            bk, bk_b = pbank[s]
                            copy_op(evac_engine(), ybuf[:, s, fb * 512:(fb + 1) * 512], bk[:], (bk_b,), (yb[s],))
                    residual_norm_add(ybuf, yb, pfg_d[l])

                dump("xffn", x_res[:].rearrange("p a b -> p (a b)"), 4 * D)
                norm_to_actT(vc + 32)
                with ExitStack() as st:
                    ctx.dma("pool", pTt[:], pT_d[l, :, g * TG:(g + 1) * TG].rearrange("(kc p) t -> p kc t", p=P), psem, reads=(), writes=(pT_b,))
                    sgs = [sb(f"p_sg{i}", [P, 512], F32, st) for i in range(2)]
                    tm2 = [sb(f"p_tm{i}", [P, 512], F32, st) for i in range(2)]
                    k = 0
                    for fb in range(4):
                        wg, wg_b = w_get("pg")
                        wp, wp_b = w_get("pp", held=1)
                        for s in range(NS):
                            bg, bg_b = next_bank()
                            bp, bp_b = next_bank()

                            def mmg():
                                for kc in range(16):
                                    i = T.matmul(bg[:], lhsT=actT[:, kc, s * P:(s + 1) * P], rhs=wg[:, kc, :], start=(kc == 0), stop=(kc == 15))
                                return i

                            def mmp():
                                for kc in range(2):
                                    i = T.matmul(bp[:], lhsT=pTt[:, kc, s * P:(s + 1) * P], rhs=wp[:, kc, :], start=(kc == 0), stop=(kc == 1))
                                return i
                            ctx.op("pe", mmg, actb + [wg_b], (bg_b,))
                            ctx.op("pe", mmp, (pT_b, wp_b), (bp_b,))
                            s_t, s_b = sgs[k % 2]
                            t_t, t_b = tm2[k % 2]
                            k += 1
                            ctx.op("act", lambda: A.activation(out=s_t[:], in_=bg[:], func=AF.Sigmoid), (bg_b,), (s_b,))
                            ctx.op("dve", lambda: V.tensor_tensor(out=t_t[:], in0=s_t[:], in1=bp[:], op=ALU.mult), (s_b, bp_b), (t_b,))
                            xs = x_res[:, s, fb * 512:(fb + 1) * 512]
                            ctx.op("dve", lambda: V.tensor_tensor(out=xs, in0=xs, in1=t_t[:], op=ALU.add), (xb[s], t_b), (xb[s],))
                    ctx.barrier()

            ctx.dma("sp", y_d[g * TG:(g + 1) * TG, :].rearrange("(s p) d -> p s d", p=P), x_res[:], ysem, reads=tuple(xb), writes=())


    try:
      main_body(NG)
    except _Stop:
        top.close()
        return nc
    ctx._wait("sp", (ysem.sem, ysem.cnt, ysem.key))
    assert wst["consumed"] == len(wplan), (wst, len(wplan))
    top.close()
    print(f"[build] ops={ctx.nops} waits={ctx.nwaits} weight_tiles={len(wplan)}")
    return nc


def _fm(v):
    v = np.asarray(v, np.float32)
    return np.ascontiguousarray(v.reshape(-1, P).T)


def _host_consts():
    half = 32
    invf = (10000.0 ** (-np.arange(half, dtype=np.float32) / half)).astype(np.float32)
    tq = np.arange(P)[:, None]
    tk = np.arange(P)[None, :]
    cur = np.where(tk <= tq, 0.0, MASKV).astype(np.float32)
    prev = np.where(tk > tq, 0.0, MASKV).astype(np.float32)
    dead = np.full((P, P), MASKV, np.float32)
    am = np.stack([
        np.concatenate([cur, prev], 1),
        np.concatenate([prev, cur], 1),
        np.concatenate([cur, dead], 1),
        np.concatenate([dead, cur], 1),
    ]).astype(np.float32)
    s = np.arange(P)[:, None]
    t = np.arange(P)[None, :]
    hm = ((s // 64 == t // 64) & (s <= t)).astype(np.float32)
    return invf, am, hm


def _prep_shared(inp, NL=2):
    w_in = np.asarray(inp["w_in"], np.float32)
    cols = list(range(1536))
    for h in range(NHH):
        for sec in range(4):
            base = 1536 + sec * 1024 + h * P
            cols.extend(range(base, base + P))
    w_in_p = np.ascontiguousarray(w_in[:, :, cols])
    vec = np.zeros((P, 2 * NVEC), np.float32)
    for l in range(2):
        b = l * NVEC
        vec[:, b + 0:b + 16] = _fm(inp["pre_mix_gain"][l])
        vec[:, b + 16:b + 32] = _fm(inp["pre_ffn_gain"][l])
        vec[:, b + 32:b + 48] = _fm(inp["ple_gain"][l])
        vec[:, b + 48:b + 56] = _fm(inp["attn_out_gain"][l])
        vec[:, b + 56:b + 64] = _fm(inp["hgrn_out_gain"][l])
        vec[:, b + 64:b + 72] = _fm(inp["hgrn_lb_logits"][l])
    invf, am, hm = _host_consts()
    f = lambda k: np.ascontiguousarray(np.asarray(inp[k], np.float32))
    return {
        "w_in": w_in_p, "w_out": f("w_out"), "w_gate": f("w_ffn_gate"), "w_up": f("w_ffn_up"), "w_down": f("w_ffn_down"),
        "w_pg": f("w_ple_gate"), "w_pp": f("w_ple_proj"), "vec_fm": vec,
        "post_mix_gain": f("post_mix_gain"), "post_ffn_gain": f("post_ffn_gain"),
        "sinks": np.ascontiguousarray(np.asarray(inp["attn_sinks"], np.float32).reshape(32)),
        "invf": invf, "amask": am, "hmask": hm,
    }


def _prep_core(inp, b, NTOK):
    x = np.ascontiguousarray(np.asarray(inp["x"], np.float32)[b, :NTOK])
    pT = np.ascontiguousarray(np.transpose(np.asarray(inp["p"], np.float32)[:, b, :NTOK, :], (0, 2, 1)))
    pos = np.asarray(inp["positions"])[b, :NTOK].astype(np.int32)
    posT = np.ascontiguousarray(pos.reshape(-1, P).T)
    return {"x": x, "pT": pT, "posT": posT}


def kernel(**inputs):
    B, S = 4, 4096
    nc = build(S, 2)
    shared = _prep_shared(inputs)
    in_maps = []
    for b in range(B):
        m = dict(shared)
        m.update(_prep_core(inputs, b, S))
        in_maps.append(m)
    res = run_bass_kernel_spmd(nc, in_maps, core_ids=list(range(B)))
    out = np.stack([np.asarray(res.results[b]["y"], np.float32) for b in range(B)], 0)
    return out
```

```python
import math
from contextlib import ExitStack

import numpy as np
import concourse.bass as bass
import concourse.mybir as mybir
from concourse.bass_utils import run_bass_kernel_spmd

F32 = mybir.dt.float32
BF16 = mybir.dt.bfloat16
I32 = mybir.dt.int32
AF = mybir.ActivationFunctionType
ALU = mybir.AluOpType
AX = mybir.AxisListType

P = 128
D = 2048
TG = 512
NS = 4
DFF = 5632
DIN = 5632
NQH = 16
NKV = 4
HD = 64
NHH = 8
EPS = 1e-6
MASKV = -30000.0
NVEC = 72


class Buf:
    __slots__ = ("name", "w", "r")

    def __init__(self, name):
        self.name = name
        self.w = None
        self.r = {}


class DSem:
    ALL = []

    def __init__(self, nc, name):
        self.sem = nc.alloc_semaphore(name)
        self.cnt = 0
        self.key = name
        DSem.ALL.append(self)


class Ctx:
    def __init__(self, nc):
        self.nc = nc
        self.E = {"pe": nc.tensor, "act": nc.scalar, "dve": nc.vector, "pool": nc.gpsimd, "sp": nc.sync}
        self.sem = {}
        self.cnt = {}
        for e in self.E:
            self.sem[e] = nc.alloc_semaphore("s_" + e)
            self.cnt[e] = 0
        self.waited = {}
        self.nwaits = 0
        self.nops = 0

    def _wait(self, e, sig):
        if sig is None:
            return
        sem, val, key = sig
        k = (e, key)
        if self.waited.get(k, 0) >= val:
            return
        self.E[e].wait_ge(sem, val)
        self.waited[k] = val
        self.nwaits += 1

    def _pre(self, e, reads, writes):
        for b in reads:
            if not (e == "pe" and b.w is not None and b.w[2] == "pe"):
                self._wait(e, b.w)
        for b in writes:
            if not (e == "pe" and b.w is not None and b.w[2] == "pe"):
                self._wait(e, b.w)
            for rk, rs in b.r.items():
                if not (e == "pe" and rk == "pe"):
                    self._wait(e, rs)

    def _post(self, sig, reads, writes):
        for b in reads:
            b.r[sig[2]] = sig
        for b in writes:
            b.w = sig
            b.r = {}

    def op(self, e, fn, reads=(), writes=()):
        self._pre(e, reads, writes)
        inst = fn()
        self.cnt[e] += 1
        inst.then_inc(self.sem[e], 1)
        sig = (self.sem[e], self.cnt[e], e)
        self._post(sig, reads, writes)
        self.nops += 1
        return sig

    def dma(self, q, out, in_, dsem, reads=(), writes=()):
        self._pre(q, reads, writes)
        inst = self.E[q].dma_start(out=out, in_=in_)
        dsem.cnt += 16
        inst.then_inc(dsem.sem, 16)
        sig = (dsem.sem, dsem.cnt, dsem.key)
        self._post(sig, reads, writes)
        return sig

    def barrier(self, engines=("pe", "act", "dve")):
        for e in engines:
            for e2 in engines:
                if self.cnt[e2] > 0:
                    self._wait(e, (self.sem[e2], self.cnt[e2], e2))


class _Stop(Exception):
    pass


def build(NTOK, NL, dbg=None):
    NG = NTOK // TG
    NB = NTOK // P
    nc = bass.Bass("TRN2", target_bir_lowering=False)
    DSem.ALL = []
    V, A, T, G = nc.vector, nc.scalar, nc.tensor, nc.gpsimd

    def din(name, shape, dt=F32):
        return nc.dram_tensor(name, shape, dt, kind="ExternalInput").ap()

    x_d = din("x", [NTOK, D])
    pT_d = din("pT", [2, 256, NTOK])
    pos_d = din("posT", [P, NB], I32)
    w_in_d = din("w_in", [2, D, DIN])
    w_out_d = din("w_out", [2, D, D])
    w_g_d = din("w_gate", [2, D, DFF])
    w_u_d = din("w_up", [2, D, DFF])
    w_d_d = din("w_down", [2, DFF, D])
    w_pg_d = din("w_pg", [2, D, D])
    w_pp_d = din("w_pp", [2, 256, D])
    vec_d = din("vec_fm", [P, 2 * NVEC])
    pmg_d = din("post_mix_gain", [2, D])
    pfg_d = din("post_ffn_gain", [2, D])
    sink_d = din("sinks", [32])
    invf_d = din("invf", [32])
    amask_d = din("amask", [4, P, 256])
    hmask_d = din("hmask", [P, P])
    y_d = nc.dram_tensor("y", [NTOK, D], F32, kind="ExternalOutput").ap()
    dbg_d = None
    if dbg is not None:
        dbg_d = nc.dram_tensor("dbg", [P, 16 * TG], F32, kind="ExternalOutput").ap()

    ctx = Ctx(nc)
    top = ExitStack()
    dsem_dbg = DSem(nc, "d_dbg")

    def dump(stage, ap2d, ncols):
        if dbg is None or dbg != stage:
            return
        ctx.barrier(("pe", "act", "dve", "pool"))
        nc.gpsimd.dma_start(out=dbg_d[:, 0:ncols], in_=ap2d).then_inc(dsem_dbg.sem, 16)
        dsem_dbg.cnt = 16
        for d_ in DSem.ALL:
            if d_.cnt > 0:
                nc.gpsimd.wait_ge(d_.sem, d_.cnt)
        raise _Stop()

    uniq = {"n": 0}

    def sb(name, shape, dt, st=top):
        uniq["n"] += 1
        t = st.enter_context(nc.sbuf_tensor(f"sb_{name}_{uniq['n']}", list(shape), dt))
        return t, Buf(name)

    def ps(name, shape, dt):
        t = top.enter_context(nc.psum_tensor("ps_" + name, list(shape), dt))
        return t, Buf(name)

    x_res, _ = sb("x_res", [P, NS, D], F32)
    xb = [Buf(f"x{s}") for s in range(NS)]
    actT, _ = sb("actT", [P, 16, TG], BF16)
    actb = [Buf(f"actT{k}") for k in range(16)]
    R_hid, _ = sb("hidT", [P, 44, TG], BF16)
    hidb = [Buf(f"hid{k}") for k in range(44)]
    mixT = R_hid
    WS = 3
    wslots = [sb(f"wslot{i}", [P, 16, 512], BF16) for i in range(WS)]
    wsems = [DSem(nc, f"d_w{i}") for i in range(WS)]
    Sst, Sst_b = sb("Sst", [P, 2, NHH, P], F32)
    kTb, kTb_b = sb("kTb", [P, 2, 2, NKV, 2, P], BF16)
    Vb, Vb_b = sb("Vb", [P, 2, 2, NKV, HD], BF16)
    cs_t, cs_b = sb("cs", [P, 2, NS, 32], F32)
    amask, amask_b = sb("amask", [P, 4, 256], F32)
    hmask, hmask_b = sb("hmask", [P, P], F32)
    vec, vec_b = sb("vec", [P, 2 * NVEC], F32)
    ident, ident_b = sb("ident", [P, P], BF16)
    ones_f, ones_b = sb("ones_f", [P, P], F32)
    sinkt, sink_b = sb("sinkt", [P, 2, 32], F32)
    invf, invf_b = sb("invf", [P, 32], F32)
    posi, posi_b = sb("posi", [P, NB], I32)
    posf, posf_b = sb("posf", [P, NB], F32)
    lbt, lbt_b = sb("lbt", [P, 2, 2, NHH], F32)
    smallc, smallc_b = sb("smallc", [P, 8], F32)
    pTt, pT_b = sb("pTt", [P, 2, TG], BF16)

    pbank = [ps(f"pb{i}", [P, 512], F32) for i in range(5)]
    pdS = ps("pdS", [P, 512], F32)
    ptb = [ps(f"ptb{i}", [P, 1024], BF16) for i in range(2)]
    rot = {"pb": 0, "pt": 0}

    def next_bank():
        i = rot["pb"]
        rot["pb"] = (i + 1) % 4
        return pbank[i]

    def next_pt():
        i = rot["pt"]
        rot["pt"] = (i + 1) % 2
        return ptb[i]

    flip = {"e": 0}

    def evac_engine():
        flip["e"] ^= 1
        return "act" if flip["e"] else "dve"

    def copy_op(e, out, in_, reads, writes):
        if e == "act":
            return ctx.op("act", lambda: A.copy(out=out, in_=in_), reads, writes)
        return ctx.op("dve", lambda: V.tensor_copy(out=out, in_=in_), reads, writes)

    def scale_op(e, out, in_, sc_ap, reads, writes):
        if e == "act":
            return ctx.op("act", lambda: A.activation(out=out, in_=in_, func=AF.Identity, scale=sc_ap), reads, writes)
        return ctx.op("dve", lambda: V.tensor_scalar(out=out, in0=in_, scalar1=sc_ap, scalar2=None, op0=ALU.mult), reads, writes)

    def setup_load(name, out, in_, buf, q="sp"):
        ctx.dma(q, out, in_, DSem(nc, "d_" + name), reads=(), writes=(buf,))

    setup_load("vec", vec[:], vec_d[:, :], vec_b)
    setup_load("amask", amask[:], amask_d.rearrange("v p k -> p v k"), amask_b)
    setup_load("hmask", hmask[:], hmask_d[:, :], hmask_b)
    setup_load("sink", sinkt[:, 0, :], sink_d.partition_broadcast(P), sink_b)
    setup_load("invf", invf[:], invf_d.partition_broadcast(P), invf_b)
    setup_load("pos", posi[:], pos_d[:, :], posi_b)

    ctx.op("dve", lambda: V.memset(ones_f[:], 1.0), (), (ones_b,))
    ctx.op("dve", lambda: V.memset(ident[:], 1.0), (), (ident_b,))
    ctx.op("pool", lambda: G.affine_select(out=ident[:], in_=ident[:], pattern=[[-1, P]], compare_op=ALU.is_equal,
                                           fill=0.0, base=0, channel_multiplier=1), (ident_b,), (ident_b,))
    ctx.op("dve", lambda: V.memset(Sst[:], 0.0), (), (Sst_b,))
    ctx.op("dve", lambda: V.memset(kTb[:], 0.0), (), (kTb_b,))
    ctx.op("dve", lambda: V.memset(Vb[:], 0.0), (), (Vb_b,))
    ctx.op("dve", lambda: V.tensor_copy(out=posf[:], in_=posi[:]), (posi_b,), (posf_b,))
    ctx.op("dve", lambda: V.tensor_scalar(out=sinkt[:, 1, :], in0=sinkt[:, 0, :], scalar1=-1.0, scalar2=None, op0=ALU.mult),
           (sink_b,), (sink_b,))
    LB0 = 64
    l0 = vec[:, LB0:LB0 + 8]
    l1 = vec[:, NVEC + LB0:NVEC + LB0 + 8]
    with ExitStack() as st:
        tm, tm_b = sb("lb_m", [P, 8], F32, st)
        e0, e0_b = sb("lb_e0", [P, 8], F32, st)
        e1, e1_b = sb("lb_e1", [P, 8], F32, st)
        ctx.op("dve", lambda: V.tensor_tensor(out=tm[:], in0=l0, in1=l1, op=ALU.max), (vec_b,), (tm_b,))
        ctx.op("dve", lambda: V.tensor_tensor(out=e0[:], in0=l0, in1=tm[:], op=ALU.subtract), (vec_b, tm_b), (e0_b,))
        ctx.op("dve", lambda: V.tensor_tensor(out=e1[:], in0=l1, in1=tm[:], op=ALU.subtract), (vec_b, tm_b), (e1_b,))
        ctx.op("act", lambda: A.activation(out=e0[:], in_=e0[:], func=AF.Exp), (e0_b,), (e0_b,))
        ctx.op("act", lambda: A.activation(out=e1[:], in_=e1[:], func=AF.Exp), (e1_b,), (e1_b,))
        ctx.op("dve", lambda: V.tensor_tensor(out=tm[:], in0=e0[:], in1=e1[:], op=ALU.add), (e0_b, e1_b), (tm_b,))
        ctx.op("dve", lambda: V.reciprocal(out=tm[:], in_=tm[:]), (tm_b,), (tm_b,))
        ctx.op("dve", lambda: V.memset(lbt[:, 0, 0, :], 0.0), (), (lbt_b,))
        ctx.op("dve", lambda: V.tensor_tensor(out=lbt[:, 1, 0, :], in0=e1[:], in1=tm[:], op=ALU.mult), (e1_b, tm_b), (lbt_b,))
        ctx.op("dve", lambda: V.tensor_scalar(out=lbt[:, :, 1, :], in0=lbt[:, :, 0, :], scalar1=-1.0, scalar2=1.0,
                                              op0=ALU.mult, op1=ALU.add), (lbt_b,), (lbt_b,))
        ctx.barrier()

    def weight_plan():
        for g in range(NG):
            for l in range(NL):
                for t in range(11):
                    yield ("in", w_in_d[l, :, t * 512:(t + 1) * 512].rearrange("(kc p) n -> p kc n", p=P), 16)
                for fb in range(4):
                    yield ("out", w_out_d[l, :, fb * 512:(fb + 1) * 512].rearrange("(kc p) n -> p kc n", p=P), 16)
                for t in range(11):
                    yield ("gate", w_g_d[l, :, t * 512:(t + 1) * 512].rearrange("(kc p) n -> p kc n", p=P), 16)
                    yield ("up", w_u_d[l, :, t * 512:(t + 1) * 512].rearrange("(kc p) n -> p kc n", p=P), 16)
                for fb in range(4):
                    for kp in range(4):
                        yield ("down", w_d_d[l, kp * 1408:(kp + 1) * 1408, fb * 512:(fb + 1) * 512]
                               .rearrange("(kc p) n -> p kc n", p=P), 11)
                for fb in range(4):
                    yield ("pg", w_pg_d[l, :, fb * 512:(fb + 1) * 512].rearrange("(kc p) n -> p kc n", p=P), 16)
                    yield ("pp", w_pp_d[l, :, fb * 512:(fb + 1) * 512].rearrange("(kc p) n -> p kc n", p=P), 2)

    wplan = list(weight_plan())
    wst = {"issued": 0, "consumed": 0}

    def w_issue(upto):
        while wst["issued"] < min(upto, len(wplan)):
            i = wst["issued"]
            tag, src, kc = wplan[i]
            t, b = wslots[i % WS]
            ctx.dma("pool", t[:, 0:kc, :], src, wsems[i % WS], reads=(), writes=(b,))
            wst["issued"] += 1

    def w_get(tag, held=0):
        i = wst["consumed"]
        assert wplan[i][0] == tag, (wplan[i][0], tag)
        w_issue(i + WS - held)
        wst["consumed"] += 1
        return wslots[i % WS]

    xsem = DSem(nc, "d_x")
    ysem = DSem(nc, "d_y")
    psem = DSem(nc, "d_p")
    gsem = DSem(nc, "d_g")

    def rstd_from_ss(ss, ss_b, n, width):
        ctx.op("act", lambda: A.activation(out=ss[:, 0:n], in_=ss[:, 0:n], func=AF.Ln, scale=1.0 / width, bias=eps_ap),
               (ss_b, smallc_b), (ss_b,))
        ctx.op("act", lambda: A.activation(out=ss[:, 0:n], in_=ss[:, 0:n], func=AF.Exp, scale=-0.5), (ss_b,), (ss_b,))

    ctx.op("dve", lambda: V.memset(smallc[:, 0:1], EPS), (), (smallc_b,))
    eps_ap = smallc[:, 0:1]

    def norm_to_actT(gcol):
        with ExitStack() as st:
            hb4, hb4_b = sb("hb4", [P, NS, D], BF16, st)
            ss, ss_b = sb("nss", [P, NS], F32, st)
            hbs = [Buf(f"hb{s}") for s in range(NS)]
            for s in range(NS):
                ctx.op("act", lambda: A.activation(out=hb4[:, s, :], in_=x_res[:, s, :], func=AF.Square,
                                                   accum_out=ss[:, s:s + 1]), (xb[s],), (hbs[s], ss_b))
            rstd_from_ss(ss, ss_b, NS, D)
            for s in range(NS):
                scale_op("dve" if s % 2 else "act", hb4[:, s, :], x_res[:, s, :], ss[:, s:s + 1], (xb[s], ss_b), (hbs[s],))
            for kc in range(16):
                pt, pt_b = next_pt()

                def tr():
                    for s in range(NS):
                        i = T.transpose(out=pt[:, s * P:(s + 1) * P], in_=hb4[:, s, kc * P:(kc + 1) * P], identity=ident[:])
                    return i
                ctx.op("pe", tr, hbs + [ident_b], (pt_b,))
                scale_op(evac_engine(), actT[:, kc, :], pt[:, 0:TG], vec[:, gcol + kc:gcol + kc + 1], (pt_b, vec_b), (actb[kc],))
            ctx.barrier()

    def residual_norm_add(ybuf, yb, gain_d_row):
        with ExitStack() as st:
            junk = actT[:, 0:4, :].rearrange("p a b -> p (a b)")
            junk_bs = actb[0:4]
            gbc = actT[:, 4:12, :].rearrange("p a b -> p (a b)").bitcast(F32)
            gbc_bs = actb[4:12]
            ss, ss_b = sb("rss", [P, NS], F32, st)
            ctx.dma("sp", gbc, gain_d_row.partition_broadcast(P), gsem, reads=(), writes=tuple(gbc_bs))
            for s in range(NS):
                ctx.op("act", lambda: A.activation(out=junk, in_=ybuf[:, s, :], func=AF.Square, accum_out=ss[:, s:s + 1]),
                       (yb[s],), tuple(junk_bs) + (ss_b,))
            rstd_from_ss(ss, ss_b, NS, D)
            for s in range(NS):
                ctx.op("dve", lambda: V.scalar_tensor_tensor(out=ybuf[:, s, :], in0=ybuf[:, s, :], scalar=ss[:, s:s + 1], in1=gbc,
                                                             op0=ALU.mult, op1=ALU.mult), (yb[s], ss_b) + tuple(gbc_bs), (yb[s],))
                ctx.op("dve", lambda: V.tensor_tensor(out=x_res[:, s, :], in0=x_res[:, s, :], in1=ybuf[:, s, :], op=ALU.add),
                       (xb[s], yb[s]), (xb[s],))
            ctx.barrier()

    def tokmajor_proj(tag, srcT, srcb, nk, ybuf, yb, fb):
        wt, wb = w_get(tag)
        for s in range(NS):
            bk, bk_b = next_bank()

            def mm():
                for kc in range(nk):
                    i = T.matmul(bk[:], lhsT=srcT[:, kc, s * P:(s + 1) * P], rhs=wt[:, kc, :], start=(kc == 0), stop=(kc == nk - 1))
                return i
            ctx.op("pe", mm, list(srcb[0:nk]) + [wb], (bk_b,))
            copy_op(evac_engine(), ybuf[:, s, fb * 512:(fb + 1) * 512], bk[:], (bk_b,), (yb[s],))

    def main_body(NG):
        for g in range(NG):
            ctx.dma("sp", x_res[:], x_d[g * TG:(g + 1) * TG, :].rearrange("(s p) d -> p s d", p=P), xsem, reads=(), writes=tuple(xb))
            with ExitStack() as st:
                ang, ang_b = sb("ang", [P, NS, 32], F32, st)
                kk, kk_b = sb("kk", [P, NS, 32], F32, st)
                ki, ki_b = sb("ki", [P, NS, 32], I32, st)
                yy, yy_b = sb("yy", [P, NS, 32], F32, st)
                mk, mk_b = sb("mk", [P, NS, 32], F32, st)
                C1 = 6.28125
                C2 = 2.0 * math.pi - C1
                ctx.op("dve", lambda: V.tensor_tensor(out=ang[:], in0=invf[:].unsqueeze(1).to_broadcast([P, NS, 32]),
                                                      in1=posf[:, g * NS:(g + 1) * NS].unsqueeze(2).to_broadcast([P, NS, 32]),
                                                      op=ALU.mult), (invf_b, posf_b), (ang_b,))
                for which, shift in ((1, 0.0), (0, math.pi / 2)):
                    ctx.op("dve", lambda: V.tensor_scalar(out=kk[:], in0=ang[:], scalar1=shift, scalar2=1.0 / (2 * math.pi),
                                                          op0=ALU.add, op1=ALU.mult), (ang_b,), (kk_b,))
                    ctx.op("dve", lambda: V.tensor_copy(out=ki[:], in_=kk[:]), (kk_b,), (ki_b,))
                    ctx.op("dve", lambda: V.tensor_copy(out=kk[:], in_=ki[:]), (ki_b,), (kk_b,))
                    ctx.op("dve", lambda: V.scalar_tensor_tensor(out=yy[:], in0=kk[:], scalar=-C1, in1=ang[:], op0=ALU.mult, op1=ALU.add),
                           (kk_b, ang_b), (yy_b,))
                    ctx.op("dve", lambda: V.scalar_tensor_tensor(out=yy[:], in0=kk[:], scalar=-C2, in1=yy[:], op0=ALU.mult, op1=ALU.add),
                           (kk_b, yy_b), (yy_b,))
                    if shift != 0.0:
                        ctx.op("dve", lambda: V.tensor_scalar(out=yy[:], in0=yy[:], scalar1=shift, scalar2=None, op0=ALU.add), (yy_b,), (yy_b,))
                    ctx.op("dve", lambda: V.tensor_scalar(out=mk[:], in0=yy[:], scalar1=math.pi, scalar2=-2 * math.pi, op0=ALU.is_gt, op1=ALU.mult),
                           (yy_b,), (mk_b,))
                    ctx.op("dve", lambda: V.tensor_tensor(out=yy[:], in0=yy[:], in1=mk[:], op=ALU.add), (yy_b, mk_b), (yy_b,))
                    ctx.op("dve", lambda: V.tensor_scalar(out=mk[:], in0=yy[:], scalar1=-math.pi, scalar2=2 * math.pi, op0=ALU.is_lt, op1=ALU.mult),
                           (yy_b,), (mk_b,))
                    ctx.op("dve", lambda: V.tensor_tensor(out=yy[:], in0=yy[:], in1=mk[:], op=ALU.add), (yy_b, mk_b), (yy_b,))
                    ctx.op("dve", lambda: V.tensor_scalar(out=yy[:], in0=yy[:], scalar1=3.1415925, scalar2=-3.1415925, op0=ALU.min, op1=ALU.max),
                           (yy_b,), (yy_b,))
                    ctx.op("act", lambda: A.activation(out=cs_t[:, which, :, :], in_=yy[:], func=AF.Sin), (yy_b,), (cs_b,))
                ctx.barrier()

            for l in range(NL):
                vc = l * NVEC

                norm_to_actT(vc + 0)
                dump("actT", actT[:].rearrange("p a b -> p (a b)"), 16 * TG)
                with ExitStack() as st:
                    qkv = R_hid[:, 16:40, :].rearrange("p a b -> p (a b)").bitcast(F32).rearrange("p (s c) -> p s c", c=1536)
                    qkvb = [Buf(f"qkv{s}") for s in range(NS)]
                    an4, _ = sb("an4", [P, NS, 1024], BF16, st)
                    an4b = [Buf(f"an4_{s}") for s in range(NS)]
                    for t in range(3):
                        wt, wb = w_get("in")
                        for s in range(NS):
                            bk, bk_b = next_bank()

                            def mm():
                                for kc in range(16):
                                    i = T.matmul(bk[:], lhsT=actT[:, kc, s * P:(s + 1) * P], rhs=wt[:, kc, :], start=(kc == 0), stop=(kc == 15))
                                return i
                            ctx.op("pe", mm, actb + [wb], (bk_b,))
                            copy_op(evac_engine(), qkv[:, s, t * 512:(t + 1) * 512], bk[:], (bk_b,), (qkvb[s],))

                    with ExitStack() as st2:
                        t1, t1_b = sb("rt1", [P, 20, 32], F32, st2)
                        t2, t2_b = sb("rt2", [P, 20, 32], F32, st2)
                        qr, qr_b = sb("qr", [P, NQH, HD], BF16, st2)
                        kdup, kdup_b = sb("kdup", [P, NKV, 2, HD], BF16, st2)
                        qT, qT_b = sb("qT", [P, 8, P], BF16, st2)
                        NR = 3
                        Ssb = [sb(f"Ssb{i}", [P, 2, 256], F32, st2) for i in range(NR)]
                        eb = [sb(f"eb{i}", [P, 2, 256], BF16, st2) for i in range(NR)]
                        eT = [sb(f"eT{i}", [P, 4, P], BF16, st2) for i in range(NR)]
                        stat, stat_b = sb("stat", [P, 8, NQH], F32, st2)
                        atok, atok_b = sb("atok", [P, 1024], F32, st2)
                        ajunk, ajunk_b = sb("ajunk", [P, 1024], BF16, st2)
                        ass, ass_b = sb("ass", [P, NS], F32, st2)

                        for s in range(NS):
                            bi = g * NS + s
                            slot = s % 2
                            first = (bi == 0)
                            mvar = (2 if first else 0) + slot
                            cosb = cs_t[:, 0, s, :].unsqueeze(1).to_broadcast([P, 20, 32])
                            sinb = cs_t[:, 1, s, :].unsqueeze(1).to_broadcast([P, 20, 32])
                            qk = qkv[:, s, 0:1280].rearrange("p (h d) -> p h d", d=HD)
                            x1 = qk[:, :, 0:32]
                            x2 = qk[:, :, 32:64]
                            ctx.op("dve", lambda: V.tensor_tensor(out=t1[:], in0=x1, in1=cosb, op=ALU.mult), (qkvb[s], cs_b), (t1_b,))
                            ctx.op("dve", lambda: V.tensor_tensor(out=t2[:], in0=x2, in1=sinb, op=ALU.mult), (qkvb[s], cs_b), (t2_b,))
                            ctx.op("dve", lambda: V.tensor_tensor(out=qr[:, :, 0:32], in0=t1[:, 0:16, :], in1=t2[:, 0:16, :], op=ALU.subtract),
                                   (t1_b, t2_b), (qr_b,))
                            for dd in range(2):
                                ctx.op("dve", lambda: V.tensor_tensor(out=kdup[:, :, dd, 0:32], in0=t1[:, 16:20, :], in1=t2[:, 16:20, :], op=ALU.subtract),
                                       (t1_b, t2_b), (kdup_b,))
                            ctx.op("dve", lambda: V.tensor_tensor(out=t1[:], in0=x2, in1=cosb, op=ALU.mult), (qkvb[s], cs_b), (t1_b,))
                            ctx.op("dve", lambda: V.tensor_tensor(out=t2[:], in0=x1, in1=sinb, op=ALU.mult), (qkvb[s], cs_b), (t2_b,))
                            ctx.op("dve", lambda: V.tensor_tensor(out=qr[:, :, 32:64], in0=t1[:, 0:16, :], in1=t2[:, 0:16, :], op=ALU.add),
                                   (t1_b, t2_b), (qr_b,))
                            for dd in range(2):
                                ctx.op("dve", lambda: V.tensor_tensor(out=kdup[:, :, dd, 32:64], in0=t1[:, 16:20, :], in1=t2[:, 16:20, :], op=ALU.add),
                                       (t1_b, t2_b), (kdup_b,))
                            ctx.op("act", lambda: A.copy(out=Vb[:, l, slot, :, :], in_=qkv[:, s, 1280:1536].rearrange("p (g d) -> p g d", d=HD)),
                                   (qkvb[s],), (Vb_b,))
                            qrf = qr[:].rearrange("p h d -> p (h d)")
                            for half in range(2):
                                pt, pt_b = next_pt()

                                def trq():
                                    for j in range(4):
                                        jj = half * 4 + j
                                        i = T.transpose(out=pt[:, j * P:(j + 1) * P], in_=qrf[:, jj * P:(jj + 1) * P], identity=ident[:])
                                    return i
                                ctx.op("pe", trq, (qr_b, ident_b), (pt_b,))
                                copy_op(evac_engine(), qT[:, half * 4:(half + 1) * 4, :], pt[:, 0:512].rearrange("p (j t) -> p j t", t=P),
                                        (pt_b,), (qT_b,))
                            pt, pt_b = next_pt()
                            kdf = kdup[:].rearrange("p g t d -> p g (t d)")

                            def trk():
                                for gg in range(NKV):
                                    i = T.transpose(out=pt[:, gg * P:(gg + 1) * P], in_=kdf[:, gg, :], identity=ident[:])
                                return i
                            ctx.op("pe", trk, (kdup_b, ident_b), (pt_b,))
                            copy_op("act", kTb[0:64, l, 0, :, slot, :], pt[0:64, 0:512].rearrange("p (j t) -> p j t", t=P), (pt_b,), (kTb_b,))
                            copy_op("dve", kTb[64:128, l, 1, :, slot, :], pt[64:128, 0:512].rearrange("p (j t) -> p j t", t=P), (pt_b,), (kTb_b,))

                            obk = [pbank[4], pdS]

                            def stage1(j):
                                gq = j // 2
                                bk, bk_b = next_bank()
                                S_t, S_b = Ssb[j % NR]
                                e_t, e_b = eb[j % NR]

                                def mm():
                                    for i2 in range(2):
                                        i = T.matmul(bk[:, i2 * 256:(i2 + 1) * 256], lhsT=qT[:, j, :],
                                                     rhs=kTb[:, l, i2, gq, :, :].rearrange("p s t -> p (s t)"), start=True, stop=True)
                                    return i
                                ctx.op("pe", mm, (qT_b, kTb_b), (bk_b,))
                                for i2 in range(2):
                                    ctx.op("dve", lambda: V.scalar_tensor_tensor(out=S_t[:, i2, :], in0=bk[:, i2 * 256:(i2 + 1) * 256], scalar=0.125,
                                                                                 in1=amask[:, mvar, :], op0=ALU.mult, op1=ALU.add),
                                           (bk_b, amask_b), (S_b,))
                                h0 = 2 * j
                                ctx.op("dve", lambda: V.tensor_reduce(out=stat[:, 0, h0:h0 + 2], in_=S_t[:], axis=AX.X, op=ALU.max, negate=True),
                                       (S_b,), (stat_b,))
                                ctx.op("dve", lambda: V.tensor_tensor(out=stat[:, 0, h0:h0 + 2], in0=stat[:, 0, h0:h0 + 2],
                                                                      in1=sinkt[:, 1, l * 16 + h0:l * 16 + h0 + 2], op=ALU.min), (stat_b, sink_b), (stat_b,))
                                for i2 in range(2):
                                    ctx.op("act", lambda: A.activation(out=e_t[:, i2, :], in_=S_t[:, i2, :], func=AF.Exp,
                                                                       bias=stat[:, 0, h0 + i2:h0 + i2 + 1], scale=1.0,
                                                                       accum_out=stat[:, 1, h0 + i2:h0 + i2 + 1]), (S_b, stat_b), (e_b, stat_b))

                            def stage2(j):
                                e_t, e_b = e_bufs = eb[j % NR]
                                eT_t, eT_b = eT[j % NR]
                                pt, pt_b = next_pt()

                                def tr():
                                    for i2 in range(2):
                                        for sl in range(2):
                                            i = T.transpose(out=pt[:, (i2 * 2 + sl) * P:(i2 * 2 + sl + 1) * P],
                                                            in_=e_t[:, i2, sl * P:(sl + 1) * P], identity=ident[:])
                                    return i
                                ctx.op("pe", tr, (e_b, ident_b), (pt_b,))
                                copy_op(evac_engine(), eT_t[:], pt[:, 0:512].rearrange("p (j t) -> p j t", t=P), (pt_b,), (eT_b,))

                            def stage3(j):
                                gq = j // 2
                                eT_t, eT_b = eT[j % NR]
                                for i2 in range(2):
                                    h = 2 * j + i2
                                    ob, ob_b = obk[h // 8]
                                    col = (h % 8) * HD

                                    def mm():
                                        T.matmul(ob[:, col:col + HD], lhsT=eT_t[:, i2 * 2 + 0, :], rhs=Vb[:, l, 0, gq, :], start=True, stop=False)
                                        return T.matmul(ob[:, col:col + HD], lhsT=eT_t[:, i2 * 2 + 1, :], rhs=Vb[:, l, 1, gq, :], start=False, stop=True)
                                    ctx.op("pe", mm, (eT_b, Vb_b), (ob_b,))

                            for step in range(8 + 2):
                                if step < 8:
                                    stage1(step)
                                if 0 <= step - 1 < 8:
                                    stage2(step - 1)
                                if 0 <= step - 2 < 8:
                                    stage3(step - 2)
                            ctx.op("dve", lambda: V.tensor_tensor(out=stat[:, 2, :], in0=stat[:, 0, :], in1=sinkt[:, 0, l * 16:(l + 1) * 16], op=ALU.add),
                                   (stat_b, sink_b), (stat_b,))
                            ctx.op("act", lambda: A.activation(out=stat[:, 3, :], in_=stat[:, 2, :], func=AF.Exp), (stat_b,), (stat_b,))
                            ctx.op("dve", lambda: V.tensor_tensor(out=stat[:, 3, :], in0=stat[:, 3, :], in1=stat[:, 1, :], op=ALU.add), (stat_b,), (stat_b,))
                            ctx.op("dve", lambda: V.reciprocal(out=stat[:, 4, :], in_=stat[:, 3, :]), (stat_b,), (stat_b,))
                            for hb_ in range(2):
                                ob, ob_b = obk[hb_]
                                ctx.op("dve", lambda: V.tensor_tensor(out=atok[:, hb_ * 512:(hb_ + 1) * 512].rearrange("p (h d) -> p h d", d=HD),
                                                                      in0=ob[:].rearrange("p (h d) -> p h d", d=HD),
                                                                      in1=stat[:, 4, hb_ * 8:(hb_ + 1) * 8].unsqueeze(2).to_broadcast([P, 8, HD]),
                                                                      op=ALU.mult), (ob_b, stat_b), (atok_b,))
                            ctx.op("act", lambda: A.activation(out=ajunk[:], in_=atok[:], func=AF.Square, accum_out=ass[:, s:s + 1]),
                                   (atok_b,), (ajunk_b, ass_b))
                            rstd_from_ss(ass[:, s:s + 1], ass_b, 1, 1024)
                            scale_op("act", an4[:, s, :], atok[:], ass[:, s:s + 1], (atok_b, ass_b), (an4b[s],))
                        ctx.barrier()
                    for j in range(8):
                        pt, pt_b = next_pt()

                        def tr():
                            for s in range(NS):
                                i = T.transpose(out=pt[:, s * P:(s + 1) * P], in_=an4[:, s, j * P:(j + 1) * P], identity=ident[:])
                            return i
                        ctx.op("pe", tr, an4b + [ident_b], (pt_b,))
                        scale_op(evac_engine(), mixT[:, j, :], pt[:, 0:TG], vec[:, vc + 48 + j:vc + 48 + j + 1], (pt_b, vec_b), (hidb[j],))
                    ctx.barrier()

                dump("attn", mixT[:, 0:8, :].rearrange("p a b -> p (a b)"), 8 * TG)
                with ExitStack() as st:
                    def f32t(name, shape=(P, TG)):
                        return sb(name, list(shape), F32, st)
                    ff, ff_b = f32t("h_f")
                    qs, qs_b = f32t("h_qs")
                    sgate, sgate_b = f32t("h_sgate")
                    vT, vT_b = sb("h_vT", [P, TG], BF16, st)
                    bT, bT_b = f32t("h_b")
                    dd_, dd_b = f32t("h_d")
                    E1, E1_b = f32t("h_E1")
                    kin, kin_b = f32t("h_kin")
                    qt, qt_b = sb("h_qt", [P, TG], BF16, st)
                    kt, kt_b = sb("h_kt", [P, TG], BF16, st)
                    vtok, vtok_b = sb("h_vtok", [P, 4, P], BF16, st)
                    ktokA, ktokA_b = sb("h_ktokA", [P, 4, P], BF16, st)
                    ktokB, ktokB_b = sb("h_ktokB", [P, 4, P], BF16, st)
                    AT, AT_b = sb("h_AT", [P, 4, P], BF16, st)
                    T3, T3_b = f32t("h_T3", (P, P, 9))
                    D3, D3_b = f32t("h_D3", (P, P, 9))
                    S3, S3_b = f32t("h_S3", (P, P, 9))
                    Sra, Sra_b = sb("h_Sra", [P, 8, P], BF16, st)
                    sq, sq_b = dd_, dd_b
                    rs_, rs_b = bT, bT_b
                    sc, sc_b = f32t("h_sc", (P, 4, 8))
                    rmask, rmask_b = f32t("h_rmask")
                    ctx.op("dve", lambda: V.memset(ktokA[:], 0.0), (), (ktokA_b,))
                    ctx.op("dve", lambda: V.memset(ktokB[:], 0.0), (), (ktokB_b,))
                    ctx.op("dve", lambda: V.memset(D3[:], 0.0), (), (D3_b,))
                    ctx.op("dve", lambda: V.memset(rmask[:], 1.0), (), (rmask_b,))
                    ctx.op("dve", lambda: V.memset(rmask[:].rearrange("p (c t) -> p c t", t=64)[:, :, 0:1], 0.0), (), (rmask_b,))

                    for h in range(NHH):
                        wt, wb = w_get("in")
                        banks = []
                        for c in range(4):
                            bk, bk_b = next_bank()

                            def mm():
                                for kc in range(16):
                                    i = T.matmul(bk[:], lhsT=wt[:, kc, c * P:(c + 1) * P], rhs=actT[:, kc, :], start=(kc == 0), stop=(kc == 15))
                                return i
                            ctx.op("pe", mm, actb + [wb], (bk_b,))
                            banks.append((bk, bk_b))
                        (bq, bq_b), (bf_, bf_b), (bi_, bi_b), (bg, bg_b) = banks
                        ctx.op("act", lambda: A.activation(out=ff[:], in_=bf_[:], func=AF.Sigmoid), (bf_b,), (ff_b,))
                        ctx.op("act", lambda: A.activation(out=qs[:], in_=bq[:], func=AF.Silu), (bq_b,), (qs_b,))
                        ctx.op("act", lambda: A.copy(out=vT[:], in_=bi_[:]), (bi_b,), (vT_b,))
                        ctx.op("act", lambda: A.activation(out=sgate[:], in_=bg[:], func=AF.Silu), (bg_b,), (sgate_b,))
                        ctx.op("dve", lambda: V.tensor_scalar(out=ff[:], in0=ff[:], scalar1=lbt[:, l, 1, h:h + 1], scalar2=lbt[:, l, 0, h:h + 1],
                                                              op0=ALU.mult, op1=ALU.add), (ff_b, lbt_b), (ff_b,))
                        ctx.op("dve", lambda: V.tensor_scalar(out=kin[:], in0=ff[:], scalar1=-1.0, scalar2=1.0, op0=ALU.mult, op1=ALU.add),
                               (ff_b,), (kin_b,))
                        ctx.op("act", lambda: A.activation(out=ff[:], in_=ff[:], func=AF.Ln), (ff_b,), (ff_b,))
                        ctx.op("dve", lambda: V.tensor_tensor_scan(out=bT[:], data0=rmask[:], data1=ff[:], initial=0.0, op0=ALU.mult, op1=ALU.add),
                               (rmask_b, ff_b), (bT_b,))
                        b3 = bT[:].rearrange("p (c t) -> p c t", t=64)
                        ctx.op("dve", lambda: V.tensor_tensor(out=dd_[:].rearrange("p (c t) -> p c t", t=64), in0=b3,
                                                              in1=b3[:, :, 31:32].to_broadcast([P, 8, 64]), op=ALU.subtract), (bT_b,), (dd_b,))
                        ctx.op("act", lambda: A.activation(out=E1[:], in_=dd_[:], func=AF.Exp), (dd_b,), (E1_b,))
                        ctx.op("dve", lambda: V.tensor_tensor(out=qt[:], in0=qs[:], in1=E1[:], op=ALU.mult), (qs_b, E1_b), (qt_b,))
                        ctx.op("act", lambda: A.activation(out=E1[:], in_=dd_[:], func=AF.Exp, scale=-1.0), (dd_b,), (E1_b,))
                        ctx.op("dve", lambda: V.tensor_tensor(out=kt[:], in0=kin[:], in1=E1[:], op=ALU.mult), (kin_b, E1_b), (kt_b,))
                        ctx.op("act", lambda: A.activation(out=sc[:, 0, :], in_=b3[:, :, 31], func=AF.Exp), (bT_b,), (sc_b,))
                        ctx.op("act", lambda: A.activation(out=sc[:, 1, :], in_=b3[:, :, 63], func=AF.Exp), (bT_b,), (sc_b,))
                        ctx.op("dve", lambda: V.tensor_tensor(out=sc[:, 3, :], in0=b3[:, :, 63], in1=b3[:, :, 31], op=ALU.subtract), (bT_b,), (sc_b,))
                        ctx.op("act", lambda: A.activation(out=sc[:, 2, :], in_=sc[:, 3, :], func=AF.Exp), (sc_b,), (sc_b,))
                        pt, pt_b = next_pt()

                        def trv():
                            for pc in range(4):
                                i = T.transpose(out=pt[:, pc * P:(pc + 1) * P], in_=vT[:, pc * P:(pc + 1) * P], identity=ident[:])
                            return i
                        ctx.op("pe", trv, (vT_b, ident_b), (pt_b,))
                        copy_op("act", vtok[:], pt[:, 0:512].rearrange("p (c k) -> p c k", k=P), (pt_b,), (vtok_b,))
                        pt, pt_b = next_pt()

                        def trk2():
                            for pc in range(4):
                                i = T.transpose(out=pt[:, pc * P:(pc + 1) * P], in_=kt[:, pc * P:(pc + 1) * P], identity=ident[:])
                            return i
                        ctx.op("pe", trk2, (kt_b, ident_b), (pt_b,))
                        copy_op("act", ktokA[0:64, :, :], pt[0:64, 0:512].rearrange("p (c k) -> p c k", k=P), (pt_b,), (ktokA_b,))
                        copy_op("dve", ktokB[64:128, :, :], pt[64:128, 0:512].rearrange("p (c k) -> p c k", k=P), (pt_b,), (ktokB_b,))
                        dSb = [pdS, next_bank()]
                        for j2 in range(2):
                            dbk, dbk_b = dSb[j2]

                            def mmd():
                                for cq in range(4):
                                    c = 4 * j2 + cq
                                    pc = c // 2
                                    ktk = ktokA if c % 2 == 0 else ktokB
                                    i = T.matmul(dbk[:, cq * P:(cq + 1) * P], lhsT=ktk[:, pc, :], rhs=vtok[:, pc, :], start=True, stop=True)
                                return i
                            ctx.op("pe", mmd, (ktokA_b, ktokB_b, vtok_b), (dbk_b,))
                            ctx.op("dve", lambda: V.tensor_tensor(out=T3[:, :, 1 + 4 * j2:5 + 4 * j2].rearrange("p v c -> p c v"),
                                                                  in0=dbk[:].rearrange("p (c v) -> p c v", v=P),
                                                                  in1=sc[:, 2, 4 * j2:4 * j2 + 4].unsqueeze(2).to_broadcast([P, 4, P]), op=ALU.mult),
                                   (dbk_b, sc_b), (T3_b,))
                        Sh = Sst[:, l, h, :]
                        ctx.op("act", lambda: A.copy(out=T3[:, :, 0], in_=Sh), (Sst_b,), (T3_b,))
                        ctx.op("dve", lambda: V.tensor_copy(out=D3[:, :, 1:9], in_=sc[:, 1, :].unsqueeze(1).to_broadcast([P, P, 8])), (sc_b,), (D3_b,))
                        ctx.op("dve", lambda: V.tensor_tensor_scan(out=S3[:].rearrange("p v j -> p (v j)"), data0=D3[:].rearrange("p v j -> p (v j)"),
                                                                   data1=T3[:].rearrange("p v j -> p (v j)"), initial=0.0, op0=ALU.mult, op1=ALU.add),
                               (D3_b, T3_b), (S3_b,))
                        ctx.op("dve", lambda: V.tensor_tensor(out=Sra[:], in0=S3[:, :, 0:8].rearrange("p v c -> p c v"),
                                                              in1=sc[:, 0, :].unsqueeze(2).to_broadcast([P, 8, P]), op=ALU.mult), (S3_b, sc_b), (Sra_b,))
                        ctx.op("act", lambda: A.copy(out=Sh, in_=S3[:, :, 8]), (S3_b,), (Sst_b,))
                        bk, bk_b = next_bank()

                        def mmA():
                            for pc in range(4):
                                i = T.matmul(bk[:, pc * P:(pc + 1) * P], lhsT=kt[:, pc * P:(pc + 1) * P], rhs=qt[:, pc * P:(pc + 1) * P],
                                             start=True, stop=True)
                            return i
                        ctx.op("pe", mmA, (kt_b, qt_b), (bk_b,))
                        ctx.op("dve", lambda: V.tensor_tensor(out=AT[:], in0=bk[:].rearrange("p (c t) -> p c t", t=P),
                                                              in1=hmask[:].unsqueeze(1).to_broadcast([P, 4, P]), op=ALU.mult),
                               (bk_b, hmask_b), (AT_b,))
                        oT, oT_b = pbank[4]

                        def mmo():
                            for pc in range(4):
                                T.matmul(oT[:, pc * P:(pc + 1) * P], lhsT=vtok[:, pc, :], rhs=AT[:, pc, :], start=True, stop=False)
                                for cc in range(2):
                                    c = 2 * pc + cc
                                    i = T.matmul(oT[:, c * 64:(c + 1) * 64], lhsT=Sra[:, c, :], rhs=qt[:, c * 64:(c + 1) * 64],
                                                 start=False, stop=(cc == 1))
                            return i
                        ctx.op("pe", mmo, (vtok_b, AT_b, Sra_b, qt_b), (oT_b,))
                        ctx.op("act", lambda: A.activation(out=sq[:], in_=oT[:], func=AF.Square), (oT_b,), (sq_b,))
                        bk, bk_b = next_bank()
                        ctx.op("pe", lambda: T.matmul(bk[:], lhsT=ones_f[:], rhs=sq[:], start=True, stop=True), (ones_b, sq_b), (bk_b,))
                        ctx.op("act", lambda: A.activation(out=rs_[:], in_=bk[:], func=AF.Ln, scale=1.0 / P, bias=eps_ap), (bk_b, smallc_b), (rs_b,))
                        ctx.op("act", lambda: A.activation(out=rs_[:], in_=rs_[:], func=AF.Exp, scale=-0.5), (rs_b,), (rs_b,))
                        ctx.op("dve", lambda: V.tensor_tensor(out=sq[:], in0=oT[:], in1=rs_[:], op=ALU.mult), (oT_b, rs_b), (sq_b,))
                        ctx.op("dve", lambda: V.scalar_tensor_tensor(out=mixT[:, 8 + h, :], in0=sq[:], scalar=vec[:, vc + 56 + h:vc + 56 + h + 1],
                                                                     in1=sgate[:], op0=ALU.mult, op1=ALU.mult), (sq_b, vec_b, sgate_b), (hidb[8 + h],))
                    ctx.barrier()
                dump("hgrn", mixT[:, 8:16, :].rearrange("p a b -> p (a b)"), 8 * TG)
                with ExitStack() as st:
                    ybuf, _ = sb("ybuf", [P, NS, D], F32, st)
                    yb = [Buf(f"y{s}") for s in range(NS)]
                    for fb in range(4):
                        tokmajor_proj("out", mixT, hidb, 16, ybuf, yb, fb)
                    residual_norm_add(ybuf, yb, pmg_d[l])

                dump("xmix", x_res[:].rearrange("p a b -> p (a b)"), 4 * D)
                norm_to_actT(vc + 16)
                with ExitStack() as st:
                    sgt = [sb(f"f_sg{i}", [P, TG], F32, st) for i in range(8)]
                    for t in range(11):
                        wg, wg_b = w_get("gate")
                        for c in range(4):
                            bg, bg_b = next_bank()

                            def mmg():
                                for kc in range(16):
                                    i = T.matmul(bg[:], lhsT=wg[:, kc, c * P:(c + 1) * P], rhs=actT[:, kc, :], start=(kc == 0), stop=(kc == 15))
                                return i
                            ctx.op("pe", mmg, actb + [wg_b], (bg_b,))
                            s_t, s_b = sgt[(t % 2) * 4 + c]
                            ctx.op("act", lambda: A.activation(out=s_t[:], in_=bg[:], func=AF.Silu), (bg_b,), (s_b,))
                        wu, wu_b = w_get("up")
                        for c in range(4):
                            bu, bu_b = next_bank()

                            def mmu():
                                for kc in range(16):
                                    i = T.matmul(bu[:], lhsT=wu[:, kc, c * P:(c + 1) * P], rhs=actT[:, kc, :], start=(kc == 0), stop=(kc == 15))
                                return i
                            ctx.op("pe", mmu, actb + [wu_b], (bu_b,))
                            s_t, s_b = sgt[(t % 2) * 4 + c]
                            hc = t * 4 + c
                            ctx.op("dve", lambda: V.tensor_tensor(out=R_hid[:, hc, :], in0=s_t[:], in1=bu[:], op=ALU.mult), (s_b, bu_b), (hidb[hc],))
                    ctx.barrier()
                with ExitStack() as st:
                    ybuf, _ = sb("ybuf2", [P, NS, D], F32, st)
                    yb = [Buf(f"y2{s}") for s in range(NS)]
                    for fb in range(4):
                        for kp in range(4):
                            wt, wb = w_get("down")
                            for s in range(NS):
                                bk, bk_b = pbank[s]

                                def mm():
                                    for kc in range(11):
                                        hc = kp * 11 + kc
                                        i = T.matmul(bk[:], lhsT=R_hid[:, hc, s * P:(s + 1) * P], rhs=wt[:, kc, :],
                                                     start=(hc == 0), stop=(hc == 43))
                                    return i
                                ctx.op("pe", mm, hidb[kp * 11:(kp + 1) * 11] + [wb], (bk_b,))
                        for s in range(NS):
                            bk, bk_b = pbank[s]
                            copy_op(evac_engine(), ybuf[:, s, fb * 512:(fb + 1) * 512], bk[:], (bk_b,), (yb[s],))
                    residual_norm_add(ybuf, yb, pfg_d[l])

                dump("xffn", x_res[:].rearrange("p a b -> p (a b)"), 4 * D)
                norm_to_actT(vc + 32)
                with ExitStack() as st:
                    ctx.dma("pool", pTt[:], pT_d[l, :, g * TG:(g + 1) * TG].rearrange("(kc p) t -> p kc t", p=P), psem, reads=(), writes=(pT_b,))
                    sgs = [sb(f"p_sg{i}", [P, 512], F32, st) for i in range(2)]
                    tm2 = [sb(f"p_tm{i}", [P, 512], F32, st) for i in range(2)]
                    k = 0
                    for fb in range(4):
                        wg, wg_b = w_get("pg")
                        wp, wp_b = w_get("pp", held=1)
                        for s in range(NS):
                            bg, bg_b = next_bank()
                            bp, bp_b = next_bank()

                            def mmg():
                                for kc in range(16):
                                    i = T.matmul(bg[:], lhsT=actT[:, kc, s * P:(s + 1) * P], rhs=wg[:, kc, :], start=(kc == 0), stop=(kc == 15))
                                return i

                            def mmp():
                                for kc in range(2):
                                    i = T.matmul(bp[:], lhsT=pTt[:, kc, s * P:(s + 1) * P], rhs=wp[:, kc, :], start=(kc == 0), stop=(kc == 1))
                                return i
                            ctx.op("pe", mmg, actb + [wg_b], (bg_b,))
                            ctx.op("pe", mmp, (pT_b, wp_b), (bp_b,))
                            s_t, s_b = sgs[k % 2]
                            t_t, t_b = tm2[k % 2]
                            k += 1
                            ctx.op("act", lambda: A.activation(out=s_t[:], in_=bg[:], func=AF.Sigmoid), (bg_b,), (s_b,))
                            ctx.op("dve", lambda: V.tensor_tensor(out=t_t[:], in0=s_t[:], in1=bp[:], op=ALU.mult), (s_b, bp_b), (t_b,))
                            xs = x_res[:, s, fb * 512:(fb + 1) * 512]
                            ctx.op("dve", lambda: V.tensor_tensor(out=xs, in0=xs, in1=t_t[:], op=ALU.add), (xb[s], t_b), (xb[s],))
                    ctx.barrier()

            ctx.dma("sp", y_d[g * TG:(g + 1) * TG, :].rearrange("(s p) d -> p s d", p=P), x_res[:], ysem, reads=tuple(xb), writes=())


    try:
      main_body(NG)
    except _Stop:
        top.close()
        return nc
    ctx._wait("sp", (ysem.sem, ysem.cnt, ysem.key))
    assert wst["consumed"] == len(wplan), (wst, len(wplan))
    top.close()
    print(f"[build] ops={ctx.nops} waits={ctx.nwaits} weight_tiles={len(wplan)}")
    return nc


def _fm(v):
    v = np.asarray(v, np.float32)
    return np.ascontiguousarray(v.reshape(-1, P).T)


def _host_consts():
    half = 32
    invf = (10000.0 ** (-np.arange(half, dtype=np.float32) / half)).astype(np.float32)
    tq = np.arange(P)[:, None]
    tk = np.arange(P)[None, :]
    cur = np.where(tk <= tq, 0.0, MASKV).astype(np.float32)
    prev = np.where(tk > tq, 0.0, MASKV).astype(np.float32)
    dead = np.full((P, P), MASKV, np.float32)
    am = np.stack([
        np.concatenate([cur, prev], 1),
        np.concatenate([prev, cur], 1),
        np.concatenate([cur, dead], 1),
        np.concatenate([dead, cur], 1),
    ]).astype(np.float32)
    s = np.arange(P)[:, None]
    t = np.arange(P)[None, :]
    hm = ((s // 64 == t // 64) & (s <= t)).astype(np.float32)
    return invf, am, hm


def _prep_shared(inp, NL=2):
    w_in = np.asarray(inp["w_in"], np.float32)
    cols = list(range(1536))
    for h in range(NHH):
        for sec in range(4):
            base = 1536 + sec * 1024 + h * P
            cols.extend(range(base, base + P))
    w_in_p = np.ascontiguousarray(w_in[:, :, cols])
    vec = np.zeros((P, 2 * NVEC), np.float32)
    for l in range(2):
        b = l * NVEC
        vec[:, b + 0:b + 16] = _fm(inp["pre_mix_gain"][l])
        vec[:, b + 16:b + 32] = _fm(inp["pre_ffn_gain"][l])
        vec[:, b + 32:b + 48] = _fm(inp["ple_gain"][l])
        vec[:, b + 48:b + 56] = _fm(inp["attn_out_gain"][l])
        vec[:, b + 56:b + 64] = _fm(inp["hgrn_out_gain"][l])
        vec[:, b + 64:b + 72] = _fm(inp["hgrn_lb_logits"][l])
    invf, am, hm = _host_consts()
    f = lambda k: np.ascontiguousarray(np.asarray(inp[k], np.float32))
    return {
        "w_in": w_in_p, "w_out": f("w_out"), "w_gate": f("w_ffn_gate"), "w_up": f("w_ffn_up"), "w_down": f("w_ffn_down"),
        "w_pg": f("w_ple_gate"), "w_pp": f("w_ple_proj"), "vec_fm": vec,
        "post_mix_gain": f("post_mix_gain"), "post_ffn_gain": f("post_ffn_gain"),
        "sinks": np.ascontiguousarray(np.asarray(inp["attn_sinks"], np.float32).reshape(32)),
        "invf": invf, "amask": am, "hmask": hm,
    }


def _prep_core(inp, b, NTOK):
    x = np.ascontiguousarray(np.asarray(inp["x"], np.float32)[b, :NTOK])
    pT = np.ascontiguousarray(np.transpose(np.asarray(inp["p"], np.float32)[:, b, :NTOK, :], (0, 2, 1)))
    pos = np.asarray(inp["positions"])[b, :NTOK].astype(np.int32)
    posT = np.ascontiguousarray(pos.reshape(-1, P).T)
    return {"x": x, "pT": pT, "posT": posT}


def kernel(**inputs):
    B, S = 4, 4096
    nc = build(S, 2)
    shared = _prep_shared(inputs)
    in_maps = []
    for b in range(B):
        m = dict(shared)
        m.update(_prep_core(inputs, b, S))
        in_maps.append(m)
    res = run_bass_kernel_spmd(nc, in_maps, core_ids=list(range(B)))
    out = np.stack([np.asarray(res.results[b]["y"], np.float32) for b in range(B)], 0)
    return out
```

```python
import math
from contextlib import ExitStack

import numpy as np
import concourse.bass as bass
import concourse.mybir as mybir
from concourse.bass_utils import run_bass_kernel_spmd

F32 = mybir.dt.float32
BF16 = mybir.dt.bfloat16
I32 = mybir.dt.int32
AF = mybir.ActivationFunctionType
ALU = mybir.AluOpType
AX = mybir.AxisListType

P = 128
D = 2048
TG = 512
NS = 4
DFF = 5632
DIN = 5632
NQH = 16
NKV = 4
HD = 64
NHH = 8
EPS = 1e-6
MASKV = -30000.0
NVEC = 72


class Buf:
    __slots__ = ("name", "w", "r")

    def __init__(self, name):
        self.name = name
        self.w = None
        self.r = {}


class DSem:
    ALL = []

    def __init__(self, nc, name):
        self.sem = nc.alloc_semaphore(name)
        self.cnt = 0
        self.key = name
        DSem.ALL.append(self)


class Ctx:
    def __init__(self, nc):
        self.nc = nc
        self.E = {"pe": nc.tensor, "act": nc.scalar, "dve": nc.vector, "pool": nc.gpsimd, "sp": nc.sync}
        self.sem = {}
        self.cnt = {}
        for e in self.E:
            self.sem[e] = nc.alloc_semaphore("s_" + e)
            self.cnt[e] = 0
        self.waited = {}
        self.nwaits = 0
        self.nops = 0

    def _wait(self, e, sig):
        if sig is None:
            return
        sem, val, key = sig
        k = (e, key)
        if self.waited.get(k, 0) >= val:
            return
        self.E[e].wait_ge(sem, val)
        self.waited[k] = val
        self.nwaits += 1

    def _pre(self, e, reads, writes):
        for b in reads:
            if not (e == "pe" and b.w is not None and b.w[2] == "pe"):
                self._wait(e, b.w)
        for b in writes:
            if not (e == "pe" and b.w is not None and b.w[2] == "pe"):
                self._wait(e, b.w)
            for rk, rs in b.r.items():
                if not (e == "pe" and rk == "pe"):
                    self._wait(e, rs)

    def _post(self, sig, reads, writes):
        for b in reads:
            b.r[sig[2]] = sig
        for b in writes:
            b.w = sig
            b.r = {}

    def op(self, e, fn, reads=(), writes=()):
        self._pre(e, reads, writes)
        inst = fn()
        self.cnt[e] += 1
        inst.then_inc(self.sem[e], 1)
        sig = (self.sem[e], self.cnt[e], e)
        self._post(sig, reads, writes)
        self.nops += 1
        return sig

    def dma(self, q, out, in_, dsem, reads=(), writes=()):
        self._pre(q, reads, writes)
        inst = self.E[q].dma_start(out=out, in_=in_)
        dsem.cnt += 16
        inst.then_inc(dsem.sem, 16)
        sig = (dsem.sem, dsem.cnt, dsem.key)
        self._post(sig, reads, writes)
        return sig

    def barrier(self, engines=("pe", "act", "dve")):
        for e in engines:
            for e2 in engines:
                if self.cnt[e2] > 0:
                    self._wait(e, (self.sem[e2], self.cnt[e2], e2))


class _Stop(Exception):
    pass


def build(NTOK, NL, dbg=None, split=False):
    NG = NTOK // TG
    NB = NTOK // P
    NGH = NG // 2
    NTH = NTOK // 2
    nc = bass.Bass("TRN2", target_bir_lowering=False)
    DSem.ALL = []
    V, A, T, G = nc.vector, nc.scalar, nc.tensor, nc.gpsimd

    def din(name, shape, dt=F32):
        return nc.dram_tensor(name, shape, dt, kind="ExternalInput").ap()

    x_d = din("x", [NTOK, D])
    pT_d = din("pT", [2, 256, NTOK])
    pos_d = din("posT", [P, NB], I32)
    w_in_d = din("w_in", [2, D, DIN])
    w_out_d = din("w_out", [2, D, D])
    w_g_d = din("w_gate", [2, D, DFF])
    w_u_d = din("w_up", [2, D, DFF])
    w_d_d = din("w_down", [2, DFF, D])
    w_pg_d = din("w_pg", [2, D, D])
    w_pp_d = din("w_pp", [2, 256, D])
    vec_d = din("vec_fm", [P, 2 * NVEC])
    pmg_d = din("post_mix_gain", [2, D])
    pfg_d = din("post_ffn_gain", [2, D])
    sink_d = din("sinks", [32])
    invf_d = din("invf", [32])
    amask_d = din("amask", [4, P, 256])
    hmask_d = din("hmask", [P, P])
    if split:
        pT1_d = din("pT1", [256, NTH])
        pos1_d = din("posT1", [P, NB // 2], I32)
        flags_d = din("flags", [2])
        amask1_d = din("amask1", [P, 256])
        x1s_d = nc.dram_tensor("x1s", [NTOK, D], F32).ap()
        y_d = nc.dram_tensor("y", [NTH, D], F32, kind="ExternalOutput").ap()
    else:
        y_d = nc.dram_tensor("y", [NTOK, D], F32, kind="ExternalOutput").ap()
    dbg_d = None
    if dbg is not None:
        dbg_d = nc.dram_tensor("dbg", [P, 16 * TG], F32, kind="ExternalOutput").ap()

    ctx = Ctx(nc)
    top = ExitStack()
    dsem_dbg = DSem(nc, "d_dbg")

    def dump(stage, ap2d, ncols):
        if dbg is None or dbg != stage:
            return
        ctx.barrier(("pe", "act", "dve", "pool"))
        nc.gpsimd.dma_start(out=dbg_d[:, 0:ncols], in_=ap2d).then_inc(dsem_dbg.sem, 16)
        dsem_dbg.cnt = 16
        for d_ in DSem.ALL:
            if d_.cnt > 0:
                nc.gpsimd.wait_ge(d_.sem, d_.cnt)
        raise _Stop()

    uniq = {"n": 0}

    def sb(name, shape, dt, st=top):
        uniq["n"] += 1
        t = st.enter_context(nc.sbuf_tensor(f"sb_{name}_{uniq['n']}", list(shape), dt))
        return t, Buf(name)

    def ps(name, shape, dt):
        t = top.enter_context(nc.psum_tensor("ps_" + name, list(shape), dt))
        return t, Buf(name)

    x_res, _ = sb("x_res", [P, NS, D], F32)
    xb = [Buf(f"x{s}") for s in range(NS)]
    actT, _ = sb("actT", [P, 16, TG], BF16)
    actb = [Buf(f"actT{k}") for k in range(16)]
    R_hid, _ = sb("hidT", [P, 44, TG], BF16)
    hidb = [Buf(f"hid{k}") for k in range(44)]
    mixT = R_hid
    WS = 3
    wslots = [sb(f"wslot{i}", [P, 16, 512], BF16) for i in range(WS)]
    wsems = [DSem(nc, f"d_w{i}") for i in range(WS)]
    Sst, Sst_b = sb("Sst", [P, 2, NHH, P], F32)
    kTb, kTb_b = sb("kTb", [P, 2, 2, NKV, 2, P], BF16)
    Vb, Vb_b = sb("Vb", [P, 2, 2, NKV, HD], BF16)
    cs_t, cs_b = sb("cs", [P, 2, NS, 32], F32)
    amask, amask_b = sb("amask", [P, 4, 256], F32)
    amask1_b = Buf("amask1")
    if split:
        amask1, amask1_b = sb("amask1", [P, 256], F32)
        flg, flg_b = sb("flg", [P, 2], F32)
        posi1, posi1_b = sb("posi1", [P, NB // 2], I32)
        posf1, posf1_b = sb("posf1", [P, NB // 2], F32)
    hmask, hmask_b = sb("hmask", [P, P], F32)
    vec, vec_b = sb("vec", [P, 2 * NVEC], F32)
    ident, ident_b = sb("ident", [P, P], BF16)
    ones_f, ones_b = sb("ones_f", [P, P], F32)
    sinkt, sink_b = sb("sinkt", [P, 2, 32], F32)
    invf, invf_b = sb("invf", [P, 32], F32)
    posi, posi_b = sb("posi", [P, NB], I32)
    posf, posf_b = sb("posf", [P, NB], F32)
    lbt, lbt_b = sb("lbt", [P, 2, 2, NHH], F32)
    smallc, smallc_b = sb("smallc", [P, 8], F32)
    pTt, pT_b = sb("pTt", [P, 2, TG], BF16)

    pbank = [ps(f"pb{i}", [P, 512], F32) for i in range(5)]
    pdS = ps("pdS", [P, 512], F32)
    ptb = [ps(f"ptb{i}", [P, 1024], BF16) for i in range(2)]
    rot = {"pb": 0, "pt": 0}

    def next_bank():
        i = rot["pb"]
        rot["pb"] = (i + 1) % 4
        return pbank[i]

    def next_pt():
        i = rot["pt"]
        rot["pt"] = (i + 1) % 2
        return ptb[i]

    flip = {"e": 0}

    def evac_engine():
        flip["e"] ^= 1
        return "act" if flip["e"] else "dve"

    def copy_op(e, out, in_, reads, writes):
        if e == "act":
            return ctx.op("act", lambda: A.copy(out=out, in_=in_), reads, writes)
        return ctx.op("dve", lambda: V.tensor_copy(out=out, in_=in_), reads, writes)

    def scale_op(e, out, in_, sc_ap, reads, writes):
        if e == "act":
            return ctx.op("act", lambda: A.activation(out=out, in_=in_, func=AF.Identity, scale=sc_ap), reads, writes)
        return ctx.op("dve", lambda: V.tensor_scalar(out=out, in0=in_, scalar1=sc_ap, scalar2=None, op0=ALU.mult), reads, writes)

    def setup_load(name, out, in_, buf, q="sp"):
        ctx.dma(q, out, in_, DSem(nc, "d_" + name), reads=(), writes=(buf,))

    setup_load("vec", vec[:], vec_d[:, :], vec_b)
    setup_load("amask", amask[:], amask_d.rearrange("v p k -> p v k"), amask_b)
    setup_load("hmask", hmask[:], hmask_d[:, :], hmask_b)
    setup_load("sink", sinkt[:, 0, :], sink_d.partition_broadcast(P), sink_b)
    setup_load("invf", invf[:], invf_d.partition_broadcast(P), invf_b)
    setup_load("pos", posi[:], pos_d[:, :], posi_b)
    if split:
        setup_load("amask1", amask1[:], amask1_d[:, :], amask1_b)
        setup_load("flg", flg[:], flags_d.partition_broadcast(P), flg_b)
        setup_load("pos1", posi1[:], pos1_d[:, :], posi1_b)
        ctx.op("dve", lambda: V.tensor_copy(out=posf1[:], in_=posi1[:]), (posi1_b,), (posf1_b,))

    ctx.op("dve", lambda: V.memset(ones_f[:], 1.0), (), (ones_b,))
    ctx.op("dve", lambda: V.memset(ident[:], 1.0), (), (ident_b,))
    ctx.op("pool", lambda: G.affine_select(out=ident[:], in_=ident[:], pattern=[[-1, P]], compare_op=ALU.is_equal,
                                           fill=0.0, base=0, channel_multiplier=1), (ident_b,), (ident_b,))
    ctx.op("dve", lambda: V.memset(Sst[:], 0.0), (), (Sst_b,))
    ctx.op("dve", lambda: V.memset(kTb[:], 0.0), (), (kTb_b,))
    ctx.op("dve", lambda: V.memset(Vb[:], 0.0), (), (Vb_b,))
    ctx.op("dve", lambda: V.tensor_copy(out=posf[:], in_=posi[:]), (posi_b,), (posf_b,))
    ctx.op("dve", lambda: V.tensor_scalar(out=sinkt[:, 1, :], in0=sinkt[:, 0, :], scalar1=-1.0, scalar2=None, op0=ALU.mult),
           (sink_b,), (sink_b,))
    LB0 = 64
    l0 = vec[:, LB0:LB0 + 8]
    l1 = vec[:, NVEC + LB0:NVEC + LB0 + 8]
    with ExitStack() as st:
        tm, tm_b = sb("lb_m", [P, 8], F32, st)
        e0, e0_b = sb("lb_e0", [P, 8], F32, st)
        e1, e1_b = sb("lb_e1", [P, 8], F32, st)
        ctx.op("dve", lambda: V.tensor_tensor(out=tm[:], in0=l0, in1=l1, op=ALU.max), (vec_b,), (tm_b,))
        ctx.op("dve", lambda: V.tensor_tensor(out=e0[:], in0=l0, in1=tm[:], op=ALU.subtract), (vec_b, tm_b), (e0_b,))
        ctx.op("dve", lambda: V.tensor_tensor(out=e1[:], in0=l1, in1=tm[:], op=ALU.subtract), (vec_b, tm_b), (e1_b,))
        ctx.op("act", lambda: A.activation(out=e0[:], in_=e0[:], func=AF.Exp), (e0_b,), (e0_b,))
        ctx.op("act", lambda: A.activation(out=e1[:], in_=e1[:], func=AF.Exp), (e1_b,), (e1_b,))
        ctx.op("dve", lambda: V.tensor_tensor(out=tm[:], in0=e0[:], in1=e1[:], op=ALU.add), (e0_b, e1_b), (tm_b,))
        ctx.op("dve", lambda: V.reciprocal(out=tm[:], in_=tm[:]), (tm_b,), (tm_b,))
        ctx.op("dve", lambda: V.memset(lbt[:, 0, 0, :], 0.0), (), (lbt_b,))
        ctx.op("dve", lambda: V.tensor_tensor(out=lbt[:, 1, 0, :], in0=e1[:], in1=tm[:], op=ALU.mult), (e1_b, tm_b), (lbt_b,))
        ctx.op("dve", lambda: V.tensor_scalar(out=lbt[:, :, 1, :], in0=lbt[:, :, 0, :], scalar1=-1.0, scalar2=1.0,
                                              op0=ALU.mult, op1=ALU.add), (lbt_b,), (lbt_b,))
        ctx.barrier()

    def layer_tiles(l):
        for t in range(11):
            yield ("in", w_in_d[l, :, t * 512:(t + 1) * 512].rearrange("(kc p) n -> p kc n", p=P), 16, 512)
        for fb in range(4):
            yield ("out", w_out_d[l, :, fb * 512:(fb + 1) * 512].rearrange("(kc p) n -> p kc n", p=P), 16, 512)
        for t in range(11):
            yield ("gate", w_g_d[l, :, t * 512:(t + 1) * 512].rearrange("(kc p) n -> p kc n", p=P), 16, 512)
            yield ("up", w_u_d[l, :, t * 512:(t + 1) * 512].rearrange("(kc p) n -> p kc n", p=P), 16, 512)
        for fb in range(4):
            for kp in range(4):
                yield ("down", w_d_d[l, kp * 1408:(kp + 1) * 1408, fb * 512:(fb + 1) * 512]
                       .rearrange("(kc p) n -> p kc n", p=P), 11, 512)
        for fb in range(4):
            yield ("pg", w_pg_d[l, :, fb * 512:(fb + 1) * 512].rearrange("(kc p) n -> p kc n", p=P), 16, 512)
            yield ("pp", w_pp_d[l, :, fb * 512:(fb + 1) * 512].rearrange("(kc p) n -> p kc n", p=P), 2, 512)

    def weight_plan():
        if not split:
            for g in range(NG):
                for l in range(NL):
                    yield from layer_tiles(l)
            return
        for g in range(NG):
            yield from layer_tiles(0)
        for g in range(NGH):
            if g == NGH - 1:
                yield ("in", w_in_d[1, :, 1024:1536].rearrange("(kc p) n -> p kc n", p=P), 16, 512)
            for h in range(NHH):
                c0 = 1536 + h * 512 + 128
                yield ("inp", w_in_d[1, :, c0:c0 + 256].rearrange("(kc p) n -> p kc n", p=P), 16, 256)
        for j in range(NGH):
            yield from layer_tiles(1)

    wplan = list(weight_plan())
    wst = {"issued": 0, "consumed": 0}

    def w_issue(upto):
        while wst["issued"] < min(upto, len(wplan)):
            i = wst["issued"]
            tag, src, kc, ncol = wplan[i]
            t, b = wslots[i % WS]
            ctx.dma("pool", t[:, 0:kc, 0:ncol], src, wsems[i % WS], reads=(), writes=(b,))
            wst["issued"] += 1

    def w_get(tag, held=0):
        i = wst["consumed"]
        assert wplan[i][0] == tag, (wplan[i][0], tag)
        w_issue(i + WS - held)
        wst["consumed"] += 1
        return wslots[i % WS]

    xsem = DSem(nc, "d_x")
    ysem = DSem(nc, "d_y")
    psem = DSem(nc, "d_p")
    gsem = DSem(nc, "d_g")

    def rstd_from_ss(ss, ss_b, n, width):
        ctx.op("act", lambda: A.activation(out=ss[:, 0:n], in_=ss[:, 0:n], func=AF.Ln, scale=1.0 / width, bias=eps_ap),
               (ss_b, smallc_b), (ss_b,))
        ctx.op("act", lambda: A.activation(out=ss[:, 0:n], in_=ss[:, 0:n], func=AF.Exp, scale=-0.5), (ss_b,), (ss_b,))

    ctx.op("dve", lambda: V.memset(smallc[:, 0:1], EPS), (), (smallc_b,))
    eps_ap = smallc[:, 0:1]

    def norm_to_actT(gcol):
        with ExitStack() as st:
            hb4, hb4_b = sb("hb4", [P, NS, D], BF16, st)
            ss, ss_b = sb("nss", [P, NS], F32, st)
            hbs = [Buf(f"hb{s}") for s in range(NS)]
            for s in range(NS):
                ctx.op("act", lambda: A.activation(out=hb4[:, s, :], in_=x_res[:, s, :], func=AF.Square,
                                                   accum_out=ss[:, s:s + 1]), (xb[s],), (hbs[s], ss_b))
            rstd_from_ss(ss, ss_b, NS, D)
            for s in range(NS):
                scale_op("dve" if s % 2 else "act", hb4[:, s, :], x_res[:, s, :], ss[:, s:s + 1], (xb[s], ss_b), (hbs[s],))
            for kc in range(16):
                pt, pt_b = next_pt()

                def tr():
                    for s in range(NS):
                        i = T.transpose(out=pt[:, s * P:(s + 1) * P], in_=hb4[:, s, kc * P:(kc + 1) * P], identity=ident[:])
                    return i
                ctx.op("pe", tr, hbs + [ident_b], (pt_b,))
                scale_op(evac_engine(), actT[:, kc, :], pt[:, 0:TG], vec[:, gcol + kc:gcol + kc + 1], (pt_b, vec_b), (actb[kc],))
            ctx.barrier()

    def residual_norm_add(ybuf, yb, gain_d_row):
        with ExitStack() as st:
            junk = actT[:, 0:4, :].rearrange("p a b -> p (a b)")
            junk_bs = actb[0:4]
            gbc = actT[:, 4:12, :].rearrange("p a b -> p (a b)").bitcast(F32)
            gbc_bs = actb[4:12]
            ss, ss_b = sb("rss", [P, NS], F32, st)
            ctx.dma("sp", gbc, gain_d_row.partition_broadcast(P), gsem, reads=(), writes=tuple(gbc_bs))
            for s in range(NS):
                ctx.op("act", lambda: A.activation(out=junk, in_=ybuf[:, s, :], func=AF.Square, accum_out=ss[:, s:s + 1]),
                       (yb[s],), tuple(junk_bs) + (ss_b,))
            rstd_from_ss(ss, ss_b, NS, D)
            for s in range(NS):
                ctx.op("dve", lambda: V.scalar_tensor_tensor(out=ybuf[:, s, :], in0=ybuf[:, s, :], scalar=ss[:, s:s + 1], in1=gbc,
                                                             op0=ALU.mult, op1=ALU.mult), (yb[s], ss_b) + tuple(gbc_bs), (yb[s],))
                ctx.op("dve", lambda: V.tensor_tensor(out=x_res[:, s, :], in0=x_res[:, s, :], in1=ybuf[:, s, :], op=ALU.add),
                       (xb[s], yb[s]), (xb[s],))
            ctx.barrier()

    def tokmajor_proj(tag, srcT, srcb, nk, ybuf, yb, fb):
        wt, wb = w_get(tag)
        for s in range(NS):
            bk, bk_b = next_bank()

            def mm():
                for kc in range(nk):
                    i = T.matmul(bk[:], lhsT=srcT[:, kc, s * P:(s + 1) * P], rhs=wt[:, kc, :], start=(kc == 0), stop=(kc == nk - 1))
                return i
            ctx.op("pe", mm, list(srcb[0:nk]) + [wb], (bk_b,))
            copy_op(evac_engine(), ybuf[:, s, fb * 512:(fb + 1) * 512], bk[:], (bk_b,), (yb[s],))

    def rope_tables(pos_ap):
        with ExitStack() as st:
            ang, ang_b = sb("ang", [P, NS, 32], F32, st)
            kk, kk_b = sb("kk", [P, NS, 32], F32, st)
            ki, ki_b = sb("ki", [P, NS, 32], I32, st)
            yy, yy_b = sb("yy", [P, NS, 32], F32, st)
            mk, mk_b = sb("mk", [P, NS, 32], F32, st)
            C1 = 6.28125
            C2 = 2.0 * math.pi - C1
            ctx.op("dve", lambda: V.tensor_tensor(out=ang[:], in0=invf[:].unsqueeze(1).to_broadcast([P, NS, 32]),
                                                  in1=pos_ap.unsqueeze(2).to_broadcast([P, NS, 32]),
                                                  op=ALU.mult), (invf_b, posf_b) + ((posf1_b,) if split else ()), (ang_b,))
            for which, shift in ((1, 0.0), (0, math.pi / 2)):
                ctx.op("dve", lambda: V.tensor_scalar(out=kk[:], in0=ang[:], scalar1=shift, scalar2=1.0 / (2 * math.pi),
                                                      op0=ALU.add, op1=ALU.mult), (ang_b,), (kk_b,))
                ctx.op("dve", lambda: V.tensor_copy(out=ki[:], in_=kk[:]), (kk_b,), (ki_b,))
                ctx.op("dve", lambda: V.tensor_copy(out=kk[:], in_=ki[:]), (ki_b,), (kk_b,))
                ctx.op("dve", lambda: V.scalar_tensor_tensor(out=yy[:], in0=kk[:], scalar=-C1, in1=ang[:], op0=ALU.mult, op1=ALU.add),
                       (kk_b, ang_b), (yy_b,))
                ctx.op("dve", lambda: V.scalar_tensor_tensor(out=yy[:], in0=kk[:], scalar=-C2, in1=yy[:], op0=ALU.mult, op1=ALU.add),
                       (kk_b, yy_b), (yy_b,))
                if shift != 0.0:
                    ctx.op("dve", lambda: V.tensor_scalar(out=yy[:], in0=yy[:], scalar1=shift, scalar2=None, op0=ALU.add), (yy_b,), (yy_b,))
                ctx.op("dve", lambda: V.tensor_scalar(out=mk[:], in0=yy[:], scalar1=math.pi, scalar2=-2 * math.pi, op0=ALU.is_gt, op1=ALU.mult),
                       (yy_b,), (mk_b,))
                ctx.op("dve", lambda: V.tensor_tensor(out=yy[:], in0=yy[:], in1=mk[:], op=ALU.add), (yy_b, mk_b), (yy_b,))
                ctx.op("dve", lambda: V.tensor_scalar(out=mk[:], in0=yy[:], scalar1=-math.pi, scalar2=2 * math.pi, op0=ALU.is_lt, op1=ALU.mult),
                       (yy_b,), (mk_b,))
                ctx.op("dve", lambda: V.tensor_tensor(out=yy[:], in0=yy[:], in1=mk[:], op=ALU.add), (yy_b, mk_b), (yy_b,))
                ctx.op("dve", lambda: V.tensor_scalar(out=yy[:], in0=yy[:], scalar1=3.1415925, scalar2=-3.1415925, op0=ALU.min, op1=ALU.max),
                       (yy_b,), (yy_b,))
                ctx.op("act", lambda: A.activation(out=cs_t[:, which, :, :], in_=yy[:], func=AF.Sin), (yy_b,), (cs_b,))
            ctx.barrier()


    def layer_body(l, first_mask, pT_src, state_only=False, kv_tail=False):
        vc = l * NVEC

        norm_to_actT(vc + 0)
        dump("actT", actT[:].rearrange("p a b -> p (a b)"), 16 * TG)
        if state_only and kv_tail:
            with ExitStack() as st:
                kvt, kvt_b = sb("kvt", [P, 512], F32, st)
                t1, t1_b = sb("kt1", [P, NKV, 32], F32, st)
                t2, t2_b = sb("kt2", [P, NKV, 32], F32, st)
                kdup, kdup_b = sb("kdup2", [P, NKV, 2, HD], BF16, st)
                wt, wb = w_get("in")
                bk, bk_b = next_bank()

                def mmkv():
                    for kc in range(16):
                        i = T.matmul(bk[:], lhsT=actT[:, kc, 3 * P:4 * P], rhs=wt[:, kc, :], start=(kc == 0), stop=(kc == 15))
                    return i
                ctx.op("pe", mmkv, actb + [wb], (bk_b,))
                copy_op("act", kvt[:], bk[:], (bk_b,), (kvt_b,))
                cosb = cs_t[:, 0, 3, :].unsqueeze(1).to_broadcast([P, NKV, 32])
                sinb = cs_t[:, 1, 3, :].unsqueeze(1).to_broadcast([P, NKV, 32])
                kk4 = kvt[:, 0:256].rearrange("p (h d) -> p h d", d=HD)
                x1 = kk4[:, :, 0:32]
                x2 = kk4[:, :, 32:64]
                ctx.op("dve", lambda: V.tensor_tensor(out=t1[:], in0=x1, in1=cosb, op=ALU.mult), (kvt_b, cs_b), (t1_b,))
                ctx.op("dve", lambda: V.tensor_tensor(out=t2[:], in0=x2, in1=sinb, op=ALU.mult), (kvt_b, cs_b), (t2_b,))
                for dd in range(2):
                    ctx.op("dve", lambda: V.tensor_tensor(out=kdup[:, :, dd, 0:32], in0=t1[:], in1=t2[:], op=ALU.subtract), (t1_b, t2_b), (kdup_b,))
                ctx.op("dve", lambda: V.tensor_tensor(out=t1[:], in0=x2, in1=cosb, op=ALU.mult), (kvt_b, cs_b), (t1_b,))
                ctx.op("dve", lambda: V.tensor_tensor(out=t2[:], in0=x1, in1=sinb, op=ALU.mult), (kvt_b, cs_b), (t2_b,))
                for dd in range(2):
                    ctx.op("dve", lambda: V.tensor_tensor(out=kdup[:, :, dd, 32:64], in0=t1[:], in1=t2[:], op=ALU.add), (t1_b, t2_b), (kdup_b,))
                ctx.op("act", lambda: A.copy(out=Vb[:, l, 1, :, :], in_=kvt[:, 256:512].rearrange("p (g d) -> p g d", d=HD)), (kvt_b,), (Vb_b,))
                pt, pt_b = next_pt()
                kdf = kdup[:].rearrange("p g t d -> p g (t d)")

                def trk():
                    for gg in range(NKV):
                        i = T.transpose(out=pt[:, gg * P:(gg + 1) * P], in_=kdf[:, gg, :], identity=ident[:])
                    return i
                ctx.op("pe", trk, (kdup_b, ident_b), (pt_b,))
                copy_op("act", kTb[0:64, l, 0, :, 1, :], pt[0:64, 0:512].rearrange("p (j t) -> p j t", t=P), (pt_b,), (kTb_b,))
                copy_op("dve", kTb[64:128, l, 1, :, 1, :], pt[64:128, 0:512].rearrange("p (j t) -> p j t", t=P), (pt_b,), (kTb_b,))
                ctx.barrier()
        if not state_only:
            with ExitStack() as st:
                qkv = R_hid[:, 16:40, :].rearrange("p a b -> p (a b)").bitcast(F32).rearrange("p (s c) -> p s c", c=1536)
                qkvb = [Buf(f"qkv{s}") for s in range(NS)]
                an4, _ = sb("an4", [P, NS, 1024], BF16, st)
                an4b = [Buf(f"an4_{s}") for s in range(NS)]
                for t in range(3):
                    wt, wb = w_get("in")
                    for s in range(NS):
                        bk, bk_b = next_bank()

                        def mm():
                            for kc in range(16):
                                i = T.matmul(bk[:], lhsT=actT[:, kc, s * P:(s + 1) * P], rhs=wt[:, kc, :], start=(kc == 0), stop=(kc == 15))
                            return i
                        ctx.op("pe", mm, actb + [wb], (bk_b,))
                        copy_op(evac_engine(), qkv[:, s, t * 512:(t + 1) * 512], bk[:], (bk_b,), (qkvb[s],))

                with ExitStack() as st2:
                    t1, t1_b = sb("rt1", [P, 20, 32], F32, st2)
                    t2, t2_b = sb("rt2", [P, 20, 32], F32, st2)
                    qr, qr_b = sb("qr", [P, NQH, HD], BF16, st2)
                    kdup, kdup_b = sb("kdup", [P, NKV, 2, HD], BF16, st2)
                    qT, qT_b = sb("qT", [P, 8, P], BF16, st2)
                    NR = 3
                    Ssb = [sb(f"Ssb{i}", [P, 2, 256], F32, st2) for i in range(NR)]
                    eb = [sb(f"eb{i}", [P, 2, 256], BF16, st2) for i in range(NR)]
                    eT = [sb(f"eT{i}", [P, 4, P], BF16, st2) for i in range(NR)]
                    stat, stat_b = sb("stat", [P, 8, NQH], F32, st2)
                    atok, atok_b = sb("atok", [P, 1024], F32, st2)
                    ajunk, ajunk_b = sb("ajunk", [P, 1024], BF16, st2)
                    ass, ass_b = sb("ass", [P, NS], F32, st2)

                    for s in range(NS):
                        slot = s % 2
                        mask_ap = first_mask if (s == 0 and first_mask is not None) else amask[:, slot, :]
                        cosb = cs_t[:, 0, s, :].unsqueeze(1).to_broadcast([P, 20, 32])
                        sinb = cs_t[:, 1, s, :].unsqueeze(1).to_broadcast([P, 20, 32])
                        qk = qkv[:, s, 0:1280].rearrange("p (h d) -> p h d", d=HD)
                        x1 = qk[:, :, 0:32]
                        x2 = qk[:, :, 32:64]
                        ctx.op("dve", lambda: V.tensor_tensor(out=t1[:], in0=x1, in1=cosb, op=ALU.mult), (qkvb[s], cs_b), (t1_b,))
                        ctx.op("dve", lambda: V.tensor_tensor(out=t2[:], in0=x2, in1=sinb, op=ALU.mult), (qkvb[s], cs_b), (t2_b,))
                        ctx.op("dve", lambda: V.tensor_tensor(out=qr[:, :, 0:32], in0=t1[:, 0:16, :], in1=t2[:, 0:16, :], op=ALU.subtract),
                               (t1_b, t2_b), (qr_b,))
                        for dd in range(2):
                            ctx.op("dve", lambda: V.tensor_tensor(out=kdup[:, :, dd, 0:32], in0=t1[:, 16:20, :], in1=t2[:, 16:20, :], op=ALU.subtract),
                                   (t1_b, t2_b), (kdup_b,))
                        ctx.op("dve", lambda: V.tensor_tensor(out=t1[:], in0=x2, in1=cosb, op=ALU.mult), (qkvb[s], cs_b), (t1_b,))
                        ctx.op("dve", lambda: V.tensor_tensor(out=t2[:], in0=x1, in1=sinb, op=ALU.mult), (qkvb[s], cs_b), (t2_b,))
                        ctx.op("dve", lambda: V.tensor_tensor(out=qr[:, :, 32:64], in0=t1[:, 0:16, :], in1=t2[:, 0:16, :], op=ALU.add),
                               (t1_b, t2_b), (qr_b,))
                        for dd in range(2):
                            ctx.op("dve", lambda: V.tensor_tensor(out=kdup[:, :, dd, 32:64], in0=t1[:, 16:20, :], in1=t2[:, 16:20, :], op=ALU.add),
                                   (t1_b, t2_b), (kdup_b,))
                        ctx.op("act", lambda: A.copy(out=Vb[:, l, slot, :, :], in_=qkv[:, s, 1280:1536].rearrange("p (g d) -> p g d", d=HD)),
                               (qkvb[s],), (Vb_b,))
                        qrf = qr[:].rearrange("p h d -> p (h d)")
                        for half in range(2):
                            pt, pt_b = next_pt()

                            def trq():
                                for j in range(4):
                                    jj = half * 4 + j
                                    i = T.transpose(out=pt[:, j * P:(j + 1) * P], in_=qrf[:, jj * P:(jj + 1) * P], identity=ident[:])
                                return i
                            ctx.op("pe", trq, (qr_b, ident_b), (pt_b,))
                            copy_op(evac_engine(), qT[:, half * 4:(half + 1) * 4, :], pt[:, 0:512].rearrange("p (j t) -> p j t", t=P),
                                    (pt_b,), (qT_b,))
                        pt, pt_b = next_pt()
                        kdf = kdup[:].rearrange("p g t d -> p g (t d)")

                        def trk():
                            for gg in range(NKV):
                                i = T.transpose(out=pt[:, gg * P:(gg + 1) * P], in_=kdf[:, gg, :], identity=ident[:])
                            return i
                        ctx.op("pe", trk, (kdup_b, ident_b), (pt_b,))
                        copy_op("act", kTb[0:64, l, 0, :, slot, :], pt[0:64, 0:512].rearrange("p (j t) -> p j t", t=P), (pt_b,), (kTb_b,))
                        copy_op("dve", kTb[64:128, l, 1, :, slot, :], pt[64:128, 0:512].rearrange("p (j t) -> p j t", t=P), (pt_b,), (kTb_b,))

                        obk = [pbank[4], pdS]

                        def stage1(j):
                            gq = j // 2
                            bk, bk_b = next_bank()
                            S_t, S_b = Ssb[j % NR]
                            e_t, e_b = eb[j % NR]

                            def mm():
                                for i2 in range(2):
                                    i = T.matmul(bk[:, i2 * 256:(i2 + 1) * 256], lhsT=qT[:, j, :],
                                                 rhs=kTb[:, l, i2, gq, :, :].rearrange("p s t -> p (s t)"), start=True, stop=True)
                                return i
                            ctx.op("pe", mm, (qT_b, kTb_b), (bk_b,))
                            for i2 in range(2):
                                ctx.op("dve", lambda: V.scalar_tensor_tensor(out=S_t[:, i2, :], in0=bk[:, i2 * 256:(i2 + 1) * 256], scalar=0.125,
                                                                             in1=mask_ap, op0=ALU.mult, op1=ALU.add),
                                       (bk_b, amask_b, amask1_b), (S_b,))
                            h0 = 2 * j
                            ctx.op("dve", lambda: V.tensor_reduce(out=stat[:, 0, h0:h0 + 2], in_=S_t[:], axis=AX.X, op=ALU.max, negate=True),
                                   (S_b,), (stat_b,))
                            ctx.op("dve", lambda: V.tensor_tensor(out=stat[:, 0, h0:h0 + 2], in0=stat[:, 0, h0:h0 + 2],
                                                                  in1=sinkt[:, 1, l * 16 + h0:l * 16 + h0 + 2], op=ALU.min), (stat_b, sink_b), (stat_b,))
                            for i2 in range(2):
                                ctx.op("act", lambda: A.activation(out=e_t[:, i2, :], in_=S_t[:, i2, :], func=AF.Exp,
                                                                   bias=stat[:, 0, h0 + i2:h0 + i2 + 1], scale=1.0,
                                                                   accum_out=stat[:, 1, h0 + i2:h0 + i2 + 1]), (S_b, stat_b), (e_b, stat_b))

                        def stage2(j):
                            e_t, e_b = e_bufs = eb[j % NR]
                            eT_t, eT_b = eT[j % NR]
                            pt, pt_b = next_pt()

                            def tr():
                                for i2 in range(2):
                                    for sl in range(2):
                                        i = T.transpose(out=pt[:, (i2 * 2 + sl) * P:(i2 * 2 + sl + 1) * P],
                                                        in_=e_t[:, i2, sl * P:(sl + 1) * P], identity=ident[:])
                                return i
                            ctx.op("pe", tr, (e_b, ident_b), (pt_b,))
                            copy_op(evac_engine(), eT_t[:], pt[:, 0:512].rearrange("p (j t) -> p j t", t=P), (pt_b,), (eT_b,))

                        def stage3(j):
                            gq = j // 2
                            eT_t, eT_b = eT[j % NR]
                            for i2 in range(2):
                                h = 2 * j + i2
                                ob, ob_b = obk[h // 8]
                                col = (h % 8) * HD

                                def mm():
                                    T.matmul(ob[:, col:col + HD], lhsT=eT_t[:, i2 * 2 + 0, :], rhs=Vb[:, l, 0, gq, :], start=True, stop=False)
                                    return T.matmul(ob[:, col:col + HD], lhsT=eT_t[:, i2 * 2 + 1, :], rhs=Vb[:, l, 1, gq, :], start=False, stop=True)
                                ctx.op("pe", mm, (eT_b, Vb_b), (ob_b,))

                        for step in range(8 + 2):
                            if step < 8:
                                stage1(step)
                            if 0 <= step - 1 < 8:
                                stage2(step - 1)
                            if 0 <= step - 2 < 8:
                                stage3(step - 2)
                        ctx.op("dve", lambda: V.tensor_tensor(out=stat[:, 2, :], in0=stat[:, 0, :], in1=sinkt[:, 0, l * 16:(l + 1) * 16], op=ALU.add),
                               (stat_b, sink_b), (stat_b,))
                        ctx.op("act", lambda: A.activation(out=stat[:, 3, :], in_=stat[:, 2, :], func=AF.Exp), (stat_b,), (stat_b,))
                        ctx.op("dve", lambda: V.tensor_tensor(out=stat[:, 3, :], in0=stat[:, 3, :], in1=stat[:, 1, :], op=ALU.add), (stat_b,), (stat_b,))
                        ctx.op("dve", lambda: V.reciprocal(out=stat[:, 4, :], in_=stat[:, 3, :]), (stat_b,), (stat_b,))
                        for hb_ in range(2):
                            ob, ob_b = obk[hb_]
                            ctx.op("dve", lambda: V.tensor_tensor(out=atok[:, hb_ * 512:(hb_ + 1) * 512].rearrange("p (h d) -> p h d", d=HD),
                                                                  in0=ob[:].rearrange("p (h d) -> p h d", d=HD),
                                                                  in1=stat[:, 4, hb_ * 8:(hb_ + 1) * 8].unsqueeze(2).to_broadcast([P, 8, HD]),
                                                                  op=ALU.mult), (ob_b, stat_b), (atok_b,))
                        ctx.op("act", lambda: A.activation(out=ajunk[:], in_=atok[:], func=AF.Square, accum_out=ass[:, s:s + 1]),
                               (atok_b,), (ajunk_b, ass_b))
                        rstd_from_ss(ass[:, s:s + 1], ass_b, 1, 1024)
                        scale_op("act", an4[:, s, :], atok[:], ass[:, s:s + 1], (atok_b, ass_b), (an4b[s],))
                    ctx.barrier()
                for j in range(8):
                    pt, pt_b = next_pt()

                    def tr():
                        for s in range(NS):
                            i = T.transpose(out=pt[:, s * P:(s + 1) * P], in_=an4[:, s, j * P:(j + 1) * P], identity=ident[:])
                        return i
                    ctx.op("pe", tr, an4b + [ident_b], (pt_b,))
                    scale_op(evac_engine(), mixT[:, j, :], pt[:, 0:TG], vec[:, vc + 48 + j:vc + 48 + j + 1], (pt_b, vec_b), (hidb[j],))
                ctx.barrier()

        dump("attn", mixT[:, 0:8, :].rearrange("p a b -> p (a b)"), 8 * TG)
        with ExitStack() as st:
            def f32t(name, shape=(P, TG)):
                return sb(name, list(shape), F32, st)
            ff, ff_b = f32t("h_f")
            qs, qs_b = f32t("h_qs")
            sgate, sgate_b = f32t("h_sgate")
            vT, vT_b = sb("h_vT", [P, TG], BF16, st)
            bT, bT_b = f32t("h_b")
            dd_, dd_b = f32t("h_d")
            E1, E1_b = f32t("h_E1")
            kin, kin_b = f32t("h_kin")
            qt, qt_b = sb("h_qt", [P, TG], BF16, st)
            kt, kt_b = sb("h_kt", [P, TG], BF16, st)
            vtok, vtok_b = sb("h_vtok", [P, 4, P], BF16, st)
            ktokA, ktokA_b = sb("h_ktokA", [P, 4, P], BF16, st)
            ktokB, ktokB_b = sb("h_ktokB", [P, 4, P], BF16, st)
            AT, AT_b = sb("h_AT", [P, 4, P], BF16, st)
            T3, T3_b = f32t("h_T3", (P, P, 9))
            D3, D3_b = f32t("h_D3", (P, P, 9))
            S3, S3_b = f32t("h_S3", (P, P, 9))
            Sra, Sra_b = sb("h_Sra", [P, 8, P], BF16, st)
            sq, sq_b = dd_, dd_b
            rs_, rs_b = bT, bT_b
            sc, sc_b = f32t("h_sc", (P, 4, 8))
            rmask, rmask_b = f32t("h_rmask")
            ctx.op("dve", lambda: V.memset(ktokA[:], 0.0), (), (ktokA_b,))
            ctx.op("dve", lambda: V.memset(ktokB[:], 0.0), (), (ktokB_b,))
            ctx.op("dve", lambda: V.memset(D3[:], 0.0), (), (D3_b,))
            ctx.op("dve", lambda: V.memset(rmask[:], 1.0), (), (rmask_b,))
            ctx.op("dve", lambda: V.memset(rmask[:].rearrange("p (c t) -> p c t", t=64)[:, :, 0:1], 0.0), (), (rmask_b,))

            for h in range(NHH):
                wt, wb = w_get("inp" if state_only else "in")
                banks = []
                for c in ((0, 1) if state_only else range(4)):
                    bk, bk_b = next_bank()

                    def mm():
                        for kc in range(16):
                            i = T.matmul(bk[:], lhsT=wt[:, kc, c * P:(c + 1) * P], rhs=actT[:, kc, :], start=(kc == 0), stop=(kc == 15))
                        return i
                    ctx.op("pe", mm, actb + [wb], (bk_b,))
                    banks.append((bk, bk_b))
                if state_only:
                    (bf_, bf_b), (bi_, bi_b) = banks
                else:
                    (bq, bq_b), (bf_, bf_b), (bi_, bi_b), (bg, bg_b) = banks
                ctx.op("act", lambda: A.activation(out=ff[:], in_=bf_[:], func=AF.Sigmoid), (bf_b,), (ff_b,))
                if not state_only:
                    ctx.op("act", lambda: A.activation(out=qs[:], in_=bq[:], func=AF.Silu), (bq_b,), (qs_b,))
                ctx.op("act", lambda: A.copy(out=vT[:], in_=bi_[:]), (bi_b,), (vT_b,))
                if not state_only:
                    ctx.op("act", lambda: A.activation(out=sgate[:], in_=bg[:], func=AF.Silu), (bg_b,), (sgate_b,))
                ctx.op("dve", lambda: V.tensor_scalar(out=ff[:], in0=ff[:], scalar1=lbt[:, l, 1, h:h + 1], scalar2=lbt[:, l, 0, h:h + 1],
                                                      op0=ALU.mult, op1=ALU.add), (ff_b, lbt_b), (ff_b,))
                ctx.op("dve", lambda: V.tensor_scalar(out=kin[:], in0=ff[:], scalar1=-1.0, scalar2=1.0, op0=ALU.mult, op1=ALU.add),
                       (ff_b,), (kin_b,))
                ctx.op("act", lambda: A.activation(out=ff[:], in_=ff[:], func=AF.Ln), (ff_b,), (ff_b,))
                ctx.op("dve", lambda: V.tensor_tensor_scan(out=bT[:], data0=rmask[:], data1=ff[:], initial=0.0, op0=ALU.mult, op1=ALU.add),
                       (rmask_b, ff_b), (bT_b,))
                b3 = bT[:].rearrange("p (c t) -> p c t", t=64)
                ctx.op("dve", lambda: V.tensor_tensor(out=dd_[:].rearrange("p (c t) -> p c t", t=64), in0=b3,
                                                      in1=b3[:, :, 31:32].to_broadcast([P, 8, 64]), op=ALU.subtract), (bT_b,), (dd_b,))
                if not state_only:
                    ctx.op("act", lambda: A.activation(out=E1[:], in_=dd_[:], func=AF.Exp), (dd_b,), (E1_b,))
                    ctx.op("dve", lambda: V.tensor_tensor(out=qt[:], in0=qs[:], in1=E1[:], op=ALU.mult), (qs_b, E1_b), (qt_b,))
                ctx.op("act", lambda: A.activation(out=E1[:], in_=dd_[:], func=AF.Exp, scale=-1.0), (dd_b,), (E1_b,))
                ctx.op("dve", lambda: V.tensor_tensor(out=kt[:], in0=kin[:], in1=E1[:], op=ALU.mult), (kin_b, E1_b), (kt_b,))
                ctx.op("act", lambda: A.activation(out=sc[:, 0, :], in_=b3[:, :, 31], func=AF.Exp), (bT_b,), (sc_b,))
                ctx.op("act", lambda: A.activation(out=sc[:, 1, :], in_=b3[:, :, 63], func=AF.Exp), (bT_b,), (sc_b,))
                ctx.op("dve", lambda: V.tensor_tensor(out=sc[:, 3, :], in0=b3[:, :, 63], in1=b3[:, :, 31], op=ALU.subtract), (bT_b,), (sc_b,))
                ctx.op("act", lambda: A.activation(out=sc[:, 2, :], in_=sc[:, 3, :], func=AF.Exp), (sc_b,), (sc_b,))
                pt, pt_b = next_pt()

                def trv():
                    for pc in range(4):
                        i = T.transpose(out=pt[:, pc * P:(pc + 1) * P], in_=vT[:, pc * P:(pc + 1) * P], identity=ident[:])
                    return i
                ctx.op("pe", trv, (vT_b, ident_b), (pt_b,))
                copy_op("act", vtok[:], pt[:, 0:512].rearrange("p (c k) -> p c k", k=P), (pt_b,), (vtok_b,))
                pt, pt_b = next_pt()

                def trk2():
                    for pc in range(4):
                        i = T.transpose(out=pt[:, pc * P:(pc + 1) * P], in_=kt[:, pc * P:(pc + 1) * P], identity=ident[:])
                    return i
                ctx.op("pe", trk2, (kt_b, ident_b), (pt_b,))
                copy_op("act", ktokA[0:64, :, :], pt[0:64, 0:512].rearrange("p (c k) -> p c k", k=P), (pt_b,), (ktokA_b,))
                copy_op("dve", ktokB[64:128, :, :], pt[64:128, 0:512].rearrange("p (c k) -> p c k", k=P), (pt_b,), (ktokB_b,))
                dSb = [pdS, next_bank()]
                for j2 in range(2):
                    dbk, dbk_b = dSb[j2]

                    def mmd():
                        for cq in range(4):
                            c = 4 * j2 + cq
                            pc = c // 2
                            ktk = ktokA if c % 2 == 0 else ktokB
                            i = T.matmul(dbk[:, cq * P:(cq + 1) * P], lhsT=ktk[:, pc, :], rhs=vtok[:, pc, :], start=True, stop=True)
                        return i
                    ctx.op("pe", mmd, (ktokA_b, ktokB_b, vtok_b), (dbk_b,))
                    ctx.op("dve", lambda: V.tensor_tensor(out=T3[:, :, 1 + 4 * j2:5 + 4 * j2].rearrange("p v c -> p c v"),
                                                          in0=dbk[:].rearrange("p (c v) -> p c v", v=P),
                                                          in1=sc[:, 2, 4 * j2:4 * j2 + 4].unsqueeze(2).to_broadcast([P, 4, P]), op=ALU.mult),
                           (dbk_b, sc_b), (T3_b,))
                Sh = Sst[:, l, h, :]
                ctx.op("act", lambda: A.copy(out=T3[:, :, 0], in_=Sh), (Sst_b,), (T3_b,))
                ctx.op("dve", lambda: V.tensor_copy(out=D3[:, :, 1:9], in_=sc[:, 1, :].unsqueeze(1).to_broadcast([P, P, 8])), (sc_b,), (D3_b,))
                ctx.op("dve", lambda: V.tensor_tensor_scan(out=S3[:].rearrange("p v j -> p (v j)"), data0=D3[:].rearrange("p v j -> p (v j)"),
                                                           data1=T3[:].rearrange("p v j -> p (v j)"), initial=0.0, op0=ALU.mult, op1=ALU.add),
                       (D3_b, T3_b), (S3_b,))
                if not state_only:
                    ctx.op("dve", lambda: V.tensor_tensor(out=Sra[:], in0=S3[:, :, 0:8].rearrange("p v c -> p c v"),
                                                          in1=sc[:, 0, :].unsqueeze(2).to_broadcast([P, 8, P]), op=ALU.mult), (S3_b, sc_b), (Sra_b,))
                ctx.op("act", lambda: A.copy(out=Sh, in_=S3[:, :, 8]), (S3_b,), (Sst_b,))
                if state_only:
                    continue
                bk, bk_b = next_bank()

                def mmA():
                    for pc in range(4):
                        i = T.matmul(bk[:, pc * P:(pc + 1) * P], lhsT=kt[:, pc * P:(pc + 1) * P], rhs=qt[:, pc * P:(pc + 1) * P],
                                     start=True, stop=True)
                    return i
                ctx.op("pe", mmA, (kt_b, qt_b), (bk_b,))
                ctx.op("dve", lambda: V.tensor_tensor(out=AT[:], in0=bk[:].rearrange("p (c t) -> p c t", t=P),
                                                      in1=hmask[:].unsqueeze(1).to_broadcast([P, 4, P]), op=ALU.mult),
                       (bk_b, hmask_b), (AT_b,))
                oT, oT_b = pbank[4]

                def mmo():
                    for pc in range(4):
                        T.matmul(oT[:, pc * P:(pc + 1) * P], lhsT=vtok[:, pc, :], rhs=AT[:, pc, :], start=True, stop=False)
                        for cc in range(2):
                            c = 2 * pc + cc
                            i = T.matmul(oT[:, c * 64:(c + 1) * 64], lhsT=Sra[:, c, :], rhs=qt[:, c * 64:(c + 1) * 64],
                                         start=False, stop=(cc == 1))
                    return i
                ctx.op("pe", mmo, (vtok_b, AT_b, Sra_b, qt_b), (oT_b,))
                ctx.op("act", lambda: A.activation(out=sq[:], in_=oT[:], func=AF.Square), (oT_b,), (sq_b,))
                bk, bk_b = next_bank()
                ctx.op("pe", lambda: T.matmul(bk[:], lhsT=ones_f[:], rhs=sq[:], start=True, stop=True), (ones_b, sq_b), (bk_b,))
                ctx.op("act", lambda: A.activation(out=rs_[:], in_=bk[:], func=AF.Ln, scale=1.0 / P, bias=eps_ap), (bk_b, smallc_b), (rs_b,))
                ctx.op("act", lambda: A.activation(out=rs_[:], in_=rs_[:], func=AF.Exp, scale=-0.5), (rs_b,), (rs_b,))
                ctx.op("dve", lambda: V.tensor_tensor(out=sq[:], in0=oT[:], in1=rs_[:], op=ALU.mult), (oT_b, rs_b), (sq_b,))
                ctx.op("dve", lambda: V.scalar_tensor_tensor(out=mixT[:, 8 + h, :], in0=sq[:], scalar=vec[:, vc + 56 + h:vc + 56 + h + 1],
                                                             in1=sgate[:], op0=ALU.mult, op1=ALU.mult), (sq_b, vec_b, sgate_b), (hidb[8 + h],))
            ctx.barrier()
        if state_only:
            return
        dump("hgrn", mixT[:, 8:16, :].rearrange("p a b -> p (a b)"), 8 * TG)
        with ExitStack() as st:
            ybuf, _ = sb("ybuf", [P, NS, D], F32, st)
            yb = [Buf(f"y{s}") for s in range(NS)]
            for fb in range(4):
                tokmajor_proj("out", mixT, hidb, 16, ybuf, yb, fb)
            residual_norm_add(ybuf, yb, pmg_d[l])

        dump("xmix", x_res[:].rearrange("p a b -> p (a b)"), 4 * D)
        norm_to_actT(vc + 16)
        with ExitStack() as st:
            sgt = [sb(f"f_sg{i}", [P, TG], F32, st) for i in range(8)]
            for t in range(11):
                wg, wg_b = w_get("gate")
                for c in range(4):
                    bg, bg_b = next_bank()

                    def mmg():
                        for kc in range(16):
                            i = T.matmul(bg[:], lhsT=wg[:, kc, c * P:(c + 1) * P], rhs=actT[:, kc, :], start=(kc == 0), stop=(kc == 15))
                        return i
                    ctx.op("pe", mmg, actb + [wg_b], (bg_b,))
                    s_t, s_b = sgt[(t % 2) * 4 + c]
                    ctx.op("act", lambda: A.activation(out=s_t[:], in_=bg[:], func=AF.Silu), (bg_b,), (s_b,))
                wu, wu_b = w_get("up")
                for c in range(4):
                    bu, bu_b = next_bank()

                    def mmu():
                        for kc in range(16):
                            i = T.matmul(bu[:], lhsT=wu[:, kc, c * P:(c + 1) * P], rhs=actT[:, kc, :], start=(kc == 0), stop=(kc == 15))
                        return i
                    ctx.op("pe", mmu, actb + [wu_b], (bu_b,))
                    s_t, s_b = sgt[(t % 2) * 4 + c]
                    hc = t * 4 + c
                    ctx.op("dve", lambda: V.tensor_tensor(out=R_hid[:, hc, :], in0=s_t[:], in1=bu[:], op=ALU.mult), (s_b, bu_b), (hidb[hc],))
            ctx.barrier()
        with ExitStack() as st:
            ybuf, _ = sb("ybuf2", [P, NS, D], F32, st)
            yb = [Buf(f"y2{s}") for s in range(NS)]
            for fb in range(4):
                for kp in range(4):
                    wt, wb = w_get("down")
                    for s in range(NS):
                        bk, bk_b = pbank[s]

                        def mm():
                            for kc in range(11):
                                hc = kp * 11 + kc
                                i = T.matmul(bk[:], lhsT=R_hid[:, hc, s * P:(s + 1) * P], rhs=wt[:, kc, :],
                                             start=(hc == 0), stop=(hc == 43))
                            return i
                        ctx.op("pe", mm, hidb[kp * 11:(kp + 1) * 11] + [wb], (bk_b,))
                for s in range(NS):
                    bk, bk_b = pbank[s]
                    copy_op(evac_engine(), ybuf[:, s, fb * 512:(fb + 1) * 512], bk[:], (bk_b,), (yb[s],))
            residual_norm_add(ybuf, yb, pfg_d[l])

        dump("xffn", x_res[:].rearrange("p a b -> p (a b)"), 4 * D)
        norm_to_actT(vc + 32)
        with ExitStack() as st:
            ctx.dma("pool", pTt[:], pT_src.rearrange("(kc p) t -> p kc t", p=P), psem, reads=(), writes=(pT_b,))
            sgs = [sb(f"p_sg{i}", [P, 512], F32, st) for i in range(2)]
            tm2 = [sb(f"p_tm{i}", [P, 512], F32, st) for i in range(2)]
            k = 0
            for fb in range(4):
                wg, wg_b = w_get("pg")
                wp, wp_b = w_get("pp", held=1)
                for s in range(NS):
                    bg, bg_b = next_bank()
                    bp, bp_b = next_bank()

                    def mmg():
                        for kc in range(16):
                            i = T.matmul(bg[:], lhsT=actT[:, kc, s * P:(s + 1) * P], rhs=wg[:, kc, :], start=(kc == 0), stop=(kc == 15))
                        return i

                    def mmp():
                        for kc in range(2):
                            i = T.matmul(bp[:], lhsT=pTt[:, kc, s * P:(s + 1) * P], rhs=wp[:, kc, :], start=(kc == 0), stop=(kc == 1))
                        return i
                    ctx.op("pe", mmg, actb + [wg_b], (bg_b,))
                    ctx.op("pe", mmp, (pT_b, wp_b), (bp_b,))
                    s_t, s_b = sgs[k % 2]
                    t_t, t_b = tm2[k % 2]
                    k += 1
                    ctx.op("act", lambda: A.activation(out=s_t[:], in_=bg[:], func=AF.Sigmoid), (bg_b,), (s_b,))
                    ctx.op("dve", lambda: V.tensor_tensor(out=t_t[:], in0=s_t[:], in1=bp[:], op=ALU.mult), (s_b, bp_b), (t_b,))
                    xs = x_res[:, s, fb * 512:(fb + 1) * 512]
                    ctx.op("dve", lambda: V.tensor_tensor(out=xs, in0=xs, in1=t_t[:], op=ALU.add), (xb[s], t_b), (xb[s],))
            ctx.barrier()


    def main_body(NG):
        for g in range(NG):
            ctx.dma("sp", x_res[:], x_d[g * TG:(g + 1) * TG, :].rearrange("(s p) d -> p s d", p=P), xsem, reads=(), writes=tuple(xb))
            rope_tables(posf[:, g * NS:(g + 1) * NS])
            for l in range(NL):
                layer_body(l, amask[:, 2, :] if g == 0 else None, pT_d[l, :, g * TG:(g + 1) * TG])
            ctx.dma("sp", y_d[g * TG:(g + 1) * TG, :].rearrange("(s p) d -> p s d", p=P), x_res[:], ysem, reads=tuple(xb), writes=())

    def main_split():
        x1b = [Buf(f"x1s{g}") for g in range(NG)]
        ssem = DSem(nc, "d_s")
        xsem2 = DSem(nc, "d_x2")
        grp = lambda ap, g: ap[g * TG:(g + 1) * TG, :].rearrange("(s p) d -> p s d", p=P)
        for g in range(NG):
            ctx.dma("sp", x_res[:], grp(x_d, g), xsem, reads=(), writes=tuple(xb))
            rope_tables(posf[:, g * NS:(g + 1) * NS])
            layer_body(0, amask[:, 2, :] if g == 0 else None, pT_d[0, :, g * TG:(g + 1) * TG])
            ctx.dma("sp", grp(x1s_d, g), x_res[:], ssem, reads=tuple(xb), writes=(x1b[g],))
        for g in range(NGH):
            ctx.dma("sp", x_res[:], grp(x1s_d, g), xsem, reads=(x1b[g],), writes=tuple(xb))
            if g == NGH - 1:
                rope_tables(posf[:, g * NS:(g + 1) * NS])
            layer_body(1, None, None, state_only=True, kv_tail=(g == NGH - 1))
        ctx.op("dve", lambda: V.tensor_scalar(out=Sst[:, 1, :, :], in0=Sst[:, 1, :, :], scalar1=flg[:, 1:2], scalar2=None, op0=ALU.mult),
               (Sst_b, flg_b), (Sst_b,))
        ctx.op("dve", lambda: V.tensor_scalar(out=kTb[:, 1, :, :, 1, :], in0=kTb[:, 1, :, :, 1, :], scalar1=flg[:, 1:2], scalar2=None, op0=ALU.mult),
               (kTb_b, flg_b), (kTb_b,))
        ctx.op("dve", lambda: V.tensor_scalar(out=Vb[:, 1, 1, :, :], in0=Vb[:, 1, 1, :, :], scalar1=flg[:, 1:2], scalar2=None, op0=ALU.mult),
               (Vb_b, flg_b), (Vb_b,))
        for j in range(NGH):
            with ExitStack() as st:
                xtmp, xtmp_b = sb("xtmp", [P, NS, D], F32, st)
                ctx.dma("sp", x_res[:], grp(x1s_d, j), xsem, reads=(x1b[j],), writes=tuple(xb))
                for e2 in ("pe", "act", "dve"):
                    ctx._wait("sp", (ctx.sem[e2], ctx.cnt[e2], e2))
                ctx.dma("sp", xtmp[:], grp(x1s_d, NGH + j), xsem2, reads=(x1b[NGH + j],), writes=(xtmp_b,))
                for s in range(NS):
                    ctx.op("dve", lambda: V.tensor_scalar(out=x_res[:, s, :], in0=x_res[:, s, :], scalar1=flg[:, 0:1], scalar2=None, op0=ALU.mult),
                           (xb[s], flg_b), (xb[s],))
                    ctx.op("dve", lambda: V.scalar_tensor_tensor(out=x_res[:, s, :], in0=xtmp[:, s, :], scalar=flg[:, 1:2], in1=x_res[:, s, :],
                                                                 op0=ALU.mult, op1=ALU.add), (xtmp_b, flg_b, xb[s]), (xb[s],))
                ctx.barrier()
            rope_tables(posf1[:, j * NS:(j + 1) * NS])
            layer_body(1, amask1[:] if j == 0 else None, pT1_d[:, j * TG:(j + 1) * TG])
            ctx.dma("sp", grp(y_d, j), x_res[:], ysem, reads=tuple(xb), writes=())
        ctx._wait("sp", (ssem.sem, ssem.cnt, ssem.key))

    try:
      if split:
          main_split()
      else:
          main_body(NG)
    except _Stop:
        top.close()
        return nc
    ctx._wait("sp", (ysem.sem, ysem.cnt, ysem.key))
    assert wst["consumed"] == len(wplan), (wst, len(wplan))
    top.close()
    print(f"[build] ops={ctx.nops} waits={ctx.nwaits} weight_tiles={len(wplan)}")
    return nc


def _fm(v):
    v = np.asarray(v, np.float32)
    return np.ascontiguousarray(v.reshape(-1, P).T)


def _host_consts():
    half = 32
    invf = (10000.0 ** (-np.arange(half, dtype=np.float32) / half)).astype(np.float32)
    tq = np.arange(P)[:, None]
    tk = np.arange(P)[None, :]
    cur = np.where(tk <= tq, 0.0, MASKV).astype(np.float32)
    prev = np.where(tk > tq, 0.0, MASKV).astype(np.float32)
    dead = np.full((P, P), MASKV, np.float32)
    am = np.stack([
        np.concatenate([cur, prev], 1),
        np.concatenate([prev, cur], 1),
        np.concatenate([cur, dead], 1),
        np.concatenate([dead, cur], 1),
    ]).astype(np.float32)
    s = np.arange(P)[:, None]
    t = np.arange(P)[None, :]
    hm = ((s // 64 == t // 64) & (s <= t)).astype(np.float32)
    return invf, am, hm


def _prep_shared(inp, NL=2):
    w_in = np.asarray(inp["w_in"], np.float32)
    cols = list(range(1536))
    for h in range(NHH):
        for sec in range(4):
            base = 1536 + sec * 1024 + h * P
            cols.extend(range(base, base + P))
    w_in_p = np.ascontiguousarray(w_in[:, :, cols])
    vec = np.zeros((P, 2 * NVEC), np.float32)
    for l in range(2):
        b = l * NVEC
        vec[:, b + 0:b + 16] = _fm(inp["pre_mix_gain"][l])
        vec[:, b + 16:b + 32] = _fm(inp["pre_ffn_gain"][l])
        vec[:, b + 32:b + 48] = _fm(inp["ple_gain"][l])
        vec[:, b + 48:b + 56] = _fm(inp["attn_out_gain"][l])
        vec[:, b + 56:b + 64] = _fm(inp["hgrn_out_gain"][l])
        vec[:, b + 64:b + 72] = _fm(inp["hgrn_lb_logits"][l])
    invf, am, hm = _host_consts()
    f = lambda k: np.ascontiguousarray(np.asarray(inp[k], np.float32))
    return {
        "w_in": w_in_p, "w_out": f("w_out"), "w_gate": f("w_ffn_gate"), "w_up": f("w_ffn_up"), "w_down": f("w_ffn_down"),
        "w_pg": f("w_ple_gate"), "w_pp": f("w_ple_proj"), "vec_fm": vec,
        "post_mix_gain": f("post_mix_gain"), "post_ffn_gain": f("post_ffn_gain"),
        "sinks": np.ascontiguousarray(np.asarray(inp["attn_sinks"], np.float32).reshape(32)),
        "invf": invf, "amask": am, "hmask": hm,
    }


def _prep_core(inp, b, NTOK):
    x = np.ascontiguousarray(np.asarray(inp["x"], np.float32)[b, :NTOK])
    pT = np.ascontiguousarray(np.transpose(np.asarray(inp["p"], np.float32)[:, b, :NTOK, :], (0, 2, 1)))
    pos = np.asarray(inp["positions"])[b, :NTOK].astype(np.int32)
    posT = np.ascontiguousarray(pos.reshape(-1, P).T)
    return {"x": x, "pT": pT, "posT": posT}


def _prep_core_split(inp, b, half, S):
    H = S // 2
    x = np.ascontiguousarray(np.asarray(inp["x"], np.float32)[b])
    p = np.asarray(inp["p"], np.float32)
    pT = np.ascontiguousarray(np.transpose(p[:, b], (0, 2, 1)))
    pT1 = np.ascontiguousarray(pT[1][:, half * H:(half + 1) * H])
    pos = np.asarray(inp["positions"])[b].astype(np.int32)
    posT = np.ascontiguousarray(pos.reshape(-1, P).T)
    posT1 = np.ascontiguousarray(pos[half * H:(half + 1) * H].reshape(-1, P).T)
    _, am, _ = _host_consts()
    flags = np.array([1.0, 0.0] if half == 0 else [0.0, 1.0], np.float32)
    amask1 = np.ascontiguousarray(am[2] if half == 0 else am[0])
    return {"x": x, "pT": pT, "posT": posT, "pT1": pT1, "posT1": posT1, "flags": flags, "amask1": amask1}


def kernel(**inputs):
    B, S = 4, 4096
    nc = build(S, 2, split=True)
    shared = _prep_shared(inputs)
    in_maps = []
    for c in range(2 * B):
        m = dict(shared)
        m.update(_prep_core_split(inputs, c // 2, c % 2, S))
        in_maps.append(m)
    res = run_bass_kernel_spmd(nc, in_maps, core_ids=list(range(2 * B)))
    H = S // 2
    out = np.empty((B, S, D), np.float32)
    for c in range(2 * B):
        out[c // 2, (c % 2) * H:(c % 2 + 1) * H] = np.asarray(res.results[c]["y"], np.float32)
    return out
```

```python
import math
from contextlib import ExitStack

import numpy as np
import concourse.bass as bass
import concourse.mybir as mybir
from concourse.bass_utils import run_bass_kernel_spmd

F32 = mybir.dt.float32
BF16 = mybir.dt.bfloat16
I32 = mybir.dt.int32
AF = mybir.ActivationFunctionType
ALU = mybir.AluOpType
AX = mybir.AxisListType

P = 128
D = 2048
TG = 512
NS = 4
DFF = 5632
DIN = 5632
NQH = 16
NKV = 4
HD = 64
NHH = 8
EPS = 1e-6
MASKV = -30000.0
NVEC = 72


class Buf:
    __slots__ = ("name", "w", "r")

    def __init__(self, name):
        self.name = name
        self.w = None
        self.r = {}


class DSem:
    ALL = []

    def __init__(self, nc, name):
        self.sem = nc.alloc_semaphore(name)
        self.cnt = 0
        self.key = name
        DSem.ALL.append(self)


class Ctx:
    def __init__(self, nc):
        self.nc = nc
        self.E = {"pe": nc.tensor, "act": nc.scalar, "dve": nc.vector, "pool": nc.gpsimd, "sp": nc.sync}
        self.sem = {}
        self.cnt = {}
        for e in self.E:
            self.sem[e] = nc.alloc_semaphore("s_" + e)
            self.cnt[e] = 0
        self.waited = {}
        self.nwaits = 0
        self.nops = 0

    def _wait(self, e, sig):
        if sig is None:
            return
        sem, val, key = sig
        k = (e, key)
        if self.waited.get(k, 0) >= val:
            return
        self.E[e].wait_ge(sem, val)
        self.waited[k] = val
        self.nwaits += 1

    def _pre(self, e, reads, writes):
        for b in reads:
            if not (e == "pe" and b.w is not None and b.w[2] == "pe"):
                self._wait(e, b.w)
        for b in writes:
            if not (e == "pe" and b.w is not None and b.w[2] == "pe"):
                self._wait(e, b.w)
            for rk, rs in b.r.items():
                if not (e == "pe" and rk == "pe"):
                    self._wait(e, rs)

    def _post(self, sig, reads, writes):
        for b in reads:
            b.r[sig[2]] = sig
        for b in writes:
            b.w = sig
            b.r = {}

    def op(self, e, fn, reads=(), writes=()):
        self._pre(e, reads, writes)
        inst = fn()
        self.cnt[e] += 1
        inst.then_inc(self.sem[e], 1)
        sig = (self.sem[e], self.cnt[e], e)
        self._post(sig, reads, writes)
        self.nops += 1
        return sig

    def dma(self, q, out, in_, dsem, reads=(), writes=()):
        self._pre(q, reads, writes)
        inst = self.E[q].dma_start(out=out, in_=in_)
        dsem.cnt += 16
        inst.then_inc(dsem.sem, 16)
        sig = (dsem.sem, dsem.cnt, dsem.key)
        self._post(sig, reads, writes)
        return sig

    def barrier(self, engines=("pe", "act", "dve")):
        for e in engines:
            for e2 in engines:
                if self.cnt[e2] > 0:
                    self._wait(e, (self.sem[e2], self.cnt[e2], e2))


class _Stop(Exception):
    pass


def build(NTOK, NL, dbg=None, split=False):
    NG = NTOK // TG
    NB = NTOK // P
    NGH = NG // 2
    NTH = NTOK // 2
    nc = bass.Bass("TRN2", target_bir_lowering=False)
    DSem.ALL = []
    V, A, T, G = nc.vector, nc.scalar, nc.tensor, nc.gpsimd

    def din(name, shape, dt=F32):
        return nc.dram_tensor(name, shape, dt, kind="ExternalInput").ap()

    x_d = din("x", [NTOK, D])
    pT_d = din("pT", [2, 256, NTOK])
    pos_d = din("posT", [P, NB], I32)
    w_in_d = din("w_in", [2, D, DIN])
    w_out_d = din("w_out", [2, D, D])
    w_g_d = din("w_gate", [2, D, DFF])
    w_u_d = din("w_up", [2, D, DFF])
    w_d_d = din("w_down", [2, DFF, D])
    w_pg_d = din("w_pg", [2, D, D])
    w_pp_d = din("w_pp", [2, 256, D])
    vec_d = din("vec_fm", [P, 2 * NVEC])
    pmg_d = din("post_mix_gain", [2, D])
    pfg_d = din("post_ffn_gain", [2, D])
    sink_d = din("sinks", [32])
    invf_d = din("invf", [32])
    amask_d = din("amask", [4, P, 256])
    hmask_d = din("hmask", [P, P])
    if split:
        pT1_d = din("pT1", [256, NTH])
        pos1_d = din("posT1", [P, NB // 2], I32)
        flags_d = din("flags", [2])
        amask1_d = din("amask1", [P, 256])
        x1s_d = nc.dram_tensor("x1s", [NTOK, D], F32).ap()
        y_d = nc.dram_tensor("y", [NTH, D], F32, kind="ExternalOutput").ap()
    else:
        y_d = nc.dram_tensor("y", [NTOK, D], F32, kind="ExternalOutput").ap()
    dbg_d = None
    if dbg is not None:
        dbg_d = nc.dram_tensor("dbg", [P, 16 * TG], F32, kind="ExternalOutput").ap()

    ctx = Ctx(nc)
    top = ExitStack()
    dsem_dbg = DSem(nc, "d_dbg")

    def dump(stage, ap2d, ncols):
        if dbg is None or dbg != stage:
            return
        ctx.barrier(("pe", "act", "dve", "pool"))
        nc.gpsimd.dma_start(out=dbg_d[:, 0:ncols], in_=ap2d).then_inc(dsem_dbg.sem, 16)
        dsem_dbg.cnt = 16
        for d_ in DSem.ALL:
            if d_.cnt > 0:
                nc.gpsimd.wait_ge(d_.sem, d_.cnt)
        raise _Stop()

    uniq = {"n": 0}

    def sb(name, shape, dt, st=top):
        uniq["n"] += 1
        t = st.enter_context(nc.sbuf_tensor(f"sb_{name}_{uniq['n']}", list(shape), dt))
        return t, Buf(name)

    def ps(name, shape, dt):
        t = top.enter_context(nc.psum_tensor("ps_" + name, list(shape), dt))
        return t, Buf(name)

    x_res, _ = sb("x_res", [P, NS, D], F32)
    xb = [Buf(f"x{s}") for s in range(NS)]
    actT, _ = sb("actT", [P, 16, TG], BF16)
    actb = [Buf(f"actT{k}") for k in range(16)]
    R_hid, _ = sb("hidT", [P, 44, TG], BF16)
    hidb = [Buf(f"hid{k}") for k in range(44)]
    mixT = R_hid
    WS = 3
    wslots = [sb(f"wslot{i}", [P, 16, 512], BF16) for i in range(WS)]
    wsems = [DSem(nc, f"d_w{i}") for i in range(WS)]
    Sst, Sst_b = sb("Sst", [P, 2, NHH, P], F32)
    kTb, kTb_b = sb("kTb", [P, 2, 2, NKV, 2, P], BF16)
    Vb, Vb_b = sb("Vb", [P, 2, 2, NKV, HD], BF16)
    cs_t, cs_b = sb("cs", [P, 2, NS, 32], F32)
    amask, amask_b = sb("amask", [P, 4, 256], F32)
    amask1_b = Buf("amask1")
    if split:
        amask1, amask1_b = sb("amask1", [P, 256], F32)
        flg, flg_b = sb("flg", [P, 2], F32)
        posi1, posi1_b = sb("posi1", [P, NB // 2], I32)
        posf1, posf1_b = sb("posf1", [P, NB // 2], F32)
    hmask, hmask_b = sb("hmask", [P, P], F32)
    vec, vec_b = sb("vec", [P, 2 * NVEC], F32)
    ident, ident_b = sb("ident", [P, P], BF16)
    ones_f, ones_b = sb("ones_f", [P, P], F32)
    sinkt, sink_b = sb("sinkt", [P, 2, 32], F32)
    invf, invf_b = sb("invf", [P, 32], F32)
    posi, posi_b = sb("posi", [P, NB], I32)
    posf, posf_b = sb("posf", [P, NB], F32)
    lbt, lbt_b = sb("lbt", [P, 2, 2, NHH], F32)
    smallc, smallc_b = sb("smallc", [P, 8], F32)
    pTt, pT_b = sb("pTt", [P, 2, TG], BF16)

    pbank = [ps(f"pb{i}", [P, 512], F32) for i in range(5)]
    pdS = ps("pdS", [P, 512], F32)
    ptb = [ps(f"ptb{i}", [P, 1024], BF16) for i in range(2)]
    rot = {"pb": 0, "pt": 0}

    def next_bank():
        i = rot["pb"]
        rot["pb"] = (i + 1) % 4
        return pbank[i]

    def next_pt():
        i = rot["pt"]
        rot["pt"] = (i + 1) % 2
        return ptb[i]

    flip = {"e": 0}

    def evac_engine():
        flip["e"] ^= 1
        return "act" if flip["e"] else "dve"

    def copy_op(e, out, in_, reads, writes):
        if e == "act":
            return ctx.op("act", lambda: A.copy(out=out, in_=in_), reads, writes)
        return ctx.op("dve", lambda: V.tensor_copy(out=out, in_=in_), reads, writes)

    def scale_op(e, out, in_, sc_ap, reads, writes):
        if e == "act":
            return ctx.op("act", lambda: A.activation(out=out, in_=in_, func=AF.Identity, scale=sc_ap), reads, writes)
        return ctx.op("dve", lambda: V.tensor_scalar(out=out, in0=in_, scalar1=sc_ap, scalar2=None, op0=ALU.mult), reads, writes)

    def setup_load(name, out, in_, buf, q="sp"):
        ctx.dma(q, out, in_, DSem(nc, "d_" + name), reads=(), writes=(buf,))

    setup_load("vec", vec[:], vec_d[:, :], vec_b)
    setup_load("amask", amask[:], amask_d.rearrange("v p k -> p v k"), amask_b)
    setup_load("hmask", hmask[:], hmask_d[:, :], hmask_b)
    setup_load("sink", sinkt[:, 0, :], sink_d.partition_broadcast(P), sink_b)
    setup_load("invf", invf[:], invf_d.partition_broadcast(P), invf_b)
    setup_load("pos", posi[:], pos_d[:, :], posi_b)
    if split:
        setup_load("amask1", amask1[:], amask1_d[:, :], amask1_b)
        setup_load("flg", flg[:], flags_d.partition_broadcast(P), flg_b)
        setup_load("pos1", posi1[:], pos1_d[:, :], posi1_b)
        ctx.op("dve", lambda: V.tensor_copy(out=posf1[:], in_=posi1[:]), (posi1_b,), (posf1_b,))

    ctx.op("dve", lambda: V.memset(ones_f[:], 1.0), (), (ones_b,))
    ctx.op("dve", lambda: V.memset(ident[:], 1.0), (), (ident_b,))
    ctx.op("pool", lambda: G.affine_select(out=ident[:], in_=ident[:], pattern=[[-1, P]], compare_op=ALU.is_equal,
                                           fill=0.0, base=0, channel_multiplier=1), (ident_b,), (ident_b,))
    ctx.op("dve", lambda: V.memset(Sst[:], 0.0), (), (Sst_b,))
    ctx.op("dve", lambda: V.memset(kTb[:], 0.0), (), (kTb_b,))
    ctx.op("dve", lambda: V.memset(Vb[:], 0.0), (), (Vb_b,))
    ctx.op("dve", lambda: V.tensor_copy(out=posf[:], in_=posi[:]), (posi_b,), (posf_b,))
    ctx.op("dve", lambda: V.tensor_scalar(out=sinkt[:, 1, :], in0=sinkt[:, 0, :], scalar1=-1.0, scalar2=None, op0=ALU.mult),
           (sink_b,), (sink_b,))
    LB0 = 64
    l0 = vec[:, LB0:LB0 + 8]
    l1 = vec[:, NVEC + LB0:NVEC + LB0 + 8]
    with ExitStack() as st:
        tm, tm_b = sb("lb_m", [P, 8], F32, st)
        e0, e0_b = sb("lb_e0", [P, 8], F32, st)
        e1, e1_b = sb("lb_e1", [P, 8], F32, st)
        ctx.op("dve", lambda: V.tensor_tensor(out=tm[:], in0=l0, in1=l1, op=ALU.max), (vec_b,), (tm_b,))
        ctx.op("dve", lambda: V.tensor_tensor(out=e0[:], in0=l0, in1=tm[:], op=ALU.subtract), (vec_b, tm_b), (e0_b,))
        ctx.op("dve", lambda: V.tensor_tensor(out=e1[:], in0=l1, in1=tm[:], op=ALU.subtract), (vec_b, tm_b), (e1_b,))
        ctx.op("act", lambda: A.activation(out=e0[:], in_=e0[:], func=AF.Exp), (e0_b,), (e0_b,))
        ctx.op("act", lambda: A.activation(out=e1[:], in_=e1[:], func=AF.Exp), (e1_b,), (e1_b,))
        ctx.op("dve", lambda: V.tensor_tensor(out=tm[:], in0=e0[:], in1=e1[:], op=ALU.add), (e0_b, e1_b), (tm_b,))
        ctx.op("dve", lambda: V.reciprocal(out=tm[:], in_=tm[:]), (tm_b,), (tm_b,))
        ctx.op("dve", lambda: V.memset(lbt[:, 0, 0, :], 0.0), (), (lbt_b,))
        ctx.op("dve", lambda: V.tensor_tensor(out=lbt[:, 1, 0, :], in0=e1[:], in1=tm[:], op=ALU.mult), (e1_b, tm_b), (lbt_b,))
        ctx.op("dve", lambda: V.tensor_scalar(out=lbt[:, :, 1, :], in0=lbt[:, :, 0, :], scalar1=-1.0, scalar2=1.0,
                                              op0=ALU.mult, op1=ALU.add), (lbt_b,), (lbt_b,))
        ctx.barrier()

    def layer_tiles(l):
        for t in range(11):
            yield ("in", w_in_d[l, :, t * 512:(t + 1) * 512].rearrange("(kc p) n -> p kc n", p=P), 16, 512)
        for fb in range(4):
            yield ("out", w_out_d[l, :, fb * 512:(fb + 1) * 512].rearrange("(kc p) n -> p kc n", p=P), 16, 512)
        for t in range(11):
            yield ("gate", w_g_d[l, :, t * 512:(t + 1) * 512].rearrange("(kc p) n -> p kc n", p=P), 16, 512)
            yield ("up", w_u_d[l, :, t * 512:(t + 1) * 512].rearrange("(kc p) n -> p kc n", p=P), 16, 512)
        for fb in range(4):
            for kp in range(4):
                yield ("down", w_d_d[l, kp * 1408:(kp + 1) * 1408, fb * 512:(fb + 1) * 512]
                       .rearrange("(kc p) n -> p kc n", p=P), 11, 512)
        for fb in range(4):
            yield ("pg", w_pg_d[l, :, fb * 512:(fb + 1) * 512].rearrange("(kc p) n -> p kc n", p=P), 16, 512)
            yield ("pp", w_pp_d[l, :, fb * 512:(fb + 1) * 512].rearrange("(kc p) n -> p kc n", p=P), 2, 512)

    def weight_plan():
        if not split:
            for g in range(NG):
                for l in range(NL):
                    yield from layer_tiles(l)
            return
        for g in range(NG):
            yield from layer_tiles(0)
        for g in range(NGH):
            if g == NGH - 1:
                yield ("in", w_in_d[1, :, 1024:1536].rearrange("(kc p) n -> p kc n", p=P), 16, 512)
            for h in range(NHH):
                c0 = 1536 + h * 512 + 128
                yield ("inp", w_in_d[1, :, c0:c0 + 256].rearrange("(kc p) n -> p kc n", p=P), 16, 256)
        for j in range(NGH):
            yield from layer_tiles(1)

    wplan = list(weight_plan())
    wst = {"issued": 0, "consumed": 0}

    def w_issue(upto):
        while wst["issued"] < min(upto, len(wplan)):
            i = wst["issued"]
            tag, src, kc, ncol = wplan[i]
            t, b = wslots[i % WS]
            ctx.dma("pool", t[:, 0:kc, 0:ncol], src, wsems[i % WS], reads=(), writes=(b,))
            wst["issued"] += 1

    def w_get(tag, held=0):
        i = wst["consumed"]
        assert wplan[i][0] == tag, (wplan[i][0], tag)
        w_issue(i + WS - held)
        wst["consumed"] += 1
        return wslots[i % WS]

    xsem = DSem(nc, "d_x")
    ysem = DSem(nc, "d_y")
    psem = DSem(nc, "d_p")
    gsem = DSem(nc, "d_g")

    def rstd_from_ss(ss, ss_b, n, width):
        ctx.op("act", lambda: A.activation(out=ss[:, 0:n], in_=ss[:, 0:n], func=AF.Ln, scale=1.0 / width, bias=eps_ap),
               (ss_b, smallc_b), (ss_b,))
        ctx.op("act", lambda: A.activation(out=ss[:, 0:n], in_=ss[:, 0:n], func=AF.Exp, scale=-0.5), (ss_b,), (ss_b,))

    ctx.op("dve", lambda: V.memset(smallc[:, 0:1], EPS), (), (smallc_b,))
    eps_ap = smallc[:, 0:1]

    def norm_to_actT(gcol):
        with ExitStack() as st:
            hb4, hb4_b = sb("hb4", [P, NS, D], BF16, st)
            ss, ss_b = sb("nss", [P, NS], F32, st)
            hbs = [Buf(f"hb{s}") for s in range(NS)]
            for s in range(NS):
                ctx.op("act", lambda: A.activation(out=hb4[:, s, :], in_=x_res[:, s, :], func=AF.Square,
                                                   accum_out=ss[:, s:s + 1]), (xb[s],), (hbs[s], ss_b))
            rstd_from_ss(ss, ss_b, NS, D)
            for s in range(NS):
                scale_op("dve" if s % 2 else "act", hb4[:, s, :], x_res[:, s, :], ss[:, s:s + 1], (xb[s], ss_b), (hbs[s],))
            for kc in range(16):
                pt, pt_b = next_pt()

                def tr():
                    for s in range(NS):
                        i = T.transpose(out=pt[:, s * P:(s + 1) * P], in_=hb4[:, s, kc * P:(kc + 1) * P], identity=ident[:])
                    return i
                ctx.op("pe", tr, hbs + [ident_b], (pt_b,))
                scale_op(evac_engine(), actT[:, kc, :], pt[:, 0:TG], vec[:, gcol + kc:gcol + kc + 1], (pt_b, vec_b), (actb[kc],))
            ctx.barrier()

    def res_begin(gain_d_row, st):
        rc = {}
        rc["junk"] = actT[:, 0, :]
        rc["junk_bs"] = (actb[0],)
        rc["gbc"] = actT[:, 4:12, :].rearrange("p a b -> p (a b)").bitcast(F32)
        rc["gbc_bs"] = tuple(actb[4:12])
        rc["ssp"], rc["ssp_b"] = sb("rssp", [P, NS, 4], F32, st)
        rc["ss"], rc["ss_b"] = sb("rss", [P, NS], F32, st)
        ctx.dma("sp", rc["gbc"], gain_d_row.partition_broadcast(P), gsem, reads=(), writes=rc["gbc_bs"])
        return rc

    def res_evac(rc, bk, bk_b, ybuf, yb, s, fb):
        ctx.op("act", lambda: A.activation(out=rc["junk"], in_=bk[:], func=AF.Square, accum_out=rc["ssp"][:, s, fb:fb + 1]),
               (bk_b,), rc["junk_bs"] + (rc["ssp_b"],))
        ctx.op("dve", lambda: V.tensor_tensor(out=ybuf[:, s, fb * 512:(fb + 1) * 512], in0=bk[:], in1=rc["gbc"][:, fb * 512:(fb + 1) * 512],
                                              op=ALU.mult), (bk_b,) + rc["gbc_bs"] + rc["junk_bs"], (yb[s],))

    def res_finish(rc, ybuf, yb):
        ss, ss_b = rc["ss"], rc["ss_b"]
        ctx.op("dve", lambda: V.tensor_reduce(out=ss[:], in_=rc["ssp"][:], axis=AX.X, op=ALU.add), (rc["ssp_b"],), (ss_b,))
        rstd_from_ss(ss, ss_b, NS, D)
        for s in range(NS):
            ctx.op("dve", lambda: V.scalar_tensor_tensor(out=x_res[:, s, :], in0=ybuf[:, s, :], scalar=ss[:, s:s + 1], in1=x_res[:, s, :],
                                                         op0=ALU.mult, op1=ALU.add), (yb[s], ss_b, xb[s]), (xb[s],))
        ctx.barrier()

    def tokmajor_proj(tag, srcT, srcb, nk, ybuf, yb, fb, rc):
        wt, wb = w_get(tag)
        for s in range(NS):
            bk, bk_b = next_bank()

            def mm():
                for kc in range(nk):
                    i = T.matmul(bk[:], lhsT=srcT[:, kc, s * P:(s + 1) * P], rhs=wt[:, kc, :], start=(kc == 0), stop=(kc == nk - 1))
                return i
            ctx.op("pe", mm, list(srcb[0:nk]) + [wb], (bk_b,))
            res_evac(rc, bk, bk_b, ybuf, yb, s, fb)

    def rope_tables(pos_ap):
        with ExitStack() as st:
            ang, ang_b = sb("ang", [P, NS, 32], F32, st)
            kk, kk_b = sb("kk", [P, NS, 32], F32, st)
            ki, ki_b = sb("ki", [P, NS, 32], I32, st)
            yy, yy_b = sb("yy", [P, NS, 32], F32, st)
            mk, mk_b = sb("mk", [P, NS, 32], F32, st)
            C1 = 6.28125
            C2 = 2.0 * math.pi - C1
            ctx.op("dve", lambda: V.tensor_tensor(out=ang[:], in0=invf[:].unsqueeze(1).to_broadcast([P, NS, 32]),
                                                  in1=pos_ap.unsqueeze(2).to_broadcast([P, NS, 32]),
                                                  op=ALU.mult), (invf_b, posf_b) + ((posf1_b,) if split else ()), (ang_b,))
            for which, shift in ((1, 0.0), (0, math.pi / 2)):
                ctx.op("dve", lambda: V.tensor_scalar(out=kk[:], in0=ang[:], scalar1=shift, scalar2=1.0 / (2 * math.pi),
                                                      op0=ALU.add, op1=ALU.mult), (ang_b,), (kk_b,))
                ctx.op("dve", lambda: V.tensor_copy(out=ki[:], in_=kk[:]), (kk_b,), (ki_b,))
                ctx.op("dve", lambda: V.tensor_copy(out=kk[:], in_=ki[:]), (ki_b,), (kk_b,))
                ctx.op("dve", lambda: V.scalar_tensor_tensor(out=yy[:], in0=kk[:], scalar=-C1, in1=ang[:], op0=ALU.mult, op1=ALU.add),
                       (kk_b, ang_b), (yy_b,))
                ctx.op("dve", lambda: V.scalar_tensor_tensor(out=yy[:], in0=kk[:], scalar=-C2, in1=yy[:], op0=ALU.mult, op1=ALU.add),
                       (kk_b, yy_b), (yy_b,))
                if shift != 0.0:
                    ctx.op("dve", lambda: V.tensor_scalar(out=yy[:], in0=yy[:], scalar1=shift, scalar2=None, op0=ALU.add), (yy_b,), (yy_b,))
                ctx.op("dve", lambda: V.tensor_scalar(out=mk[:], in0=yy[:], scalar1=math.pi, scalar2=-2 * math.pi, op0=ALU.is_gt, op1=ALU.mult),
                       (yy_b,), (mk_b,))
                ctx.op("dve", lambda: V.tensor_tensor(out=yy[:], in0=yy[:], in1=mk[:], op=ALU.add), (yy_b, mk_b), (yy_b,))
                ctx.op("dve", lambda: V.tensor_scalar(out=mk[:], in0=yy[:], scalar1=-math.pi, scalar2=2 * math.pi, op0=ALU.is_lt, op1=ALU.mult),
                       (yy_b,), (mk_b,))
                ctx.op("dve", lambda: V.tensor_tensor(out=yy[:], in0=yy[:], in1=mk[:], op=ALU.add), (yy_b, mk_b), (yy_b,))
                ctx.op("dve", lambda: V.tensor_scalar(out=yy[:], in0=yy[:], scalar1=3.1415925, scalar2=-3.1415925, op0=ALU.min, op1=ALU.max),
                       (yy_b,), (yy_b,))
                ctx.op("act", lambda: A.activation(out=cs_t[:, which, :, :], in_=yy[:], func=AF.Sin), (yy_b,), (cs_b,))
            ctx.barrier()


    def layer_body(l, first_mask, pT_src, state_only=False, kv_tail=False):
        vc = l * NVEC

        norm_to_actT(vc + 0)
        dump("actT", actT[:].rearrange("p a b -> p (a b)"), 16 * TG)
        if state_only and kv_tail:
            with ExitStack() as st:
                kvt, kvt_b = sb("kvt", [P, 512], F32, st)
                t1, t1_b = sb("kt1", [P, NKV, 32], F32, st)
                t2, t2_b = sb("kt2", [P, NKV, 32], F32, st)
                kdup, kdup_b = sb("kdup2", [P, NKV, 2, HD], BF16, st)
                wt, wb = w_get("in")
                bk, bk_b = next_bank()

                def mmkv():
                    for kc in range(16):
                        i = T.matmul(bk[:], lhsT=actT[:, kc, 3 * P:4 * P], rhs=wt[:, kc, :], start=(kc == 0), stop=(kc == 15))
                    return i
                ctx.op("pe", mmkv, actb + [wb], (bk_b,))
                copy_op("act", kvt[:], bk[:], (bk_b,), (kvt_b,))
                cosb = cs_t[:, 0, 3, :].unsqueeze(1).to_broadcast([P, NKV, 32])
                sinb = cs_t[:, 1, 3, :].unsqueeze(1).to_broadcast([P, NKV, 32])
                kk4 = kvt[:, 0:256].rearrange("p (h d) -> p h d", d=HD)
                x1 = kk4[:, :, 0:32]
                x2 = kk4[:, :, 32:64]
                ctx.op("dve", lambda: V.tensor_tensor(out=t1[:], in0=x1, in1=cosb, op=ALU.mult), (kvt_b, cs_b), (t1_b,))
                ctx.op("dve", lambda: V.tensor_tensor(out=t2[:], in0=x2, in1=sinb, op=ALU.mult), (kvt_b, cs_b), (t2_b,))
                for dd in range(2):
                    ctx.op("dve", lambda: V.tensor_tensor(out=kdup[:, :, dd, 0:32], in0=t1[:], in1=t2[:], op=ALU.subtract), (t1_b, t2_b), (kdup_b,))
                ctx.op("dve", lambda: V.tensor_tensor(out=t1[:], in0=x2, in1=cosb, op=ALU.mult), (kvt_b, cs_b), (t1_b,))
                ctx.op("dve", lambda: V.tensor_tensor(out=t2[:], in0=x1, in1=sinb, op=ALU.mult), (kvt_b, cs_b), (t2_b,))
                for dd in range(2):
                    ctx.op("dve", lambda: V.tensor_tensor(out=kdup[:, :, dd, 32:64], in0=t1[:], in1=t2[:], op=ALU.add), (t1_b, t2_b), (kdup_b,))
                ctx.op("act", lambda: A.copy(out=Vb[:, l, 1, :, :], in_=kvt[:, 256:512].rearrange("p (g d) -> p g d", d=HD)), (kvt_b,), (Vb_b,))
                pt, pt_b = next_pt()
                kdf = kdup[:].rearrange("p g t d -> p g (t d)")

                def trk():
                    for gg in range(NKV):
                        i = T.transpose(out=pt[:, gg * P:(gg + 1) * P], in_=kdf[:, gg, :], identity=ident[:])
                    return i
                ctx.op("pe", trk, (kdup_b, ident_b), (pt_b,))
                copy_op("act", kTb[0:64, l, 0, :, 1, :], pt[0:64, 0:512].rearrange("p (j t) -> p j t", t=P), (pt_b,), (kTb_b,))
                copy_op("dve", kTb[64:128, l, 1, :, 1, :], pt[64:128, 0:512].rearrange("p (j t) -> p j t", t=P), (pt_b,), (kTb_b,))
                ctx.barrier()
        if not state_only:
            with ExitStack() as st:
                qkv = R_hid[:, 16:40, :].rearrange("p a b -> p (a b)").bitcast(F32).rearrange("p (s c) -> p s c", c=1536)
                qkvb = [Buf(f"qkv{s}") for s in range(NS)]
                an4, _ = sb("an4", [P, NS, 1024], BF16, st)
                an4b = [Buf(f"an4_{s}") for s in range(NS)]
                for t in range(3):
                    wt, wb = w_get("in")
                    for s in range(NS):
                        bk, bk_b = next_bank()

                        def mm():
                            for kc in range(16):
                                i = T.matmul(bk[:], lhsT=actT[:, kc, s * P:(s + 1) * P], rhs=wt[:, kc, :], start=(kc == 0), stop=(kc == 15))
                            return i
                        ctx.op("pe", mm, actb + [wb], (bk_b,))
                        copy_op(evac_engine(), qkv[:, s, t * 512:(t + 1) * 512], bk[:], (bk_b,), (qkvb[s],))

                with ExitStack() as st2:
                    t1, t1_b = sb("rt1", [P, 20, 32], F32, st2)
                    t2, t2_b = sb("rt2", [P, 20, 32], F32, st2)
                    qr, qr_b = sb("qr", [P, NQH, HD], BF16, st2)
                    kdup, kdup_b = sb("kdup", [P, NKV, 2, HD], BF16, st2)
                    qT, qT_b = sb("qT", [P, 8, P], BF16, st2)
                    NR = 3
                    Ssb = [sb(f"Ssb{i}", [P, 2, 256], F32, st2) for i in range(NR)]
                    eb = [sb(f"eb{i}", [P, 2, 256], BF16, st2) for i in range(NR)]
                    eT = [sb(f"eT{i}", [P, 4, P], BF16, st2) for i in range(NR)]
                    stat, stat_b = sb("stat", [P, 8, NQH], F32, st2)
                    atok, atok_b = sb("atok", [P, 1024], F32, st2)
                    ajunk, ajunk_b = sb("ajunk", [P, 1024], BF16, st2)
                    ass, ass_b = sb("ass", [P, NS], F32, st2)

                    for s in range(NS):
                        slot = s % 2
                        mask_ap = first_mask if (s == 0 and first_mask is not None) else amask[:, slot, :]
                        cosb = cs_t[:, 0, s, :].unsqueeze(1).to_broadcast([P, 20, 32])
                        sinb = cs_t[:, 1, s, :].unsqueeze(1).to_broadcast([P, 20, 32])
                        qk = qkv[:, s, 0:1280].rearrange("p (h d) -> p h d", d=HD)
                        x1 = qk[:, :, 0:32]
                        x2 = qk[:, :, 32:64]
                        ctx.op("dve", lambda: V.tensor_tensor(out=t1[:], in0=x1, in1=cosb, op=ALU.mult), (qkvb[s], cs_b), (t1_b,))
                        ctx.op("dve", lambda: V.tensor_tensor(out=t2[:], in0=x2, in1=sinb, op=ALU.mult), (qkvb[s], cs_b), (t2_b,))
                        ctx.op("dve", lambda: V.tensor_tensor(out=qr[:, :, 0:32], in0=t1[:, 0:16, :], in1=t2[:, 0:16, :], op=ALU.subtract),
                               (t1_b, t2_b), (qr_b,))
                        for dd in range(2):
                            ctx.op("dve", lambda: V.tensor_tensor(out=kdup[:, :, dd, 0:32], in0=t1[:, 16:20, :], in1=t2[:, 16:20, :], op=ALU.subtract),
                                   (t1_b, t2_b), (kdup_b,))
                        ctx.op("dve", lambda: V.tensor_tensor(out=t1[:], in0=x2, in1=cosb, op=ALU.mult), (qkvb[s], cs_b), (t1_b,))
                        ctx.op("dve", lambda: V.tensor_tensor(out=t2[:], in0=x1, in1=sinb, op=ALU.mult), (qkvb[s], cs_b), (t2_b,))
                        ctx.op("dve", lambda: V.tensor_tensor(out=qr[:, :, 32:64], in0=t1[:, 0:16, :], in1=t2[:, 0:16, :], op=ALU.add),
                               (t1_b, t2_b), (qr_b,))
                        for dd in range(2):
                            ctx.op("dve", lambda: V.tensor_tensor(out=kdup[:, :, dd, 32:64], in0=t1[:, 16:20, :], in1=t2[:, 16:20, :], op=ALU.add),
                                   (t1_b, t2_b), (kdup_b,))
                        ctx.op("act", lambda: A.copy(out=Vb[:, l, slot, :, :], in_=qkv[:, s, 1280:1536].rearrange("p (g d) -> p g d", d=HD)),
                               (qkvb[s],), (Vb_b,))
                        qrf = qr[:].rearrange("p h d -> p (h d)")
                        for half in range(2):
                            pt, pt_b = next_pt()

                            def trq():
                                for j in range(4):
                                    jj = half * 4 + j
                                    i = T.transpose(out=pt[:, j * P:(j + 1) * P], in_=qrf[:, jj * P:(jj + 1) * P], identity=ident[:])
                                return i
                            ctx.op("pe", trq, (qr_b, ident_b), (pt_b,))
                            copy_op(evac_engine(), qT[:, half * 4:(half + 1) * 4, :], pt[:, 0:512].rearrange("p (j t) -> p j t", t=P),
                                    (pt_b,), (qT_b,))
                        pt, pt_b = next_pt()
                        kdf = kdup[:].rearrange("p g t d -> p g (t d)")

                        def trk():
                            for gg in range(NKV):
                                i = T.transpose(out=pt[:, gg * P:(gg + 1) * P], in_=kdf[:, gg, :], identity=ident[:])
                            return i
                        ctx.op("pe", trk, (kdup_b, ident_b), (pt_b,))
                        copy_op("act", kTb[0:64, l, 0, :, slot, :], pt[0:64, 0:512].rearrange("p (j t) -> p j t", t=P), (pt_b,), (kTb_b,))
                        copy_op("dve", kTb[64:128, l, 1, :, slot, :], pt[64:128, 0:512].rearrange("p (j t) -> p j t", t=P), (pt_b,), (kTb_b,))

                        obk = [pbank[4], pdS]

                        def stage1(j):
                            gq = j // 2
                            bk, bk_b = next_bank()
                            S_t, S_b = Ssb[j % NR]
                            e_t, e_b = eb[j % NR]

                            def mm():
                                for i2 in range(2):
                                    i = T.matmul(bk[:, i2 * 256:(i2 + 1) * 256], lhsT=qT[:, j, :],
                                                 rhs=kTb[:, l, i2, gq, :, :].rearrange("p s t -> p (s t)"), start=True, stop=True)
                                return i
                            ctx.op("pe", mm, (qT_b, kTb_b), (bk_b,))
                            for i2 in range(2):
                                ctx.op("dve", lambda: V.scalar_tensor_tensor(out=S_t[:, i2, :], in0=bk[:, i2 * 256:(i2 + 1) * 256], scalar=0.125,
                                                                             in1=mask_ap, op0=ALU.mult, op1=ALU.add),
                                       (bk_b, amask_b, amask1_b), (S_b,))
                            h0 = 2 * j
                            ctx.op("dve", lambda: V.tensor_reduce(out=stat[:, 0, h0:h0 + 2], in_=S_t[:], axis=AX.X, op=ALU.max, negate=True),
                                   (S_b,), (stat_b,))
                            ctx.op("dve", lambda: V.tensor_tensor(out=stat[:, 0, h0:h0 + 2], in0=stat[:, 0, h0:h0 + 2],
                                                                  in1=sinkt[:, 1, l * 16 + h0:l * 16 + h0 + 2], op=ALU.min), (stat_b, sink_b), (stat_b,))
                            for i2 in range(2):
                                ctx.op("act", lambda: A.activation(out=e_t[:, i2, :], in_=S_t[:, i2, :], func=AF.Exp,
                                                                   bias=stat[:, 0, h0 + i2:h0 + i2 + 1], scale=1.0,
                                                                   accum_out=stat[:, 1, h0 + i2:h0 + i2 + 1]), (S_b, stat_b), (e_b, stat_b))

                        def stage2(j):
                            e_t, e_b = e_bufs = eb[j % NR]
                            eT_t, eT_b = eT[j % NR]
                            pt, pt_b = next_pt()

                            def tr():
                                for i2 in range(2):
                                    for sl in range(2):
                                        i = T.transpose(out=pt[:, (i2 * 2 + sl) * P:(i2 * 2 + sl + 1) * P],
                                                        in_=e_t[:, i2, sl * P:(sl + 1) * P], identity=ident[:])
                                return i
                            ctx.op("pe", tr, (e_b, ident_b), (pt_b,))
                            copy_op(evac_engine(), eT_t[:], pt[:, 0:512].rearrange("p (j t) -> p j t", t=P), (pt_b,), (eT_b,))

                        def stage3(j):
                            gq = j // 2
                            eT_t, eT_b = eT[j % NR]
                            for i2 in range(2):
                                h = 2 * j + i2
                                ob, ob_b = obk[h // 8]
                                col = (h % 8) * HD

                                def mm():
                                    T.matmul(ob[:, col:col + HD], lhsT=eT_t[:, i2 * 2 + 0, :], rhs=Vb[:, l, 0, gq, :], start=True, stop=False)
                                    return T.matmul(ob[:, col:col + HD], lhsT=eT_t[:, i2 * 2 + 1, :], rhs=Vb[:, l, 1, gq, :], start=False, stop=True)
                                ctx.op("pe", mm, (eT_b, Vb_b), (ob_b,))

                        for step in range(8 + 2):
                            if step < 8:
                                stage1(step)
                            if 0 <= step - 1 < 8:
                                stage2(step - 1)
                            if 0 <= step - 2 < 8:
                                stage3(step - 2)
                        ctx.op("dve", lambda: V.tensor_tensor(out=stat[:, 2, :], in0=stat[:, 0, :], in1=sinkt[:, 0, l * 16:(l + 1) * 16], op=ALU.add),
                               (stat_b, sink_b), (stat_b,))
                        ctx.op("act", lambda: A.activation(out=stat[:, 3, :], in_=stat[:, 2, :], func=AF.Exp), (stat_b,), (stat_b,))
                        ctx.op("dve", lambda: V.tensor_tensor(out=stat[:, 3, :], in0=stat[:, 3, :], in1=stat[:, 1, :], op=ALU.add), (stat_b,), (stat_b,))
                        ctx.op("dve", lambda: V.reciprocal(out=stat[:, 4, :], in_=stat[:, 3, :]), (stat_b,), (stat_b,))
                        for hb_ in range(2):
                            ob, ob_b = obk[hb_]
                            ctx.op("dve", lambda: V.tensor_tensor(out=atok[:, hb_ * 512:(hb_ + 1) * 512].rearrange("p (h d) -> p h d", d=HD),
                                                                  in0=ob[:].rearrange("p (h d) -> p h d", d=HD),
                                                                  in1=stat[:, 4, hb_ * 8:(hb_ + 1) * 8].unsqueeze(2).to_broadcast([P, 8, HD]),
                                                                  op=ALU.mult), (ob_b, stat_b), (atok_b,))
                        ctx.op("act", lambda: A.activation(out=ajunk[:], in_=atok[:], func=AF.Square, accum_out=ass[:, s:s + 1]),
                               (atok_b,), (ajunk_b, ass_b))
                        rstd_from_ss(ass[:, s:s + 1], ass_b, 1, 1024)
                        scale_op("act", an4[:, s, :], atok[:], ass[:, s:s + 1], (atok_b, ass_b), (an4b[s],))
                    ctx.barrier()
                for j in range(8):
                    pt, pt_b = next_pt()

                    def tr():
                        for s in range(NS):
                            i = T.transpose(out=pt[:, s * P:(s + 1) * P], in_=an4[:, s, j * P:(j + 1) * P], identity=ident[:])
                        return i
                    ctx.op("pe", tr, an4b + [ident_b], (pt_b,))
                    scale_op(evac_engine(), mixT[:, j, :], pt[:, 0:TG], vec[:, vc + 48 + j:vc + 48 + j + 1], (pt_b, vec_b), (hidb[j],))
                ctx.barrier()

        dump("attn", mixT[:, 0:8, :].rearrange("p a b -> p (a b)"), 8 * TG)
        with ExitStack() as st:
            def f32t(name, shape=(P, TG)):
                return sb(name, list(shape), F32, st)
            ff, ff_b = f32t("h_f")
            qs, qs_b = f32t("h_qs")
            sgate, sgate_b = f32t("h_sgate")
            vT, vT_b = sb("h_vT", [P, TG], BF16, st)
            bT, bT_b = f32t("h_b")
            dd_, dd_b = f32t("h_d")
            E1, E1_b = f32t("h_E1")
            kin, kin_b = f32t("h_kin")
            qt, qt_b = sb("h_qt", [P, TG], BF16, st)
            kt, kt_b = sb("h_kt", [P, TG], BF16, st)
            vtok, vtok_b = sb("h_vtok", [P, 4, P], BF16, st)
            ktokA, ktokA_b = sb("h_ktokA", [P, 4, P], BF16, st)
            ktokB, ktokB_b = sb("h_ktokB", [P, 4, P], BF16, st)
            AT, AT_b = sb("h_AT", [P, 4, P], BF16, st)
            T3, T3_b = f32t("h_T3", (P, P, 9))
            D3, D3_b = f32t("h_D3", (P, P, 9))
            S3, S3_b = f32t("h_S3", (P, P, 9))
            Sra, Sra_b = sb("h_Sra", [P, 8, P], BF16, st)
            sq, sq_b = dd_, dd_b
            rs_, rs_b = bT, bT_b
            sc, sc_b = f32t("h_sc", (P, 4, 8))
            rmask, rmask_b = f32t("h_rmask")
            ctx.op("dve", lambda: V.memset(ktokA[:], 0.0), (), (ktokA_b,))
            ctx.op("dve", lambda: V.memset(ktokB[:], 0.0), (), (ktokB_b,))
            ctx.op("dve", lambda: V.memset(D3[:], 0.0), (), (D3_b,))
            ctx.op("dve", lambda: V.memset(rmask[:], 1.0), (), (rmask_b,))
            ctx.op("dve", lambda: V.memset(rmask[:].rearrange("p (c t) -> p c t", t=64)[:, :, 0:1], 0.0), (), (rmask_b,))

            NCG = 2 if state_only else 4
            hinfo = {}
            pend = []
            ptf = [ptb[0][0][:, :].bitcast(F32), ptb[1][0][:, :].bitcast(F32)]

            def inproj_start(hh):
                hinfo[hh] = {"w": w_get("inp" if state_only else "in"), "banks": []}
                pend[:] = [(hh, c) for c in range(NCG)]

            def fill(n):
                for _ in range(n):
                    if not pend:
                        return
                    hh, c = pend.pop(0)
                    wt, wb = hinfo[hh]["w"]
                    bk, bk_b = next_bank()

                    def mm():
                        for kc in range(16):
                            i = T.matmul(bk[:], lhsT=wt[:, kc, c * P:(c + 1) * P], rhs=actT[:, kc, :], start=(kc == 0), stop=(kc == 15))
                        return i
                    ctx.op("pe", mm, actb + [wb], (bk_b,))
                    hinfo[hh]["banks"].append((bk, bk_b))

            def evac(hh):
                banks = hinfo[hh]["banks"]
                if state_only:
                    (bf_, bf_b), (bi_, bi_b) = banks
                else:
                    (bq, bq_b), (bf_, bf_b), (bi_, bi_b), (bg, bg_b) = banks
                ctx.op("act", lambda: A.activation(out=ff[:], in_=bf_[:], func=AF.Sigmoid), (bf_b,), (ff_b,))
                if not state_only:
                    ctx.op("act", lambda: A.activation(out=qs[:], in_=bq[:], func=AF.Silu), (bq_b,), (qs_b,))
                ctx.op("act", lambda: A.copy(out=vT[:], in_=bi_[:]), (bi_b,), (vT_b,))
                if not state_only:
                    ctx.op("act", lambda: A.activation(out=sgate[:], in_=bg[:], func=AF.Silu), (bg_b,), (sgate_b,))

            inproj_start(0)
            fill(NCG)
            evac(0)
            for h in range(NHH):
                if h + 1 < NHH:
                    inproj_start(h + 1)
                    fill(NCG // 2)
                ctx.op("dve", lambda: V.tensor_scalar(out=ff[:], in0=ff[:], scalar1=lbt[:, l, 1, h:h + 1], scalar2=lbt[:, l, 0, h:h + 1],
                                                      op0=ALU.mult, op1=ALU.add), (ff_b, lbt_b), (ff_b,))
                ctx.op("dve", lambda: V.tensor_scalar(out=kin[:], in0=ff[:], scalar1=-1.0, scalar2=1.0, op0=ALU.mult, op1=ALU.add),
                       (ff_b,), (kin_b,))
                ctx.op("act", lambda: A.activation(out=ff[:], in_=ff[:], func=AF.Ln), (ff_b,), (ff_b,))
                ctx.op("dve", lambda: V.tensor_tensor_scan(out=bT[:], data0=rmask[:], data1=ff[:], initial=0.0, op0=ALU.mult, op1=ALU.add),
                       (rmask_b, ff_b), (bT_b,))
                b3 = bT[:].rearrange("p (c t) -> p c t", t=64)
                ctx.op("dve", lambda: V.tensor_tensor(out=dd_[:].rearrange("p (c t) -> p c t", t=64), in0=b3,
                                                      in1=b3[:, :, 31:32].to_broadcast([P, 8, 64]), op=ALU.subtract), (bT_b,), (dd_b,))
                if not state_only:
                    ctx.op("act", lambda: A.activation(out=E1[:], in_=dd_[:], func=AF.Exp), (dd_b,), (E1_b,))
                    ctx.op("dve", lambda: V.tensor_tensor(out=qt[:], in0=qs[:], in1=E1[:], op=ALU.mult), (qs_b, E1_b), (qt_b,))
                ctx.op("act", lambda: A.activation(out=E1[:], in_=dd_[:], func=AF.Exp, scale=-1.0), (dd_b,), (E1_b,))
                ctx.op("dve", lambda: V.tensor_tensor(out=kt[:], in0=kin[:], in1=E1[:], op=ALU.mult), (kin_b, E1_b), (kt_b,))
                ctx.op("act", lambda: A.activation(out=sc[:, 0, :], in_=b3[:, :, 31], func=AF.Exp), (bT_b,), (sc_b,))
                ctx.op("act", lambda: A.activation(out=sc[:, 1, :], in_=b3[:, :, 63], func=AF.Exp), (bT_b,), (sc_b,))
                ctx.op("dve", lambda: V.tensor_tensor(out=sc[:, 3, :], in0=b3[:, :, 63], in1=b3[:, :, 31], op=ALU.subtract), (bT_b,), (sc_b,))
                ctx.op("act", lambda: A.activation(out=sc[:, 2, :], in_=sc[:, 3, :], func=AF.Exp), (sc_b,), (sc_b,))
                ptA, ptA_b = ptb[0]
                ptB, ptB_b = ptb[1]

                def trv():
                    for pc in range(4):
                        i = T.transpose(out=ptA[:, pc * P:(pc + 1) * P], in_=vT[:, pc * P:(pc + 1) * P], identity=ident[:])
                    return i
                ctx.op("pe", trv, (vT_b, ident_b), (ptA_b,))
                copy_op("act", vtok[:], ptA[:, 0:512].rearrange("p (c k) -> p c k", k=P), (ptA_b,), (vtok_b,))

                def trk2():
                    for pc in range(4):
                        i = T.transpose(out=ptB[:, pc * P:(pc + 1) * P], in_=kt[:, pc * P:(pc + 1) * P], identity=ident[:])
                    return i
                ctx.op("pe", trk2, (kt_b, ident_b), (ptB_b,))
                copy_op("act", ktokA[0:64, :, :], ptB[0:64, 0:512].rearrange("p (c k) -> p c k", k=P), (ptB_b,), (ktokA_b,))
                copy_op("dve", ktokB[64:128, :, :], ptB[64:128, 0:512].rearrange("p (c k) -> p c k", k=P), (ptB_b,), (ktokB_b,))
                fill(1 if not state_only else 1)
                for j2, (dbk, dbk_b) in enumerate(((pdS[0][:, :], pdS[1]), (ptf[0], ptA_b))):
                    def mmd():
                        for cq in range(4):
                            c = 4 * j2 + cq
                            pc = c // 2
                            ktk = ktokA if c % 2 == 0 else ktokB
                            i = T.matmul(dbk[:, cq * P:(cq + 1) * P], lhsT=ktk[:, pc, :], rhs=vtok[:, pc, :], start=True, stop=True)
                        return i
                    ctx.op("pe", mmd, (ktokA_b, ktokB_b, vtok_b), (dbk_b,))
                    ctx.op("dve", lambda: V.tensor_tensor(out=T3[:, :, 1 + 4 * j2:5 + 4 * j2].rearrange("p v c -> p c v"),
                                                          in0=dbk.rearrange("p (c v) -> p c v", v=P),
                                                          in1=sc[:, 2, 4 * j2:4 * j2 + 4].unsqueeze(2).to_broadcast([P, 4, P]), op=ALU.mult),
                           (dbk_b, sc_b), (T3_b,))
                Sh = Sst[:, l, h, :]
                ctx.op("act", lambda: A.copy(out=T3[:, :, 0], in_=Sh), (Sst_b,), (T3_b,))
                ctx.op("dve", lambda: V.tensor_copy(out=D3[:, :, 1:9], in_=sc[:, 1, :].unsqueeze(1).to_broadcast([P, P, 8])), (sc_b,), (D3_b,))
                ctx.op("dve", lambda: V.tensor_tensor_scan(out=S3[:].rearrange("p v j -> p (v j)"), data0=D3[:].rearrange("p v j -> p (v j)"),
                                                           data1=T3[:].rearrange("p v j -> p (v j)"), initial=0.0, op0=ALU.mult, op1=ALU.add),
                       (D3_b, T3_b), (S3_b,))
                if not state_only:
                    ctx.op("dve", lambda: V.tensor_tensor(out=Sra[:], in0=S3[:, :, 0:8].rearrange("p v c -> p c v"),
                                                          in1=sc[:, 0, :].unsqueeze(2).to_broadcast([P, 8, P]), op=ALU.mult), (S3_b, sc_b), (Sra_b,))
                ctx.op("act", lambda: A.copy(out=Sh, in_=S3[:, :, 8]), (S3_b,), (Sst_b,))
                fill(NCG)
                if state_only:
                    if h + 1 < NHH:
                        evac(h + 1)
                    continue
                bkA = ptf[1]

                def mmA():
                    for pc in range(4):
                        i = T.matmul(bkA[:, pc * P:(pc + 1) * P], lhsT=kt[:, pc * P:(pc + 1) * P], rhs=qt[:, pc * P:(pc + 1) * P],
                                     start=True, stop=True)
                    return i
                ctx.op("pe", mmA, (kt_b, qt_b), (ptB_b,))
                ctx.op("dve", lambda: V.tensor_tensor(out=AT[:], in0=bkA.rearrange("p (c t) -> p c t", t=P),
                                                      in1=hmask[:].unsqueeze(1).to_broadcast([P, 4, P]), op=ALU.mult),
                       (ptB_b, hmask_b), (AT_b,))
                oT, oT_b = pbank[4]

                def mmo():
                    for pc in range(4):
                        T.matmul(oT[:, pc * P:(pc + 1) * P], lhsT=vtok[:, pc, :], rhs=AT[:, pc, :], start=True, stop=False)
                        for cc in range(2):
                            c = 2 * pc + cc
                            i = T.matmul(oT[:, c * 64:(c + 1) * 64], lhsT=Sra[:, c, :], rhs=qt[:, c * 64:(c + 1) * 64],
                                         start=False, stop=(cc == 1))
                    return i
                ctx.op("pe", mmo, (vtok_b, AT_b, Sra_b, qt_b), (oT_b,))
                ctx.op("act", lambda: A.activation(out=sq[:], in_=oT[:], func=AF.Square), (oT_b,), (sq_b,))
                bk, bk_b = pdS
                ctx.op("pe", lambda: T.matmul(bk[:], lhsT=ones_f[:], rhs=sq[:], start=True, stop=True), (ones_b, sq_b), (bk_b,))
                ctx.op("act", lambda: A.activation(out=rs_[:], in_=bk[:], func=AF.Ln, scale=1.0 / P, bias=eps_ap), (bk_b, smallc_b), (rs_b,))
                ctx.op("act", lambda: A.activation(out=rs_[:], in_=rs_[:], func=AF.Exp, scale=-0.5), (rs_b,), (rs_b,))
                ctx.op("dve", lambda: V.tensor_tensor(out=sq[:], in0=oT[:], in1=rs_[:], op=ALU.mult), (oT_b, rs_b), (sq_b,))
                ctx.op("dve", lambda: V.scalar_tensor_tensor(out=mixT[:, 8 + h, :], in0=sq[:], scalar=vec[:, vc + 56 + h:vc + 56 + h + 1],
                                                             in1=sgate[:], op0=ALU.mult, op1=ALU.mult), (sq_b, vec_b, sgate_b), (hidb[8 + h],))
                if h + 1 < NHH:
                    evac(h + 1)
            ctx.barrier()
        if state_only:
            return
        dump("hgrn", mixT[:, 8:16, :].rearrange("p a b -> p (a b)"), 8 * TG)
        with ExitStack() as st:
            ybuf, _ = sb("ybuf", [P, NS, D], F32, st)
            yb = [Buf(f"y{s}") for s in range(NS)]
            rc = res_begin(pmg_d[l], st)
            for fb in range(4):
                tokmajor_proj("out", mixT, hidb, 16, ybuf, yb, fb, rc)
            res_finish(rc, ybuf, yb)

        dump("xmix", x_res[:].rearrange("p a b -> p (a b)"), 4 * D)
        norm_to_actT(vc + 16)
        with ExitStack() as st:
            sgt = [sb(f"f_sg{i}", [P, TG], F32, st) for i in range(8)]
            for t in range(11):
                wg, wg_b = w_get("gate")
                for c in range(4):
                    bg, bg_b = next_bank()

                    def mmg():
                        for kc in range(16):
                            i = T.matmul(bg[:], lhsT=wg[:, kc, c * P:(c + 1) * P], rhs=actT[:, kc, :], start=(kc == 0), stop=(kc == 15))
                        return i
                    ctx.op("pe", mmg, actb + [wg_b], (bg_b,))
                    s_t, s_b = sgt[(t % 2) * 4 + c]
                    ctx.op("act", lambda: A.activation(out=s_t[:], in_=bg[:], func=AF.Silu), (bg_b,), (s_b,))
                wu, wu_b = w_get("up")
                for c in range(4):
                    bu, bu_b = next_bank()

                    def mmu():
                        for kc in range(16):
                            i = T.matmul(bu[:], lhsT=wu[:, kc, c * P:(c + 1) * P], rhs=actT[:, kc, :], start=(kc == 0), stop=(kc == 15))
                        return i
                    ctx.op("pe", mmu, actb + [wu_b], (bu_b,))
                    s_t, s_b = sgt[(t % 2) * 4 + c]
                    hc = t * 4 + c
                    ctx.op("dve", lambda: V.tensor_tensor(out=R_hid[:, hc, :], in0=s_t[:], in1=bu[:], op=ALU.mult), (s_b, bu_b), (hidb[hc],))
            ctx.barrier()
        with ExitStack() as st:
            ybuf, _ = sb("ybuf2", [P, NS, D], F32, st)
            yb = [Buf(f"y2{s}") for s in range(NS)]
            rc = res_begin(pfg_d[l], st)
            for fb in range(4):
                for kp in range(4):
                    wt, wb = w_get("down")
                    for s in range(NS):
                        bk, bk_b = pbank[s]

                        def mm():
                            for kc in range(11):
                                hc = kp * 11 + kc
                                i = T.matmul(bk[:], lhsT=R_hid[:, hc, s * P:(s + 1) * P], rhs=wt[:, kc, :],
                                             start=(hc == 0), stop=(hc == 43))
                            return i
                        ctx.op("pe", mm, hidb[kp * 11:(kp + 1) * 11] + [wb], (bk_b,))
                for s in range(NS):
                    bk, bk_b = pbank[s]
                    res_evac(rc, bk, bk_b, ybuf, yb, s, fb)
            res_finish(rc, ybuf, yb)

        dump("xffn", x_res[:].rearrange("p a b -> p (a b)"), 4 * D)
        norm_to_actT(vc + 32)
        with ExitStack() as st:
            ctx.dma("pool", pTt[:], pT_src.rearrange("(kc p) t -> p kc t", p=P), psem, reads=(), writes=(pT_b,))
            sgs = [sb(f"p_sg{i}", [P, 512], F32, st) for i in range(2)]
            tm2 = [sb(f"p_tm{i}", [P, 512], F32, st) for i in range(2)]
            k = 0
            for fb in range(4):
                wg, wg_b = w_get("pg")
                wp, wp_b = w_get("pp", held=1)
                for s in range(NS):
                    bg, bg_b = next_bank()
                    bp, bp_b = next_bank()

                    def mmg():
                        for kc in range(16):
                            i = T.matmul(bg[:], lhsT=actT[:, kc, s * P:(s + 1) * P], rhs=wg[:, kc, :], start=(kc == 0), stop=(kc == 15))
                        return i

                    def mmp():
                        for kc in range(2):
                            i = T.matmul(bp[:], lhsT=pTt[:, kc, s * P:(s + 1) * P], rhs=wp[:, kc, :], start=(kc == 0), stop=(kc == 1))
                        return i
                    ctx.op("pe", mmg, actb + [wg_b], (bg_b,))
                    ctx.op("pe", mmp, (pT_b, wp_b), (bp_b,))
                    s_t, s_b = sgs[k % 2]
                    t_t, t_b = tm2[k % 2]
                    k += 1
                    ctx.op("act", lambda: A.activation(out=s_t[:], in_=bg[:], func=AF.Sigmoid), (bg_b,), (s_b,))
                    ctx.op("dve", lambda: V.tensor_tensor(out=t_t[:], in0=s_t[:], in1=bp[:], op=ALU.mult), (s_b, bp_b), (t_b,))
                    xs = x_res[:, s, fb * 512:(fb + 1) * 512]
                    ctx.op("dve", lambda: V.tensor_tensor(out=xs, in0=xs, in1=t_t[:], op=ALU.add), (xb[s], t_b), (xb[s],))
            ctx.barrier()


    def main_body(NG):
        for g in range(NG):
            ctx.dma("sp", x_res[:], x_d[g * TG:(g + 1) * TG, :].rearrange("(s p) d -> p s d", p=P), xsem, reads=(), writes=tuple(xb))
            rope_tables(posf[:, g * NS:(g + 1) * NS])
            for l in range(NL):
                layer_body(l, amask[:, 2, :] if g == 0 else None, pT_d[l, :, g * TG:(g + 1) * TG])
            ctx.dma("sp", y_d[g * TG:(g + 1) * TG, :].rearrange("(s p) d -> p s d", p=P), x_res[:], ysem, reads=tuple(xb), writes=())

    def main_split():
        x1b = [Buf(f"x1s{g}") for g in range(NG)]
        ssem = DSem(nc, "d_s")
        xsem2 = DSem(nc, "d_x2")
        grp = lambda ap, g: ap[g * TG:(g + 1) * TG, :].rearrange("(s p) d -> p s d", p=P)
        for g in range(NG):
            ctx.dma("sp", x_res[:], grp(x_d, g), xsem, reads=(), writes=tuple(xb))
            rope_tables(posf[:, g * NS:(g + 1) * NS])
            layer_body(0, amask[:, 2, :] if g == 0 else None, pT_d[0, :, g * TG:(g + 1) * TG])
            ctx.dma("sp", grp(x1s_d, g), x_res[:], ssem, reads=tuple(xb), writes=(x1b[g],))
        for g in range(NGH):
            ctx.dma("sp", x_res[:], grp(x1s_d, g), xsem, reads=(x1b[g],), writes=tuple(xb))
            if g == NGH - 1:
                rope_tables(posf[:, g * NS:(g + 1) * NS])
            layer_body(1, None, None, state_only=True, kv_tail=(g == NGH - 1))
        ctx.op("dve", lambda: V.tensor_scalar(out=Sst[:, 1, :, :], in0=Sst[:, 1, :, :], scalar1=flg[:, 1:2], scalar2=None, op0=ALU.mult),
               (Sst_b, flg_b), (Sst_b,))
        ctx.op("dve", lambda: V.tensor_scalar(out=kTb[:, 1, :, :, 1, :], in0=kTb[:, 1, :, :, 1, :], scalar1=flg[:, 1:2], scalar2=None, op0=ALU.mult),
               (kTb_b, flg_b), (kTb_b,))
        ctx.op("dve", lambda: V.tensor_scalar(out=Vb[:, 1, 1, :, :], in0=Vb[:, 1, 1, :, :], scalar1=flg[:, 1:2], scalar2=None, op0=ALU.mult),
               (Vb_b, flg_b), (Vb_b,))
        for j in range(NGH):
            with ExitStack() as st:
                xtmp, xtmp_b = sb("xtmp", [P, NS, D], F32, st)
                ctx.dma("sp", x_res[:], grp(x1s_d, j), xsem, reads=(x1b[j],), writes=tuple(xb))
                for e2 in ("pe", "act", "dve"):
                    ctx._wait("sp", (ctx.sem[e2], ctx.cnt[e2], e2))
                ctx.dma("sp", xtmp[:], grp(x1s_d, NGH + j), xsem2, reads=(x1b[NGH + j],), writes=(xtmp_b,))
                for s in range(NS):
                    ctx.op("dve", lambda: V.tensor_scalar(out=x_res[:, s, :], in0=x_res[:, s, :], scalar1=flg[:, 0:1], scalar2=None, op0=ALU.mult),
                           (xb[s], flg_b), (xb[s],))
                    ctx.op("dve", lambda: V.scalar_tensor_tensor(out=x_res[:, s, :], in0=xtmp[:, s, :], scalar=flg[:, 1:2], in1=x_res[:, s, :],
                                                                 op0=ALU.mult, op1=ALU.add), (xtmp_b, flg_b, xb[s]), (xb[s],))
                ctx.barrier()
            rope_tables(posf1[:, j * NS:(j + 1) * NS])
            layer_body(1, amask1[:] if j == 0 else None, pT1_d[:, j * TG:(j + 1) * TG])
            ctx.dma("sp", grp(y_d, j), x_res[:], ysem, reads=tuple(xb), writes=())
        ctx._wait("sp", (ssem.sem, ssem.cnt, ssem.key))

    try:
      if split:
          main_split()
      else:
          main_body(NG)
    except _Stop:
        top.close()
        return nc
    ctx._wait("sp", (ysem.sem, ysem.cnt, ysem.key))
    assert wst["consumed"] == len(wplan), (wst, len(wplan))
    top.close()
    print(f"[build] ops={ctx.nops} waits={ctx.nwaits} weight_tiles={len(wplan)}")
    return nc


def _fm(v):
    v = np.asarray(v, np.float32)
    return np.ascontiguousarray(v.reshape(-1, P).T)


def _host_consts():
    half = 32
    invf = (10000.0 ** (-np.arange(half, dtype=np.float32) / half)).astype(np.float32)
    tq = np.arange(P)[:, None]
    tk = np.arange(P)[None, :]
    cur = np.where(tk <= tq, 0.0, MASKV).astype(np.float32)
    prev = np.where(tk > tq, 0.0, MASKV).astype(np.float32)
    dead = np.full((P, P), MASKV, np.float32)
    am = np.stack([
        np.concatenate([cur, prev], 1),
        np.concatenate([prev, cur], 1),
        np.concatenate([cur, dead], 1),
        np.concatenate([dead, cur], 1),
    ]).astype(np.float32)
    s = np.arange(P)[:, None]
    t = np.arange(P)[None, :]
    hm = ((s // 64 == t // 64) & (s <= t)).astype(np.float32)
    return invf, am, hm


def _prep_shared(inp, NL=2):
    w_in = np.asarray(inp["w_in"], np.float32)
    cols = list(range(1536))
    for h in range(NHH):
        for sec in range(4):
            base = 1536 + sec * 1024 + h * P
            cols.extend(range(base, base + P))
    w_in_p = np.ascontiguousarray(w_in[:, :, cols])
    vec = np.zeros((P, 2 * NVEC), np.float32)
    for l in range(2):
        b = l * NVEC
        vec[:, b + 0:b + 16] = _fm(inp["pre_mix_gain"][l])
        vec[:, b + 16:b + 32] = _fm(inp["pre_ffn_gain"][l])
        vec[:, b + 32:b + 48] = _fm(inp["ple_gain"][l])
        vec[:, b + 48:b + 56] = _fm(inp["attn_out_gain"][l])
        vec[:, b + 56:b + 64] = _fm(inp["hgrn_out_gain"][l])
        vec[:, b + 64:b + 72] = _fm(inp["hgrn_lb_logits"][l])
    invf, am, hm = _host_consts()
    f = lambda k: np.ascontiguousarray(np.asarray(inp[k], np.float32))
    return {
        "w_in": w_in_p, "w_out": f("w_out"), "w_gate": f("w_ffn_gate"), "w_up": f("w_ffn_up"), "w_down": f("w_ffn_down"),
        "w_pg": f("w_ple_gate"), "w_pp": f("w_ple_proj"), "vec_fm": vec,
        "post_mix_gain": f("post_mix_gain"), "post_ffn_gain": f("post_ffn_gain"),
        "sinks": np.ascontiguousarray(np.asarray(inp["attn_sinks"], np.float32).reshape(32)),
        "invf": invf, "amask": am, "hmask": hm,
    }


def _prep_core(inp, b, NTOK):
    x = np.ascontiguousarray(np.asarray(inp["x"], np.float32)[b, :NTOK])
    pT = np.ascontiguousarray(np.transpose(np.asarray(inp["p"], np.float32)[:, b, :NTOK, :], (0, 2, 1)))
    pos = np.asarray(inp["positions"])[b, :NTOK].astype(np.int32)
    posT = np.ascontiguousarray(pos.reshape(-1, P).T)
    return {"x": x, "pT": pT, "posT": posT}


def _prep_core_split(inp, b, half, S):
    H = S // 2
    x = np.ascontiguousarray(np.asarray(inp["x"], np.float32)[b])
    p = np.asarray(inp["p"], np.float32)
    pT = np.ascontiguousarray(np.transpose(p[:, b], (0, 2, 1)))
    pT1 = np.ascontiguousarray(pT[1][:, half * H:(half + 1) * H])
    pos = np.asarray(inp["positions"])[b].astype(np.int32)
    posT = np.ascontiguousarray(pos.reshape(-1, P).T)
    posT1 = np.ascontiguousarray(pos[half * H:(half + 1) * H].reshape(-1, P).T)
    _, am, _ = _host_consts()
    flags = np.array([1.0, 0.0] if half == 0 else [0.0, 1.0], np.float32)
    amask1 = np.ascontiguousarray(am[2] if half == 0 else am[0])
    return {"x": x, "pT": pT, "posT": posT, "pT1": pT1, "posT1": posT1, "flags": flags, "amask1": amask1}


def kernel(**inputs):
    B, S = 4, 4096
    nc = build(S, 2, split=True)
    shared = _prep_shared(inputs)
    in_maps = []
    for c in range(2 * B):
        m = dict(shared)
        m.update(_prep_core_split(inputs, c // 2, c % 2, S))
        in_maps.append(m)
    res = run_bass_kernel_spmd(nc, in_maps, core_ids=list(range(2 * B)))
    H = S // 2
    out = np.empty((B, S, D), np.float32)
    for c in range(2 * B):
        out[c // 2, (c % 2) * H:(c % 2 + 1) * H] = np.asarray(res.results[c]["y"], np.float32)
    return out
```

```python
import math
from contextlib import ExitStack

import numpy as np
import concourse.bass as bass
import concourse.mybir as mybir
from concourse.bass_utils import run_bass_kernel_spmd

F32 = mybir.dt.float32
BF16 = mybir.dt.bfloat16
I32 = mybir.dt.int32
AF = mybir.ActivationFunctionType
ALU = mybir.AluOpType
AX = mybir.AxisListType

P = 128
D = 2048
TG = 512
NS = 4
DFF = 5632
DIN = 5632
NQH = 16
NKV = 4
HD = 64
NHH = 8
EPS = 1e-6
MASKV = -30000.0
NVEC = 72


class Buf:
    __slots__ = ("name", "w", "r")

    def __init__(self, name):
        self.name = name
        self.w = None
        self.r = {}


class DSem:
    ALL = []

    def __init__(self, nc, name):
        self.sem = nc.alloc_semaphore(name)
        self.cnt = 0
        self.key = name
        DSem.ALL.append(self)


class Ctx:
    def __init__(self, nc):
        self.nc = nc
        self.E = {"pe": nc.tensor, "act": nc.scalar, "dve": nc.vector, "pool": nc.gpsimd, "sp": nc.sync}
        self.sem = {}
        self.cnt = {}
        for e in self.E:
            self.sem[e] = nc.alloc_semaphore("s_" + e)
            self.cnt[e] = 0
        self.waited = {}
        self.nwaits = 0
        self.nops = 0

    def _wait(self, e, sig):
        if sig is None:
            return
        sem, val, key = sig
        k = (e, key)
        if self.waited.get(k, 0) >= val:
            return
        self.E[e].wait_ge(sem, val)
        self.waited[k] = val
        self.nwaits += 1

    def _pre(self, e, reads, writes):
        for b in reads:
            if not (e == "pe" and b.w is not None and b.w[2] == "pe"):
                self._wait(e, b.w)
        for b in writes:
            if not (e == "pe" and b.w is not None and b.w[2] == "pe"):
                self._wait(e, b.w)
            for rk, rs in b.r.items():
                if not (e == "pe" and rk == "pe"):
                    self._wait(e, rs)

    def _post(self, sig, reads, writes):
        for b in reads:
            b.r[sig[2]] = sig
        for b in writes:
            b.w = sig
            b.r = {}

    def op(self, e, fn, reads=(), writes=()):
        self._pre(e, reads, writes)
        inst = fn()
        self.cnt[e] += 1
        inst.then_inc(self.sem[e], 1)
        sig = (self.sem[e], self.cnt[e], e)
        self._post(sig, reads, writes)
        self.nops += 1
        return sig

    def dma(self, q, out, in_, dsem, reads=(), writes=()):
        self._pre(q, reads, writes)
        inst = self.E[q].dma_start(out=out, in_=in_)
        dsem.cnt += 16
        inst.then_inc(dsem.sem, 16)
        sig = (dsem.sem, dsem.cnt, dsem.key)
        self._post(sig, reads, writes)
        return sig

    def barrier(self, engines=("pe", "act", "dve")):
        for e in engines:
            for e2 in engines:
                if self.cnt[e2] > 0:
                    self._wait(e, (self.sem[e2], self.cnt[e2], e2))


class _Stop(Exception):
    pass


def build(NTOK, NL, dbg=None, split=False):
    NG = NTOK // TG
    NB = NTOK // P
    NGH = NG // 2
    NTH = NTOK // 2
    nc = bass.Bass("TRN2", target_bir_lowering=False)
    DSem.ALL = []
    V, A, T, G = nc.vector, nc.scalar, nc.tensor, nc.gpsimd

    def din(name, shape, dt=F32):
        return nc.dram_tensor(name, shape, dt, kind="ExternalInput").ap()

    x_d = din("x", [NTOK, D])
    pT_d = din("pT", [2, 256, NTOK])
    pos_d = din("posT", [P, NB], I32)
    w_in_d = din("w_in", [2, D, DIN])
    w_out_d = din("w_out", [2, D, D])
    w_g_d = din("w_gate", [2, D, DFF])
    w_u_d = din("w_up", [2, D, DFF])
    w_d_d = din("w_down", [2, DFF, D])
    w_pg_d = din("w_pg", [2, D, D])
    w_pp_d = din("w_pp", [2, 256, D])
    vec_d = din("vec_fm", [P, 2 * NVEC])
    pmg_d = din("post_mix_gain", [2, D])
    pfg_d = din("post_ffn_gain", [2, D])
    sink_d = din("sinks", [32])
    invf_d = din("invf", [32])
    amask_d = din("amask", [4, P, 256])
    hmask_d = din("hmask", [P, P])
    if split:
        pT1_d = din("pT1", [256, NTH])
        pos1_d = din("posT1", [P, NB // 2], I32)
        flags_d = din("flags", [2])
        amask1_d = din("amask1", [P, 256])
        x1s_d = nc.dram_tensor("x1s", [NTOK, D], F32).ap()
        y_d = nc.dram_tensor("y", [NTH, D], F32, kind="ExternalOutput").ap()
    else:
        y_d = nc.dram_tensor("y", [NTOK, D], F32, kind="ExternalOutput").ap()
    dbg_d = None
    if dbg is not None:
        dbg_d = nc.dram_tensor("dbg", [P, 16 * TG], F32, kind="ExternalOutput").ap()

    ctx = Ctx(nc)
    top = ExitStack()
    dsem_dbg = DSem(nc, "d_dbg")

    def dump(stage, ap2d, ncols):
        if dbg is None or dbg != stage:
            return
        ctx.barrier(("pe", "act", "dve", "pool"))
        nc.gpsimd.dma_start(out=dbg_d[:, 0:ncols], in_=ap2d).then_inc(dsem_dbg.sem, 16)
        dsem_dbg.cnt = 16
        for d_ in DSem.ALL:
            if d_.cnt > 0:
                nc.gpsimd.wait_ge(d_.sem, d_.cnt)
        raise _Stop()

    uniq = {"n": 0}

    def sb(name, shape, dt, st=top):
        uniq["n"] += 1
        t = st.enter_context(nc.sbuf_tensor(f"sb_{name}_{uniq['n']}", list(shape), dt))
        return t, Buf(name)

    def ps(name, shape, dt):
        t = top.enter_context(nc.psum_tensor("ps_" + name, list(shape), dt))
        return t, Buf(name)

    x_res, _ = sb("x_res", [P, NS, D], F32)
    xb = [Buf(f"x{s}") for s in range(NS)]
    actT, _ = sb("actT", [P, 16, TG], BF16)
    actb = [Buf(f"actT{k}") for k in range(16)]
    R_hid, _ = sb("hidT", [P, 44, TG], BF16)
    hidb = [Buf(f"hid{k}") for k in range(44)]
    mixT = R_hid
    WS = 3
    wslots = [sb(f"wslot{i}", [P, 16, 512], BF16) for i in range(WS)]
    wsems = [DSem(nc, f"d_w{i}") for i in range(WS)]
    Sst, Sst_b = sb("Sst", [P, 2, NHH, P], F32)
    kTb, kTb_b = sb("kTb", [P, 2, 2, NKV, 2, P], BF16)
    Vb, Vb_b = sb("Vb", [P, 2, 2, NKV, HD], BF16)
    cs_t, cs_b = sb("cs", [P, 2, NS, 32], F32)
    amask, amask_b = sb("amask", [P, 4, 256], F32)
    amask1_b = Buf("amask1")
    if split:
        amask1, amask1_b = sb("amask1", [P, 256], F32)
        flg, flg_b = sb("flg", [P, 2], F32)
        posi1, posi1_b = sb("posi1", [P, NB // 2], I32)
        posf1, posf1_b = sb("posf1", [P, NB // 2], F32)
    hmask, hmask_b = sb("hmask", [P, P], F32)
    vec, vec_b = sb("vec", [P, 2 * NVEC], F32)
    ident, ident_b = sb("ident", [P, P], BF16)
    ones_f, ones_b = sb("ones_f", [P, P], F32)
    sinkt, sink_b = sb("sinkt", [P, 2, 32], F32)
    invf, invf_b = sb("invf", [P, 32], F32)
    posi, posi_b = sb("posi", [P, NB], I32)
    posf, posf_b = sb("posf", [P, NB], F32)
    lbt, lbt_b = sb("lbt", [P, 2, 2, NHH], F32)
    smallc, smallc_b = sb("smallc", [P, 8], F32)
    pTt, pT_b = sb("pTt", [P, 2, TG], BF16)

    pbank = [ps(f"pb{i}", [P, 512], F32) for i in range(5)]
    pdS = ps("pdS", [P, 512], F32)
    ptb = [ps(f"ptb{i}", [P, 1024], BF16) for i in range(2)]
    rot = {"pb": 0, "pt": 0}

    def next_bank():
        i = rot["pb"]
        rot["pb"] = (i + 1) % 4
        return pbank[i]

    def next_pt():
        i = rot["pt"]
        rot["pt"] = (i + 1) % 2
        return ptb[i]

    flip = {"e": 0}

    def evac_engine():
        flip["e"] ^= 1
        return "act" if flip["e"] else "dve"

    def copy_op(e, out, in_, reads, writes):
        if e == "act":
            return ctx.op("act", lambda: A.copy(out=out, in_=in_), reads, writes)
        return ctx.op("dve", lambda: V.tensor_copy(out=out, in_=in_), reads, writes)

    def scale_op(e, out, in_, sc_ap, reads, writes):
        if e == "act":
            return ctx.op("act", lambda: A.activation(out=out, in_=in_, func=AF.Identity, scale=sc_ap), reads, writes)
        return ctx.op("dve", lambda: V.tensor_scalar(out=out, in0=in_, scalar1=sc_ap, scalar2=None, op0=ALU.mult), reads, writes)

    def setup_load(name, out, in_, buf, q="sp"):
        ctx.dma(q, out, in_, DSem(nc, "d_" + name), reads=(), writes=(buf,))

    setup_load("vec", vec[:], vec_d[:, :], vec_b)
    setup_load("amask", amask[:], amask_d.rearrange("v p k -> p v k"), amask_b)
    setup_load("hmask", hmask[:], hmask_d[:, :], hmask_b)
    setup_load("sink", sinkt[:, 0, :], sink_d.partition_broadcast(P), sink_b)
    setup_load("invf", invf[:], invf_d.partition_broadcast(P), invf_b)
    setup_load("pos", posi[:], pos_d[:, :], posi_b)
    if split:
        setup_load("amask1", amask1[:], amask1_d[:, :], amask1_b)
        setup_load("flg", flg[:], flags_d.partition_broadcast(P), flg_b)
        setup_load("pos1", posi1[:], pos1_d[:, :], posi1_b)
        ctx.op("dve", lambda: V.tensor_copy(out=posf1[:], in_=posi1[:]), (posi1_b,), (posf1_b,))

    ctx.op("dve", lambda: V.memset(ones_f[:], 1.0), (), (ones_b,))
    ctx.op("dve", lambda: V.memset(ident[:], 1.0), (), (ident_b,))
    ctx.op("pool", lambda: G.affine_select(out=ident[:], in_=ident[:], pattern=[[-1, P]], compare_op=ALU.is_equal,
                                           fill=0.0, base=0, channel_multiplier=1), (ident_b,), (ident_b,))
    ctx.op("dve", lambda: V.memset(Sst[:], 0.0), (), (Sst_b,))
    ctx.op("dve", lambda: V.memset(kTb[:], 0.0), (), (kTb_b,))
    ctx.op("dve", lambda: V.memset(Vb[:], 0.0), (), (Vb_b,))
    ctx.op("dve", lambda: V.tensor_copy(out=posf[:], in_=posi[:]), (posi_b,), (posf_b,))
    ctx.op("dve", lambda: V.tensor_scalar(out=sinkt[:, 1, :], in0=sinkt[:, 0, :], scalar1=-1.0, scalar2=None, op0=ALU.mult),
           (sink_b,), (sink_b,))
    LB0 = 64
    l0 = vec[:, LB0:LB0 + 8]
    l1 = vec[:, NVEC + LB0:NVEC + LB0 + 8]
    with ExitStack() as st:
        tm, tm_b = sb("lb_m", [P, 8], F32, st)
        e0, e0_b = sb("lb_e0", [P, 8], F32, st)
        e1, e1_b = sb("lb_e1", [P, 8], F32, st)
        ctx.op("dve", lambda: V.tensor_tensor(out=tm[:], in0=l0, in1=l1, op=ALU.max), (vec_b,), (tm_b,))
        ctx.op("dve", lambda: V.tensor_tensor(out=e0[:], in0=l0, in1=tm[:], op=ALU.subtract), (vec_b, tm_b), (e0_b,))
        ctx.op("dve", lambda: V.tensor_tensor(out=e1[:], in0=l1, in1=tm[:], op=ALU.subtract), (vec_b, tm_b), (e1_b,))
        ctx.op("act", lambda: A.activation(out=e0[:], in_=e0[:], func=AF.Exp), (e0_b,), (e0_b,))
        ctx.op("act", lambda: A.activation(out=e1[:], in_=e1[:], func=AF.Exp), (e1_b,), (e1_b,))
        ctx.op("dve", lambda: V.tensor_tensor(out=tm[:], in0=e0[:], in1=e1[:], op=ALU.add), (e0_b, e1_b), (tm_b,))
        ctx.op("dve", lambda: V.reciprocal(out=tm[:], in_=tm[:]), (tm_b,), (tm_b,))
        ctx.op("dve", lambda: V.memset(lbt[:, 0, 0, :], 0.0), (), (lbt_b,))
        ctx.op("dve", lambda: V.tensor_tensor(out=lbt[:, 1, 0, :], in0=e1[:], in1=tm[:], op=ALU.mult), (e1_b, tm_b), (lbt_b,))
        ctx.op("dve", lambda: V.tensor_scalar(out=lbt[:, :, 1, :], in0=lbt[:, :, 0, :], scalar1=-1.0, scalar2=1.0,
                                              op0=ALU.mult, op1=ALU.add), (lbt_b,), (lbt_b,))
        ctx.barrier()

    def layer_tiles(l):
        for t in range(11):
            yield ("in", w_in_d[l, :, t * 512:(t + 1) * 512].rearrange("(kc p) n -> p kc n", p=P), 16, 512, (l, "in", t))
        for fb in range(4):
            yield ("out", w_out_d[l, :, fb * 512:(fb + 1) * 512].rearrange("(kc p) n -> p kc n", p=P), 16, 512, (l, "out", fb))
        for t in range(11):
            yield ("gate", w_g_d[l, :, t * 512:(t + 1) * 512].rearrange("(kc p) n -> p kc n", p=P), 16, 512, (l, "gate", t))
            yield ("up", w_u_d[l, :, t * 512:(t + 1) * 512].rearrange("(kc p) n -> p kc n", p=P), 16, 512, (l, "up", t))
        for fb in range(4):
            for kp in range(4):
                yield ("down", w_d_d[l, kp * 1408:(kp + 1) * 1408, fb * 512:(fb + 1) * 512]
                       .rearrange("(kc p) n -> p kc n", p=P), 11, 512, (l, "down", fb * 4 + kp))
        for fb in range(4):
            yield ("pg", w_pg_d[l, :, fb * 512:(fb + 1) * 512].rearrange("(kc p) n -> p kc n", p=P), 16, 512, (l, "pg", fb))
            yield ("pp", w_pp_d[l, :, fb * 512:(fb + 1) * 512].rearrange("(kc p) n -> p kc n", p=P), 2, 512, (l, "pp", fb))

    def weight_plan():
        if not split:
            for g in range(NG):
                for l in range(NL):
                    yield from layer_tiles(l)
            return
        for g in range(NG):
            yield from layer_tiles(0)
        for g in range(NGH):
            if g == NGH - 1:
                yield ("in", w_in_d[1, :, 1024:1536].rearrange("(kc p) n -> p kc n", p=P), 16, 512, (1, "in", 2))
            for h in range(NHH):
                c0 = 1536 + h * 512 + 128
                yield ("inp", w_in_d[1, :, c0:c0 + 256].rearrange("(kc p) n -> p kc n", p=P), 16, 256, (1, "inp", h))
        for j in range(NGH):
            yield from layer_tiles(1)

    wplan = list(weight_plan())
    wst = {"issued": 0, "consumed": 0}
    tile_ids = {}
    first_use = []
    for e_ in wplan:
        first_use.append(e_[4] not in tile_ids)
        tile_ids.setdefault(e_[4], len(tile_ids))
    n_reuse = sum(1 for f_ in first_use if not f_)
    use_cache = n_reuse > 0
    if use_cache:
        TPS = 64
        wscr_ds = [nc.dram_tensor(f"wscr{k}", [TPS, P, 16 * 512], BF16).ap() for k in range((len(tile_ids) + TPS - 1) // TPS)]
        scr_b = [Buf(f"wscr{i}") for i in range(len(tile_ids))]
        wbsems = [DSem(nc, f"d_wb{i}") for i in range(WS)]
        wsems_hw = [DSem(nc, f"d_wh{i}") for i in range(WS)]

    def w_issue(upto):
        while wst["issued"] < min(upto, len(wplan)):
            i = wst["issued"]
            tag, src, kc, ncol, key = wplan[i]
            t, b = wslots[i % WS]
            if not use_cache:
                ctx.dma("pool", t[:, 0:kc, 0:ncol], src, wsems[i % WS], reads=(), writes=(b,))
            else:
                tid = tile_ids[key]
                scr = wscr_ds[tid // TPS][tid % TPS, :, 0:kc * ncol].rearrange("p (k n) -> p k n", n=ncol)
                if first_use[i]:
                    ctx.dma("pool", t[:, 0:kc, 0:ncol], src, wsems[i % WS], reads=(), writes=(b,))
                    ctx.dma("sp", scr, t[:, 0:kc, 0:ncol], wbsems[i % WS], reads=(b,), writes=(scr_b[tid],))
                else:
                    ctx.dma("sp", t[:, 0:kc, 0:ncol], scr, wsems_hw[i % WS], reads=(scr_b[tid],), writes=(b,))
            wst["issued"] += 1

    def w_get(tag, held=0):
        i = wst["consumed"]
        assert wplan[i][0] == tag, (wplan[i][0], tag)
        w_issue(i + WS - held)
        wst["consumed"] += 1
        return wslots[i % WS]

    xsem = DSem(nc, "d_x")
    ysem = DSem(nc, "d_y")
    psem = DSem(nc, "d_p")
    gsem = DSem(nc, "d_g")

    def rstd_from_ss(ss, ss_b, n, width):
        ctx.op("act", lambda: A.activation(out=ss[:, 0:n], in_=ss[:, 0:n], func=AF.Ln, scale=1.0 / width, bias=eps_ap),
               (ss_b, smallc_b), (ss_b,))
        ctx.op("act", lambda: A.activation(out=ss[:, 0:n], in_=ss[:, 0:n], func=AF.Exp, scale=-0.5), (ss_b,), (ss_b,))

    ctx.op("dve", lambda: V.memset(smallc[:, 0:1], EPS), (), (smallc_b,))
    eps_ap = smallc[:, 0:1]

    def norm_to_actT(gcol):
        with ExitStack() as st:
            hb4, hb4_b = sb("hb4", [P, NS, D], BF16, st)
            ss, ss_b = sb("nss", [P, NS], F32, st)
            hbs = [Buf(f"hb{s}") for s in range(NS)]
            for s in range(NS):
                ctx.op("act", lambda: A.activation(out=hb4[:, s, :], in_=x_res[:, s, :], func=AF.Square,
                                                   accum_out=ss[:, s:s + 1]), (xb[s],), (hbs[s], ss_b))
            rstd_from_ss(ss, ss_b, NS, D)
            for s in range(NS):
                scale_op("dve" if s % 2 else "act", hb4[:, s, :], x_res[:, s, :], ss[:, s:s + 1], (xb[s], ss_b), (hbs[s],))
            for kc in range(16):
                pt, pt_b = next_pt()

                def tr():
                    for s in range(NS):
                        i = T.transpose(out=pt[:, s * P:(s + 1) * P], in_=hb4[:, s, kc * P:(kc + 1) * P], identity=ident[:])
                    return i
                ctx.op("pe", tr, hbs + [ident_b], (pt_b,))
                scale_op(evac_engine(), actT[:, kc, :], pt[:, 0:TG], vec[:, gcol + kc:gcol + kc + 1], (pt_b, vec_b), (actb[kc],))
            ctx.barrier()

    def res_begin(gain_d_row, st):
        rc = {}
        rc["junk"] = actT[:, 0, :]
        rc["junk_bs"] = (actb[0],)
        rc["gbc"] = actT[:, 4:12, :].rearrange("p a b -> p (a b)").bitcast(F32)
        rc["gbc_bs"] = tuple(actb[4:12])
        rc["ssp"], rc["ssp_b"] = sb("rssp", [P, NS, 4], F32, st)
        rc["ss"], rc["ss_b"] = sb("rss", [P, NS], F32, st)
        ctx.dma("sp", rc["gbc"], gain_d_row.partition_broadcast(P), gsem, reads=(), writes=rc["gbc_bs"])
        return rc

    def res_evac(rc, bk, bk_b, ybuf, yb, s, fb):
        ctx.op("act", lambda: A.activation(out=rc["junk"], in_=bk[:], func=AF.Square, accum_out=rc["ssp"][:, s, fb:fb + 1]),
               (bk_b,), rc["junk_bs"] + (rc["ssp_b"],))
        ctx.op("dve", lambda: V.tensor_tensor(out=ybuf[:, s, fb * 512:(fb + 1) * 512], in0=bk[:], in1=rc["gbc"][:, fb * 512:(fb + 1) * 512],
                                              op=ALU.mult), (bk_b,) + rc["gbc_bs"] + rc["junk_bs"], (yb[s],))

    def res_finish(rc, ybuf, yb):
        ss, ss_b = rc["ss"], rc["ss_b"]
        ctx.op("dve", lambda: V.tensor_reduce(out=ss[:], in_=rc["ssp"][:], axis=AX.X, op=ALU.add), (rc["ssp_b"],), (ss_b,))
        rstd_from_ss(ss, ss_b, NS, D)
        for s in range(NS):
            ctx.op("dve", lambda: V.scalar_tensor_tensor(out=x_res[:, s, :], in0=ybuf[:, s, :], scalar=ss[:, s:s + 1], in1=x_res[:, s, :],
                                                         op0=ALU.mult, op1=ALU.add), (yb[s], ss_b, xb[s]), (xb[s],))
        ctx.barrier()

    def tokmajor_proj(tag, srcT, srcb, nk, ybuf, yb, fb, rc):
        wt, wb = w_get(tag)
        for s in range(NS):
            bk, bk_b = next_bank()

            def mm():
                for kc in range(nk):
                    i = T.matmul(bk[:], lhsT=srcT[:, kc, s * P:(s + 1) * P], rhs=wt[:, kc, :], start=(kc == 0), stop=(kc == nk - 1))
                return i
            ctx.op("pe", mm, list(srcb[0:nk]) + [wb], (bk_b,))
            res_evac(rc, bk, bk_b, ybuf, yb, s, fb)

    def rope_tables(pos_ap):
        with ExitStack() as st:
            ang, ang_b = sb("ang", [P, NS, 32], F32, st)
            kk, kk_b = sb("kk", [P, NS, 32], F32, st)
            ki, ki_b = sb("ki", [P, NS, 32], I32, st)
            yy, yy_b = sb("yy", [P, NS, 32], F32, st)
            mk, mk_b = sb("mk", [P, NS, 32], F32, st)
            C1 = 6.28125
            C2 = 2.0 * math.pi - C1
            ctx.op("dve", lambda: V.tensor_tensor(out=ang[:], in0=invf[:].unsqueeze(1).to_broadcast([P, NS, 32]),
                                                  in1=pos_ap.unsqueeze(2).to_broadcast([P, NS, 32]),
                                                  op=ALU.mult), (invf_b, posf_b) + ((posf1_b,) if split else ()), (ang_b,))
            for which, shift in ((1, 0.0), (0, math.pi / 2)):
                ctx.op("dve", lambda: V.tensor_scalar(out=kk[:], in0=ang[:], scalar1=shift, scalar2=1.0 / (2 * math.pi),
                                                      op0=ALU.add, op1=ALU.mult), (ang_b,), (kk_b,))
                ctx.op("dve", lambda: V.tensor_copy(out=ki[:], in_=kk[:]), (kk_b,), (ki_b,))
                ctx.op("dve", lambda: V.tensor_copy(out=kk[:], in_=ki[:]), (ki_b,), (kk_b,))
                ctx.op("dve", lambda: V.scalar_tensor_tensor(out=yy[:], in0=kk[:], scalar=-C1, in1=ang[:], op0=ALU.mult, op1=ALU.add),
                       (kk_b, ang_b), (yy_b,))
                ctx.op("dve", lambda: V.scalar_tensor_tensor(out=yy[:], in0=kk[:], scalar=-C2, in1=yy[:], op0=ALU.mult, op1=ALU.add),
                       (kk_b, yy_b), (yy_b,))
                if shift != 0.0:
                    ctx.op("dve", lambda: V.tensor_scalar(out=yy[:], in0=yy[:], scalar1=shift, scalar2=None, op0=ALU.add), (yy_b,), (yy_b,))
                ctx.op("dve", lambda: V.tensor_scalar(out=mk[:], in0=yy[:], scalar1=math.pi, scalar2=-2 * math.pi, op0=ALU.is_gt, op1=ALU.mult),
                       (yy_b,), (mk_b,))
                ctx.op("dve", lambda: V.tensor_tensor(out=yy[:], in0=yy[:], in1=mk[:], op=ALU.add), (yy_b, mk_b), (yy_b,))
                ctx.op("dve", lambda: V.tensor_scalar(out=mk[:], in0=yy[:], scalar1=-math.pi, scalar2=2 * math.pi, op0=ALU.is_lt, op1=ALU.mult),
                       (yy_b,), (mk_b,))
                ctx.op("dve", lambda: V.tensor_tensor(out=yy[:], in0=yy[:], in1=mk[:], op=ALU.add), (yy_b, mk_b), (yy_b,))
                ctx.op("dve", lambda: V.tensor_scalar(out=yy[:], in0=yy[:], scalar1=3.1415925, scalar2=-3.1415925, op0=ALU.min, op1=ALU.max),
                       (yy_b,), (yy_b,))
                ctx.op("act", lambda: A.activation(out=cs_t[:, which, :, :], in_=yy[:], func=AF.Sin), (yy_b,), (cs_b,))
            ctx.barrier()


    def layer_body(l, first_mask, pT_src, state_only=False, kv_tail=False):
        vc = l * NVEC

        norm_to_actT(vc + 0)
        dump("actT", actT[:].rearrange("p a b -> p (a b)"), 16 * TG)
        if state_only and kv_tail:
            with ExitStack() as st:
                kvt, kvt_b = sb("kvt", [P, 512], F32, st)
                t1, t1_b = sb("kt1", [P, NKV, 32], F32, st)
                t2, t2_b = sb("kt2", [P, NKV, 32], F32, st)
                kdup, kdup_b = sb("kdup2", [P, NKV, 2, HD], BF16, st)
                wt, wb = w_get("in")
                bk, bk_b = next_bank()

                def mmkv():
                    for kc in range(16):
                        i = T.matmul(bk[:], lhsT=actT[:, kc, 3 * P:4 * P], rhs=wt[:, kc, :], start=(kc == 0), stop=(kc == 15))
                    return i
                ctx.op("pe", mmkv, actb + [wb], (bk_b,))
                copy_op("act", kvt[:], bk[:], (bk_b,), (kvt_b,))
                cosb = cs_t[:, 0, 3, :].unsqueeze(1).to_broadcast([P, NKV, 32])
                sinb = cs_t[:, 1, 3, :].unsqueeze(1).to_broadcast([P, NKV, 32])
                kk4 = kvt[:, 0:256].rearrange("p (h d) -> p h d", d=HD)
                x1 = kk4[:, :, 0:32]
                x2 = kk4[:, :, 32:64]
                ctx.op("dve", lambda: V.tensor_tensor(out=t1[:], in0=x1, in1=cosb, op=ALU.mult), (kvt_b, cs_b), (t1_b,))
                ctx.op("dve", lambda: V.tensor_tensor(out=t2[:], in0=x2, in1=sinb, op=ALU.mult), (kvt_b, cs_b), (t2_b,))
                for dd in range(2):
                    ctx.op("dve", lambda: V.tensor_tensor(out=kdup[:, :, dd, 0:32], in0=t1[:], in1=t2[:], op=ALU.subtract), (t1_b, t2_b), (kdup_b,))
                ctx.op("dve", lambda: V.tensor_tensor(out=t1[:], in0=x2, in1=cosb, op=ALU.mult), (kvt_b, cs_b), (t1_b,))
                ctx.op("dve", lambda: V.tensor_tensor(out=t2[:], in0=x1, in1=sinb, op=ALU.mult), (kvt_b, cs_b), (t2_b,))
                for dd in range(2):
                    ctx.op("dve", lambda: V.tensor_tensor(out=kdup[:, :, dd, 32:64], in0=t1[:], in1=t2[:], op=ALU.add), (t1_b, t2_b), (kdup_b,))
                ctx.op("act", lambda: A.copy(out=Vb[:, l, 1, :, :], in_=kvt[:, 256:512].rearrange("p (g d) -> p g d", d=HD)), (kvt_b,), (Vb_b,))
                pt, pt_b = next_pt()
                kdf = kdup[:].rearrange("p g t d -> p g (t d)")

                def trk():
                    for gg in range(NKV):
                        i = T.transpose(out=pt[:, gg * P:(gg + 1) * P], in_=kdf[:, gg, :], identity=ident[:])
                    return i
                ctx.op("pe", trk, (kdup_b, ident_b), (pt_b,))
                copy_op("act", kTb[0:64, l, 0, :, 1, :], pt[0:64, 0:512].rearrange("p (j t) -> p j t", t=P), (pt_b,), (kTb_b,))
                copy_op("dve", kTb[64:128, l, 1, :, 1, :], pt[64:128, 0:512].rearrange("p (j t) -> p j t", t=P), (pt_b,), (kTb_b,))
                ctx.barrier()
        if not state_only:
            with ExitStack() as st:
                qkv = R_hid[:, 16:40, :].rearrange("p a b -> p (a b)").bitcast(F32).rearrange("p (s c) -> p s c", c=1536)
                qkvb = [Buf(f"qkv{s}") for s in range(NS)]
                an4, _ = sb("an4", [P, NS, 1024], BF16, st)
                an4b = [Buf(f"an4_{s}") for s in range(NS)]
                for t in range(3):
                    wt, wb = w_get("in")
                    for s in range(NS):
                        bk, bk_b = next_bank()

                        def mm():
                            for kc in range(16):
                                i = T.matmul(bk[:], lhsT=actT[:, kc, s * P:(s + 1) * P], rhs=wt[:, kc, :], start=(kc == 0), stop=(kc == 15))
                            return i
                        ctx.op("pe", mm, actb + [wb], (bk_b,))
                        copy_op(evac_engine(), qkv[:, s, t * 512:(t + 1) * 512], bk[:], (bk_b,), (qkvb[s],))

                with ExitStack() as st2:
                    t1, t1_b = sb("rt1", [P, 20, 32], F32, st2)
                    t2, t2_b = sb("rt2", [P, 20, 32], F32, st2)
                    qr, qr_b = sb("qr", [P, NQH, HD], BF16, st2)
                    kdup, kdup_b = sb("kdup", [P, NKV, 2, HD], BF16, st2)
                    qT, qT_b = sb("qT", [P, 8, P], BF16, st2)
                    NR = 3
                    Ssb = [sb(f"Ssb{i}", [P, 2, 256], F32, st2) for i in range(NR)]
                    eb = [sb(f"eb{i}", [P, 2, 256], BF16, st2) for i in range(NR)]
                    eT = [sb(f"eT{i}", [P, 4, P], BF16, st2) for i in range(NR)]
                    stat, stat_b = sb("stat", [P, 8, NQH], F32, st2)
                    atok, atok_b = sb("atok", [P, 1024], F32, st2)
                    ajunk, ajunk_b = sb("ajunk", [P, 1024], BF16, st2)
                    ass, ass_b = sb("ass", [P, NS], F32, st2)

                    for s in range(NS):
                        slot = s % 2
                        mask_ap = first_mask if (s == 0 and first_mask is not None) else amask[:, slot, :]
                        cosb = cs_t[:, 0, s, :].unsqueeze(1).to_broadcast([P, 20, 32])
                        sinb = cs_t[:, 1, s, :].unsqueeze(1).to_broadcast([P, 20, 32])
                        qk = qkv[:, s, 0:1280].rearrange("p (h d) -> p h d", d=HD)
                        x1 = qk[:, :, 0:32]
                        x2 = qk[:, :, 32:64]
                        ctx.op("dve", lambda: V.tensor_tensor(out=t1[:], in0=x1, in1=cosb, op=ALU.mult), (qkvb[s], cs_b), (t1_b,))
                        ctx.op("dve", lambda: V.tensor_tensor(out=t2[:], in0=x2, in1=sinb, op=ALU.mult), (qkvb[s], cs_b), (t2_b,))
                        ctx.op("dve", lambda: V.tensor_tensor(out=qr[:, :, 0:32], in0=t1[:, 0:16, :], in1=t2[:, 0:16, :], op=ALU.subtract),
                               (t1_b, t2_b), (qr_b,))
                        for dd in range(2):
                            ctx.op("dve", lambda: V.tensor_tensor(out=kdup[:, :, dd, 0:32], in0=t1[:, 16:20, :], in1=t2[:, 16:20, :], op=ALU.subtract),
                                   (t1_b, t2_b), (kdup_b,))
                        ctx.op("dve", lambda: V.tensor_tensor(out=t1[:], in0=x2, in1=cosb, op=ALU.mult), (qkvb[s], cs_b), (t1_b,))
                        ctx.op("dve", lambda: V.tensor_tensor(out=t2[:], in0=x1, in1=sinb, op=ALU.mult), (qkvb[s], cs_b), (t2_b,))
                        ctx.op("dve", lambda: V.tensor_tensor(out=qr[:, :, 32:64], in0=t1[:, 0:16, :], in1=t2[:, 0:16, :], op=ALU.add),
                               (t1_b, t2_b), (qr_b,))
                        for dd in range(2):
                            ctx.op("dve", lambda: V.tensor_tensor(out=kdup[:, :, dd, 32:64], in0=t1[:, 16:20, :], in1=t2[:, 16:20, :], op=ALU.add),
                                   (t1_b, t2_b), (kdup_b,))
                        ctx.op("act", lambda: A.copy(out=Vb[:, l, slot, :, :], in_=qkv[:, s, 1280:1536].rearrange("p (g d) -> p g d", d=HD)),
                               (qkvb[s],), (Vb_b,))
                        qrf = qr[:].rearrange("p h d -> p (h d)")
                        for half in range(2):
                            pt, pt_b = next_pt()

                            def trq():
                                for j in range(4):
                                    jj = half * 4 + j
                                    i = T.transpose(out=pt[:, j * P:(j + 1) * P], in_=qrf[:, jj * P:(jj + 1) * P], identity=ident[:])
                                return i
                            ctx.op("pe", trq, (qr_b, ident_b), (pt_b,))
                            copy_op(evac_engine(), qT[:, half * 4:(half + 1) * 4, :], pt[:, 0:512].rearrange("p (j t) -> p j t", t=P),
                                    (pt_b,), (qT_b,))
                        pt, pt_b = next_pt()
                        kdf = kdup[:].rearrange("p g t d -> p g (t d)")

                        def trk():
                            for gg in range(NKV):
                                i = T.transpose(out=pt[:, gg * P:(gg + 1) * P], in_=kdf[:, gg, :], identity=ident[:])
                            return i
                        ctx.op("pe", trk, (kdup_b, ident_b), (pt_b,))
                        copy_op("act", kTb[0:64, l, 0, :, slot, :], pt[0:64, 0:512].rearrange("p (j t) -> p j t", t=P), (pt_b,), (kTb_b,))
                        copy_op("dve", kTb[64:128, l, 1, :, slot, :], pt[64:128, 0:512].rearrange("p (j t) -> p j t", t=P), (pt_b,), (kTb_b,))

                        obk = [pbank[4], pdS]

                        def stage1(j):
                            gq = j // 2
                            bk, bk_b = next_bank()
                            S_t, S_b = Ssb[j % NR]
                            e_t, e_b = eb[j % NR]

                            def mm():
                                for i2 in range(2):
                                    i = T.matmul(bk[:, i2 * 256:(i2 + 1) * 256], lhsT=qT[:, j, :],
                                                 rhs=kTb[:, l, i2, gq, :, :].rearrange("p s t -> p (s t)"), start=True, stop=True)
                                return i
                            ctx.op("pe", mm, (qT_b, kTb_b), (bk_b,))
                            for i2 in range(2):
                                ctx.op("dve", lambda: V.scalar_tensor_tensor(out=S_t[:, i2, :], in0=bk[:, i2 * 256:(i2 + 1) * 256], scalar=0.125,
                                                                             in1=mask_ap, op0=ALU.mult, op1=ALU.add),
                                       (bk_b, amask_b, amask1_b), (S_b,))
                            h0 = 2 * j
                            ctx.op("dve", lambda: V.tensor_reduce(out=stat[:, 0, h0:h0 + 2], in_=S_t[:], axis=AX.X, op=ALU.max, negate=True),
                                   (S_b,), (stat_b,))
                            ctx.op("dve", lambda: V.tensor_tensor(out=stat[:, 0, h0:h0 + 2], in0=stat[:, 0, h0:h0 + 2],
                                                                  in1=sinkt[:, 1, l * 16 + h0:l * 16 + h0 + 2], op=ALU.min), (stat_b, sink_b), (stat_b,))
                            for i2 in range(2):
                                ctx.op("act", lambda: A.activation(out=e_t[:, i2, :], in_=S_t[:, i2, :], func=AF.Exp,
                                                                   bias=stat[:, 0, h0 + i2:h0 + i2 + 1], scale=1.0,
                                                                   accum_out=stat[:, 1, h0 + i2:h0 + i2 + 1]), (S_b, stat_b), (e_b, stat_b))

                        def stage2(j):
                            e_t, e_b = e_bufs = eb[j % NR]
                            eT_t, eT_b = eT[j % NR]
                            pt, pt_b = next_pt()

                            def tr():
                                for i2 in range(2):
                                    for sl in range(2):
                                        i = T.transpose(out=pt[:, (i2 * 2 + sl) * P:(i2 * 2 + sl + 1) * P],
                                                        in_=e_t[:, i2, sl * P:(sl + 1) * P], identity=ident[:])
                                return i
                            ctx.op("pe", tr, (e_b, ident_b), (pt_b,))
                            copy_op(evac_engine(), eT_t[:], pt[:, 0:512].rearrange("p (j t) -> p j t", t=P), (pt_b,), (eT_b,))

                        def stage3(j):
                            gq = j // 2
                            eT_t, eT_b = eT[j % NR]
                            for i2 in range(2):
                                h = 2 * j + i2
                                ob, ob_b = obk[h // 8]
                                col = (h % 8) * HD

                                def mm():
                                    T.matmul(ob[:, col:col + HD], lhsT=eT_t[:, i2 * 2 + 0, :], rhs=Vb[:, l, 0, gq, :], start=True, stop=False)
                                    return T.matmul(ob[:, col:col + HD], lhsT=eT_t[:, i2 * 2 + 1, :], rhs=Vb[:, l, 1, gq, :], start=False, stop=True)
                                ctx.op("pe", mm, (eT_b, Vb_b), (ob_b,))

                        for step in range(8 + 2):
                            if step < 8:
                                stage1(step)
                            if 0 <= step - 1 < 8:
                                stage2(step - 1)
                            if 0 <= step - 2 < 8:
                                stage3(step - 2)
                        ctx.op("dve", lambda: V.tensor_tensor(out=stat[:, 2, :], in0=stat[:, 0, :], in1=sinkt[:, 0, l * 16:(l + 1) * 16], op=ALU.add),
                               (stat_b, sink_b), (stat_b,))
                        ctx.op("act", lambda: A.activation(out=stat[:, 3, :], in_=stat[:, 2, :], func=AF.Exp), (stat_b,), (stat_b,))
                        ctx.op("dve", lambda: V.tensor_tensor(out=stat[:, 3, :], in0=stat[:, 3, :], in1=stat[:, 1, :], op=ALU.add), (stat_b,), (stat_b,))
                        ctx.op("dve", lambda: V.reciprocal(out=stat[:, 4, :], in_=stat[:, 3, :]), (stat_b,), (stat_b,))
                        for hb_ in range(2):
                            ob, ob_b = obk[hb_]
                            ctx.op("dve", lambda: V.tensor_tensor(out=atok[:, hb_ * 512:(hb_ + 1) * 512].rearrange("p (h d) -> p h d", d=HD),
                                                                  in0=ob[:].rearrange("p (h d) -> p h d", d=HD),
                                                                  in1=stat[:, 4, hb_ * 8:(hb_ + 1) * 8].unsqueeze(2).to_broadcast([P, 8, HD]),
                                                                  op=ALU.mult), (ob_b, stat_b), (atok_b,))
                        ctx.op("act", lambda: A.activation(out=ajunk[:], in_=atok[:], func=AF.Square, accum_out=ass[:, s:s + 1]),
                               (atok_b,), (ajunk_b, ass_b))
                        rstd_from_ss(ass[:, s:s + 1], ass_b, 1, 1024)
                        scale_op("act", an4[:, s, :], atok[:], ass[:, s:s + 1], (atok_b, ass_b), (an4b[s],))
                    ctx.barrier()
                for j in range(8):
                    pt, pt_b = next_pt()

                    def tr():
                        for s in range(NS):
                            i = T.transpose(out=pt[:, s * P:(s + 1) * P], in_=an4[:, s, j * P:(j + 1) * P], identity=ident[:])
                        return i
                    ctx.op("pe", tr, an4b + [ident_b], (pt_b,))
                    scale_op(evac_engine(), mixT[:, j, :], pt[:, 0:TG], vec[:, vc + 48 + j:vc + 48 + j + 1], (pt_b, vec_b), (hidb[j],))
                ctx.barrier()

        dump("attn", mixT[:, 0:8, :].rearrange("p a b -> p (a b)"), 8 * TG)
        with ExitStack() as st:
            def f32t(name, shape=(P, TG)):
                return sb(name, list(shape), F32, st)
            ff, ff_b = f32t("h_f")
            qs, qs_b = f32t("h_qs")
            sgate, sgate_b = f32t("h_sgate")
            vT, vT_b = sb("h_vT", [P, TG], BF16, st)
            bT, bT_b = f32t("h_b")
            dd_, dd_b = f32t("h_d")
            E1, E1_b = f32t("h_E1")
            kin, kin_b = f32t("h_kin")
            qt, qt_b = sb("h_qt", [P, TG], BF16, st)
            kt, kt_b = sb("h_kt", [P, TG], BF16, st)
            vtok, vtok_b = sb("h_vtok", [P, 4, P], BF16, st)
            ktokA, ktokA_b = sb("h_ktokA", [P, 4, P], BF16, st)
            ktokB, ktokB_b = sb("h_ktokB", [P, 4, P], BF16, st)
            AT, AT_b = sb("h_AT", [P, 4, P], BF16, st)
            T3, T3_b = f32t("h_T3", (P, P, 9))
            D3, D3_b = f32t("h_D3", (P, P, 9))
            S3, S3_b = f32t("h_S3", (P, P, 9))
            Sra, Sra_b = sb("h_Sra", [P, 8, P], BF16, st)
            sq, sq_b = dd_, dd_b
            rs_, rs_b = bT, bT_b
            sc, sc_b = f32t("h_sc", (P, 4, 8))
            rmask, rmask_b = f32t("h_rmask")
            ctx.op("dve", lambda: V.memset(ktokA[:], 0.0), (), (ktokA_b,))
            ctx.op("dve", lambda: V.memset(ktokB[:], 0.0), (), (ktokB_b,))
            ctx.op("dve", lambda: V.memset(D3[:], 0.0), (), (D3_b,))
            ctx.op("dve", lambda: V.memset(rmask[:], 1.0), (), (rmask_b,))
            ctx.op("dve", lambda: V.memset(rmask[:].rearrange("p (c t) -> p c t", t=64)[:, :, 0:1], 0.0), (), (rmask_b,))

            NCG = 2 if state_only else 4
            hinfo = {}
            pend = []
            ptf = [ptb[0][0][:, :].bitcast(F32), ptb[1][0][:, :].bitcast(F32)]

            def inproj_start(hh):
                hinfo[hh] = {"w": w_get("inp" if state_only else "in"), "banks": []}
                pend[:] = [(hh, c) for c in range(NCG)]

            def fill(n):
                for _ in range(n):
                    if not pend:
                        return
                    hh, c = pend.pop(0)
                    wt, wb = hinfo[hh]["w"]
                    bk, bk_b = next_bank()

                    def mm():
                        for kc in range(16):
                            i = T.matmul(bk[:], lhsT=wt[:, kc, c * P:(c + 1) * P], rhs=actT[:, kc, :], start=(kc == 0), stop=(kc == 15))
                        return i
                    ctx.op("pe", mm, actb + [wb], (bk_b,))
                    hinfo[hh]["banks"].append((bk, bk_b))

            def evac(hh):
                banks = hinfo[hh]["banks"]
                if state_only:
                    (bf_, bf_b), (bi_, bi_b) = banks
                else:
                    (bq, bq_b), (bf_, bf_b), (bi_, bi_b), (bg, bg_b) = banks
                ctx.op("act", lambda: A.activation(out=ff[:], in_=bf_[:], func=AF.Sigmoid), (bf_b,), (ff_b,))
                if not state_only:
                    ctx.op("act", lambda: A.activation(out=qs[:], in_=bq[:], func=AF.Silu), (bq_b,), (qs_b,))
                ctx.op("act", lambda: A.copy(out=vT[:], in_=bi_[:]), (bi_b,), (vT_b,))
                if not state_only:
                    ctx.op("act", lambda: A.activation(out=sgate[:], in_=bg[:], func=AF.Silu), (bg_b,), (sgate_b,))

            inproj_start(0)
            fill(NCG)
            evac(0)
            for h in range(NHH):
                if h + 1 < NHH:
                    inproj_start(h + 1)
                    fill(NCG // 2)
                ctx.op("dve", lambda: V.tensor_scalar(out=ff[:], in0=ff[:], scalar1=lbt[:, l, 1, h:h + 1], scalar2=lbt[:, l, 0, h:h + 1],
                                                      op0=ALU.mult, op1=ALU.add), (ff_b, lbt_b), (ff_b,))
                ctx.op("dve", lambda: V.tensor_scalar(out=kin[:], in0=ff[:], scalar1=-1.0, scalar2=1.0, op0=ALU.mult, op1=ALU.add),
                       (ff_b,), (kin_b,))
                ctx.op("act", lambda: A.activation(out=ff[:], in_=ff[:], func=AF.Ln), (ff_b,), (ff_b,))
                ctx.op("dve", lambda: V.tensor_tensor_scan(out=bT[:], data0=rmask[:], data1=ff[:], initial=0.0, op0=ALU.mult, op1=ALU.add),
                       (rmask_b, ff_b), (bT_b,))
                b3 = bT[:].rearrange("p (c t) -> p c t", t=64)
                ctx.op("dve", lambda: V.tensor_tensor(out=dd_[:].rearrange("p (c t) -> p c t", t=64), in0=b3,
                                                      in1=b3[:, :, 31:32].to_broadcast([P, 8, 64]), op=ALU.subtract), (bT_b,), (dd_b,))
                if not state_only:
                    ctx.op("act", lambda: A.activation(out=E1[:], in_=dd_[:], func=AF.Exp), (dd_b,), (E1_b,))
                    ctx.op("dve", lambda: V.tensor_tensor(out=qt[:], in0=qs[:], in1=E1[:], op=ALU.mult), (qs_b, E1_b), (qt_b,))
                ctx.op("act", lambda: A.activation(out=E1[:], in_=dd_[:], func=AF.Exp, scale=-1.0), (dd_b,), (E1_b,))
                ctx.op("dve", lambda: V.tensor_tensor(out=kt[:], in0=kin[:], in1=E1[:], op=ALU.mult), (kin_b, E1_b), (kt_b,))
                ctx.op("act", lambda: A.activation(out=sc[:, 0, :], in_=b3[:, :, 31], func=AF.Exp), (bT_b,), (sc_b,))
                ctx.op("act", lambda: A.activation(out=sc[:, 1, :], in_=b3[:, :, 63], func=AF.Exp), (bT_b,), (sc_b,))
                ctx.op("dve", lambda: V.tensor_tensor(out=sc[:, 3, :], in0=b3[:, :, 63], in1=b3[:, :, 31], op=ALU.subtract), (bT_b,), (sc_b,))
                ctx.op("act", lambda: A.activation(out=sc[:, 2, :], in_=sc[:, 3, :], func=AF.Exp), (sc_b,), (sc_b,))
                ptA, ptA_b = ptb[0]
                ptB, ptB_b = ptb[1]

                def trv():
                    for pc in range(4):
                        i = T.transpose(out=ptA[:, pc * P:(pc + 1) * P], in_=vT[:, pc * P:(pc + 1) * P], identity=ident[:])
                    return i
                ctx.op("pe", trv, (vT_b, ident_b), (ptA_b,))
                copy_op("act", vtok[:], ptA[:, 0:512].rearrange("p (c k) -> p c k", k=P), (ptA_b,), (vtok_b,))

                def trk2():
                    for pc in range(4):
                        i = T.transpose(out=ptB[:, pc * P:(pc + 1) * P], in_=kt[:, pc * P:(pc + 1) * P], identity=ident[:])
                    return i
                ctx.op("pe", trk2, (kt_b, ident_b), (ptB_b,))
                copy_op("act", ktokA[0:64, :, :], ptB[0:64, 0:512].rearrange("p (c k) -> p c k", k=P), (ptB_b,), (ktokA_b,))
                copy_op("dve", ktokB[64:128, :, :], ptB[64:128, 0:512].rearrange("p (c k) -> p c k", k=P), (ptB_b,), (ktokB_b,))
                fill(1 if not state_only else 1)
                for j2, (dbk, dbk_b) in enumerate(((pdS[0][:, :], pdS[1]), (ptf[0], ptA_b))):
                    def mmd():
                        for cq in range(4):
                            c = 4 * j2 + cq
                            pc = c // 2
                            ktk = ktokA if c % 2 == 0 else ktokB
                            i = T.matmul(dbk[:, cq * P:(cq + 1) * P], lhsT=ktk[:, pc, :], rhs=vtok[:, pc, :], start=True, stop=True)
                        return i
                    ctx.op("pe", mmd, (ktokA_b, ktokB_b, vtok_b), (dbk_b,))
                    ctx.op("dve", lambda: V.tensor_tensor(out=T3[:, :, 1 + 4 * j2:5 + 4 * j2].rearrange("p v c -> p c v"),
                                                          in0=dbk.rearrange("p (c v) -> p c v", v=P),
                                                          in1=sc[:, 2, 4 * j2:4 * j2 + 4].unsqueeze(2).to_broadcast([P, 4, P]), op=ALU.mult),
                           (dbk_b, sc_b), (T3_b,))
                Sh = Sst[:, l, h, :]
                ctx.op("act", lambda: A.copy(out=T3[:, :, 0], in_=Sh), (Sst_b,), (T3_b,))
                ctx.op("dve", lambda: V.tensor_copy(out=D3[:, :, 1:9], in_=sc[:, 1, :].unsqueeze(1).to_broadcast([P, P, 8])), (sc_b,), (D3_b,))
                ctx.op("dve", lambda: V.tensor_tensor_scan(out=S3[:].rearrange("p v j -> p (v j)"), data0=D3[:].rearrange("p v j -> p (v j)"),
                                                           data1=T3[:].rearrange("p v j -> p (v j)"), initial=0.0, op0=ALU.mult, op1=ALU.add),
                       (D3_b, T3_b), (S3_b,))
                if not state_only:
                    ctx.op("dve", lambda: V.tensor_tensor(out=Sra[:], in0=S3[:, :, 0:8].rearrange("p v c -> p c v"),
                                                          in1=sc[:, 0, :].unsqueeze(2).to_broadcast([P, 8, P]), op=ALU.mult), (S3_b, sc_b), (Sra_b,))
                ctx.op("act", lambda: A.copy(out=Sh, in_=S3[:, :, 8]), (S3_b,), (Sst_b,))
                fill(NCG)
                if state_only:
                    if h + 1 < NHH:
                        evac(h + 1)
                    continue
                bkA = ptf[1]

                def mmA():
                    for pc in range(4):
                        i = T.matmul(bkA[:, pc * P:(pc + 1) * P], lhsT=kt[:, pc * P:(pc + 1) * P], rhs=qt[:, pc * P:(pc + 1) * P],
                                     start=True, stop=True)
                    return i
                ctx.op("pe", mmA, (kt_b, qt_b), (ptB_b,))
                ctx.op("dve", lambda: V.tensor_tensor(out=AT[:], in0=bkA.rearrange("p (c t) -> p c t", t=P),
                                                      in1=hmask[:].unsqueeze(1).to_broadcast([P, 4, P]), op=ALU.mult),
                       (ptB_b, hmask_b), (AT_b,))
                oT, oT_b = pbank[4]

                def mmo():
                    for pc in range(4):
                        T.matmul(oT[:, pc * P:(pc + 1) * P], lhsT=vtok[:, pc, :], rhs=AT[:, pc, :], start=True, stop=False)
                        for cc in range(2):
                            c = 2 * pc + cc
                            i = T.matmul(oT[:, c * 64:(c + 1) * 64], lhsT=Sra[:, c, :], rhs=qt[:, c * 64:(c + 1) * 64],
                                         start=False, stop=(cc == 1))
                    return i
                ctx.op("pe", mmo, (vtok_b, AT_b, Sra_b, qt_b), (oT_b,))
                ctx.op("act", lambda: A.activation(out=sq[:], in_=oT[:], func=AF.Square), (oT_b,), (sq_b,))
                bk, bk_b = pdS
                ctx.op("pe", lambda: T.matmul(bk[:], lhsT=ones_f[:], rhs=sq[:], start=True, stop=True), (ones_b, sq_b), (bk_b,))
                ctx.op("act", lambda: A.activation(out=rs_[:], in_=bk[:], func=AF.Ln, scale=1.0 / P, bias=eps_ap), (bk_b, smallc_b), (rs_b,))
                ctx.op("act", lambda: A.activation(out=rs_[:], in_=rs_[:], func=AF.Exp, scale=-0.5), (rs_b,), (rs_b,))
                ctx.op("dve", lambda: V.tensor_tensor(out=sq[:], in0=oT[:], in1=rs_[:], op=ALU.mult), (oT_b, rs_b), (sq_b,))
                ctx.op("dve", lambda: V.scalar_tensor_tensor(out=mixT[:, 8 + h, :], in0=sq[:], scalar=vec[:, vc + 56 + h:vc + 56 + h + 1],
                                                             in1=sgate[:], op0=ALU.mult, op1=ALU.mult), (sq_b, vec_b, sgate_b), (hidb[8 + h],))
                if h + 1 < NHH:
                    evac(h + 1)
            ctx.barrier()
        if state_only:
            return
        dump("hgrn", mixT[:, 8:16, :].rearrange("p a b -> p (a b)"), 8 * TG)
        with ExitStack() as st:
            ybuf, _ = sb("ybuf", [P, NS, D], F32, st)
            yb = [Buf(f"y{s}") for s in range(NS)]
            rc = res_begin(pmg_d[l], st)
            for fb in range(4):
                tokmajor_proj("out", mixT, hidb, 16, ybuf, yb, fb, rc)
            res_finish(rc, ybuf, yb)

        dump("xmix", x_res[:].rearrange("p a b -> p (a b)"), 4 * D)
        norm_to_actT(vc + 16)
        with ExitStack() as st:
            sgt = [sb(f"f_sg{i}", [P, TG], F32, st) for i in range(8)]
            for t in range(11):
                wg, wg_b = w_get("gate")
                for c in range(4):
                    bg, bg_b = next_bank()

                    def mmg():
                        for kc in range(16):
                            i = T.matmul(bg[:], lhsT=wg[:, kc, c * P:(c + 1) * P], rhs=actT[:, kc, :], start=(kc == 0), stop=(kc == 15))
                        return i
                    ctx.op("pe", mmg, actb + [wg_b], (bg_b,))
                    s_t, s_b = sgt[(t % 2) * 4 + c]
                    ctx.op("act", lambda: A.activation(out=s_t[:], in_=bg[:], func=AF.Silu), (bg_b,), (s_b,))
                wu, wu_b = w_get("up")
                for c in range(4):
                    bu, bu_b = next_bank()

                    def mmu():
                        for kc in range(16):
                            i = T.matmul(bu[:], lhsT=wu[:, kc, c * P:(c + 1) * P], rhs=actT[:, kc, :], start=(kc == 0), stop=(kc == 15))
                        return i
                    ctx.op("pe", mmu, actb + [wu_b], (bu_b,))
                    s_t, s_b = sgt[(t % 2) * 4 + c]
                    hc = t * 4 + c
                    ctx.op("dve", lambda: V.tensor_tensor(out=R_hid[:, hc, :], in0=s_t[:], in1=bu[:], op=ALU.mult), (s_b, bu_b), (hidb[hc],))
            ctx.barrier()
        with ExitStack() as st:
            ybuf, _ = sb("ybuf2", [P, NS, D], F32, st)
            yb = [Buf(f"y2{s}") for s in range(NS)]
            rc = res_begin(pfg_d[l], st)
            for fb in range(4):
                for kp in range(4):
                    wt, wb = w_get("down")
                    for s in range(NS):
                        bk, bk_b = pbank[s]

                        def mm():
                            for kc in range(11):
                                hc = kp * 11 + kc
                                i = T.matmul(bk[:], lhsT=R_hid[:, hc, s * P:(s + 1) * P], rhs=wt[:, kc, :],
                                             start=(hc == 0), stop=(hc == 43))
                            return i
                        ctx.op("pe", mm, hidb[kp * 11:(kp + 1) * 11] + [wb], (bk_b,))
                for s in range(NS):
                    bk, bk_b = pbank[s]
                    res_evac(rc, bk, bk_b, ybuf, yb, s, fb)
            res_finish(rc, ybuf, yb)

        dump("xffn", x_res[:].rearrange("p a b -> p (a b)"), 4 * D)
        norm_to_actT(vc + 32)
        with ExitStack() as st:
            ctx.dma("pool", pTt[:], pT_src.rearrange("(kc p) t -> p kc t", p=P), psem, reads=(), writes=(pT_b,))
            sgs = [sb(f"p_sg{i}", [P, 512], F32, st) for i in range(2)]
            tm2 = [sb(f"p_tm{i}", [P, 512], F32, st) for i in range(2)]
            k = 0
            for fb in range(4):
                wg, wg_b = w_get("pg")
                wp, wp_b = w_get("pp", held=1)
                for s in range(NS):
                    bg, bg_b = next_bank()
                    bp, bp_b = next_bank()

                    def mmg():
                        for kc in range(16):
                            i = T.matmul(bg[:], lhsT=actT[:, kc, s * P:(s + 1) * P], rhs=wg[:, kc, :], start=(kc == 0), stop=(kc == 15))
                        return i

                    def mmp():
                        for kc in range(2):
                            i = T.matmul(bp[:], lhsT=pTt[:, kc, s * P:(s + 1) * P], rhs=wp[:, kc, :], start=(kc == 0), stop=(kc == 1))
                        return i
                    ctx.op("pe", mmg, actb + [wg_b], (bg_b,))
                    ctx.op("pe", mmp, (pT_b, wp_b), (bp_b,))
                    s_t, s_b = sgs[k % 2]
                    t_t, t_b = tm2[k % 2]
                    k += 1
                    ctx.op("act", lambda: A.activation(out=s_t[:], in_=bg[:], func=AF.Sigmoid), (bg_b,), (s_b,))
                    ctx.op("dve", lambda: V.tensor_tensor(out=t_t[:], in0=s_t[:], in1=bp[:], op=ALU.mult), (s_b, bp_b), (t_b,))
                    xs = x_res[:, s, fb * 512:(fb + 1) * 512]
                    ctx.op("dve", lambda: V.tensor_tensor(out=xs, in0=xs, in1=t_t[:], op=ALU.add), (xb[s], t_b), (xb[s],))
            ctx.barrier()


    def main_body(NG):
        for g in range(NG):
            ctx.dma("sp", x_res[:], x_d[g * TG:(g + 1) * TG, :].rearrange("(s p) d -> p s d", p=P), xsem, reads=(), writes=tuple(xb))
            rope_tables(posf[:, g * NS:(g + 1) * NS])
            for l in range(NL):
                layer_body(l, amask[:, 2, :] if g == 0 else None, pT_d[l, :, g * TG:(g + 1) * TG])
            ctx.dma("sp", y_d[g * TG:(g + 1) * TG, :].rearrange("(s p) d -> p s d", p=P), x_res[:], ysem, reads=tuple(xb), writes=())

    def main_split():
        x1b = [Buf(f"x1s{g}") for g in range(NG)]
        ssem = DSem(nc, "d_s")
        xsem2 = DSem(nc, "d_x2")
        grp = lambda ap, g: ap[g * TG:(g + 1) * TG, :].rearrange("(s p) d -> p s d", p=P)
        for g in range(NG):
            ctx.dma("sp", x_res[:], grp(x_d, g), xsem, reads=(), writes=tuple(xb))
            rope_tables(posf[:, g * NS:(g + 1) * NS])
            layer_body(0, amask[:, 2, :] if g == 0 else None, pT_d[0, :, g * TG:(g + 1) * TG])
            ctx.dma("sp", grp(x1s_d, g), x_res[:], ssem, reads=tuple(xb), writes=(x1b[g],))
        for g in range(NGH):
            ctx.dma("sp", x_res[:], grp(x1s_d, g), xsem, reads=(x1b[g],), writes=tuple(xb))
            if g == NGH - 1:
                rope_tables(posf[:, g * NS:(g + 1) * NS])
            layer_body(1, None, None, state_only=True, kv_tail=(g == NGH - 1))
        ctx.op("dve", lambda: V.tensor_scalar(out=Sst[:, 1, :, :], in0=Sst[:, 1, :, :], scalar1=flg[:, 1:2], scalar2=None, op0=ALU.mult),
               (Sst_b, flg_b), (Sst_b,))
        ctx.op("dve", lambda: V.tensor_scalar(out=kTb[:, 1, :, :, 1, :], in0=kTb[:, 1, :, :, 1, :], scalar1=flg[:, 1:2], scalar2=None, op0=ALU.mult),
               (kTb_b, flg_b), (kTb_b,))
        ctx.op("dve", lambda: V.tensor_scalar(out=Vb[:, 1, 1, :, :], in0=Vb[:, 1, 1, :, :], scalar1=flg[:, 1:2], scalar2=None, op0=ALU.mult),
               (Vb_b, flg_b), (Vb_b,))
        for j in range(NGH):
            with ExitStack() as st:
                xtmp, xtmp_b = sb("xtmp", [P, NS, D], F32, st)
                ctx.dma("sp", x_res[:], grp(x1s_d, j), xsem, reads=(x1b[j],), writes=tuple(xb))
                for e2 in ("pe", "act", "dve"):
                    ctx._wait("sp", (ctx.sem[e2], ctx.cnt[e2], e2))
                ctx.dma("sp", xtmp[:], grp(x1s_d, NGH + j), xsem2, reads=(x1b[NGH + j],), writes=(xtmp_b,))
                for s in range(NS):
                    ctx.op("dve", lambda: V.tensor_scalar(out=x_res[:, s, :], in0=x_res[:, s, :], scalar1=flg[:, 0:1], scalar2=None, op0=ALU.mult),
                           (xb[s], flg_b), (xb[s],))
                    ctx.op("dve", lambda: V.scalar_tensor_tensor(out=x_res[:, s, :], in0=xtmp[:, s, :], scalar=flg[:, 1:2], in1=x_res[:, s, :],
                                                                 op0=ALU.mult, op1=ALU.add), (xtmp_b, flg_b, xb[s]), (xb[s],))
                ctx.barrier()
            rope_tables(posf1[:, j * NS:(j + 1) * NS])
            layer_body(1, amask1[:] if j == 0 else None, pT1_d[:, j * TG:(j + 1) * TG])
            ctx.dma("sp", grp(y_d, j), x_res[:], ysem, reads=tuple(xb), writes=())
        ctx._wait("sp", (ssem.sem, ssem.cnt, ssem.key))

    try:
      if split:
          main_split()
      else:
          main_body(NG)
    except _Stop:
        top.close()
        return nc
    ctx._wait("sp", (ysem.sem, ysem.cnt, ysem.key))
    if use_cache:
        for d_ in wbsems:
            if d_.cnt > 0:
                ctx._wait("sp", (d_.sem, d_.cnt, d_.key))
    assert wst["consumed"] == len(wplan), (wst, len(wplan))
    top.close()
    print(f"[build] ops={ctx.nops} waits={ctx.nwaits} weight_tiles={len(wplan)}")
    return nc


def _fm(v):
    v = np.asarray(v, np.float32)
    return np.ascontiguousarray(v.reshape(-1, P).T)


def _host_consts():
    half = 32
    invf = (10000.0 ** (-np.arange(half, dtype=np.float32) / half)).astype(np.float32)
    tq = np.arange(P)[:, None]
    tk = np.arange(P)[None, :]
    cur = np.where(tk <= tq, 0.0, MASKV).astype(np.float32)
    prev = np.where(tk > tq, 0.0, MASKV).astype(np.float32)
    dead = np.full((P, P), MASKV, np.float32)
    am = np.stack([
        np.concatenate([cur, prev], 1),
        np.concatenate([prev, cur], 1),
        np.concatenate([cur, dead], 1),
        np.concatenate([dead, cur], 1),
    ]).astype(np.float32)
    s = np.arange(P)[:, None]
    t = np.arange(P)[None, :]
    hm = ((s // 64 == t // 64) & (s <= t)).astype(np.float32)
    return invf, am, hm


def _prep_shared(inp, NL=2):
    w_in = np.asarray(inp["w_in"], np.float32)
    cols = list(range(1536))
    for h in range(NHH):
        for sec in range(4):
            base = 1536 + sec * 1024 + h * P
            cols.extend(range(base, base + P))
    w_in_p = np.ascontiguousarray(w_in[:, :, cols])
    vec = np.zeros((P, 2 * NVEC), np.float32)
    for l in range(2):
        b = l * NVEC
        vec[:, b + 0:b + 16] = _fm(inp["pre_mix_gain"][l])
        vec[:, b + 16:b + 32] = _fm(inp["pre_ffn_gain"][l])
        vec[:, b + 32:b + 48] = _fm(inp["ple_gain"][l])
        vec[:, b + 48:b + 56] = _fm(inp["attn_out_gain"][l])
        vec[:, b + 56:b + 64] = _fm(inp["hgrn_out_gain"][l])
        vec[:, b + 64:b + 72] = _fm(inp["hgrn_lb_logits"][l])
    invf, am, hm = _host_consts()
    f = lambda k: np.ascontiguousarray(np.asarray(inp[k], np.float32))
    return {
        "w_in": w_in_p, "w_out": f("w_out"), "w_gate": f("w_ffn_gate"), "w_up": f("w_ffn_up"), "w_down": f("w_ffn_down"),
        "w_pg": f("w_ple_gate"), "w_pp": f("w_ple_proj"), "vec_fm": vec,
        "post_mix_gain": f("post_mix_gain"), "post_ffn_gain": f("post_ffn_gain"),
        "sinks": np.ascontiguousarray(np.asarray(inp["attn_sinks"], np.float32).reshape(32)),
        "invf": invf, "amask": am, "hmask": hm,
    }


def _prep_core(inp, b, NTOK):
    x = np.ascontiguousarray(np.asarray(inp["x"], np.float32)[b, :NTOK])
    pT = np.ascontiguousarray(np.transpose(np.asarray(inp["p"], np.float32)[:, b, :NTOK, :], (0, 2, 1)))
    pos = np.asarray(inp["positions"])[b, :NTOK].astype(np.int32)
    posT = np.ascontiguousarray(pos.reshape(-1, P).T)
    return {"x": x, "pT": pT, "posT": posT}


def _prep_core_split(inp, b, half, S):
    H = S // 2
    x = np.ascontiguousarray(np.asarray(inp["x"], np.float32)[b])
    p = np.asarray(inp["p"], np.float32)
    pT = np.ascontiguousarray(np.transpose(p[:, b], (0, 2, 1)))
    pT1 = np.ascontiguousarray(pT[1][:, half * H:(half + 1) * H])
    pos = np.asarray(inp["positions"])[b].astype(np.int32)
    posT = np.ascontiguousarray(pos.reshape(-1, P).T)
    posT1 = np.ascontiguousarray(pos[half * H:(half + 1) * H].reshape(-1, P).T)
    _, am, _ = _host_consts()
    flags = np.array([1.0, 0.0] if half == 0 else [0.0, 1.0], np.float32)
    amask1 = np.ascontiguousarray(am[2] if half == 0 else am[0])
    return {"x": x, "pT": pT, "posT": posT, "pT1": pT1, "posT1": posT1, "flags": flags, "amask1": amask1}


def kernel(**inputs):
    B, S = 4, 4096
    nc = build(S, 2, split=True)
    shared = _prep_shared(inputs)
    in_maps = []
    for c in range(2 * B):
        m = dict(shared)
        m.update(_prep_core_split(inputs, c // 2, c % 2, S))
        in_maps.append(m)
    res = run_bass_kernel_spmd(nc, in_maps, core_ids=list(range(2 * B)))
    H = S // 2
    out = np.empty((B, S, D), np.float32)
    for c in range(2 * B):
        out[c // 2, (c % 2) * H:(c % 2 + 1) * H] = np.asarray(res.results[c]["y"], np.float32)
    return out
```

```python
import math
from contextlib import ExitStack

import numpy as np
import concourse.bass as bass
import concourse.mybir as mybir
from concourse.bass_utils import run_bass_kernel_spmd

F32 = mybir.dt.float32
BF16 = mybir.dt.bfloat16
I32 = mybir.dt.int32
AF = mybir.ActivationFunctionType
ALU = mybir.AluOpType
AX = mybir.AxisListType

P = 128
D = 2048
TG = 512
NS = 4
DFF = 5632
DIN = 5632
NQH = 16
NKV = 4
HD = 64
NHH = 8
EPS = 1e-6
MASKV = -30000.0
NVEC = 72


class Buf:
    __slots__ = ("name", "w", "r")

    def __init__(self, name):
        self.name = name
        self.w = None
        self.r = {}


class DSem:
    ALL = []

    def __init__(self, nc, name):
        self.sem = nc.alloc_semaphore(name)
        self.cnt = 0
        self.key = name
        DSem.ALL.append(self)


class Ctx:
    def __init__(self, nc):
        self.nc = nc
        self.E = {"pe": nc.tensor, "act": nc.scalar, "dve": nc.vector, "pool": nc.gpsimd, "sp": nc.sync}
        self.sem = {}
        self.cnt = {}
        for e in self.E:
            self.sem[e] = nc.alloc_semaphore("s_" + e)
            self.cnt[e] = 0
        self.waited = {}
        self.nwaits = 0
        self.nops = 0

    def _wait(self, e, sig):
        if sig is None:
            return
        sem, val, key = sig
        k = (e, key)
        if self.waited.get(k, 0) >= val:
            return
        self.E[e].wait_ge(sem, val)
        self.waited[k] = val
        self.nwaits += 1

    def _pre(self, e, reads, writes):
        for b in reads:
            if not (e == "pe" and b.w is not None and b.w[2] == "pe"):
                self._wait(e, b.w)
        for b in writes:
            if not (e == "pe" and b.w is not None and b.w[2] == "pe"):
                self._wait(e, b.w)
            for rk, rs in b.r.items():
                if not (e == "pe" and rk == "pe"):
                    self._wait(e, rs)

    def _post(self, sig, reads, writes):
        for b in reads:
            b.r[sig[2]] = sig
        for b in writes:
            b.w = sig
            b.r = {}

    def op(self, e, fn, reads=(), writes=()):
        self._pre(e, reads, writes)
        inst = fn()
        self.cnt[e] += 1
        inst.then_inc(self.sem[e], 1)
        sig = (self.sem[e], self.cnt[e], e)
        self._post(sig, reads, writes)
        self.nops += 1
        return sig

    def dma(self, q, out, in_, dsem, reads=(), writes=()):
        self._pre(q, reads, writes)
        inst = self.E[q].dma_start(out=out, in_=in_)
        dsem.cnt += 16
        inst.then_inc(dsem.sem, 16)
        sig = (dsem.sem, dsem.cnt, dsem.key)
        self._post(sig, reads, writes)
        return sig

    def barrier(self, engines=("pe", "act", "dve")):
        for e in engines:
            for e2 in engines:
                if self.cnt[e2] > 0:
                    self._wait(e, (self.sem[e2], self.cnt[e2], e2))


class _Stop(Exception):
    pass


def build(NTOK, NL, dbg=None, split=False):
    NG = NTOK // TG
    NB = NTOK // P
    NGH = NG // 2
    NTH = NTOK // 2
    nc = bass.Bass("TRN2", target_bir_lowering=False)
    DSem.ALL = []
    V, A, T, G = nc.vector, nc.scalar, nc.tensor, nc.gpsimd

    def din(name, shape, dt=F32):
        return nc.dram_tensor(name, shape, dt, kind="ExternalInput").ap()

    x_d = din("x", [NTOK, D])
    pT_d = din("pT", [2, 256, NTOK])
    pos_d = din("posT", [P, NB], I32)
    w_in_d = din("w_in", [2, D, DIN])
    w_out_d = din("w_out", [2, D, D])
    w_g_d = din("w_gate", [2, D, DFF])
    w_u_d = din("w_up", [2, D, DFF])
    w_d_d = din("w_down", [2, DFF, D])
    w_pg_d = din("w_pg", [2, D, D])
    w_pp_d = din("w_pp", [2, 256, D])
    vec_d = din("vec_fm", [P, 2 * NVEC])
    pmg_d = din("post_mix_gain", [2, D])
    pfg_d = din("post_ffn_gain", [2, D])
    sink_d = din("sinks", [32])
    invf_d = din("invf", [32])
    amask_d = din("amask", [4, P, 256])
    hmask_d = din("hmask", [P, P])
    if split:
        pT1_d = din("pT1", [256, NTH])
        pos1_d = din("posT1", [P, NB // 2], I32)
        flags_d = din("flags", [2])
        amask1_d = din("amask1", [P, 256])
        x1s_d = nc.dram_tensor("x1s", [NTOK, D], F32).ap()
        y_d = nc.dram_tensor("y", [NTH, D], F32, kind="ExternalOutput").ap()
    else:
        y_d = nc.dram_tensor("y", [NTOK, D], F32, kind="ExternalOutput").ap()
    dbg_d = None
    if dbg is not None:
        dbg_d = nc.dram_tensor("dbg", [P, 16 * TG], F32, kind="ExternalOutput").ap()

    ctx = Ctx(nc)
    top = ExitStack()
    dsem_dbg = DSem(nc, "d_dbg")

    def dump(stage, ap2d, ncols):
        if dbg is None or dbg != stage:
            return
        ctx.barrier(("pe", "act", "dve", "pool"))
        nc.gpsimd.dma_start(out=dbg_d[:, 0:ncols], in_=ap2d).then_inc(dsem_dbg.sem, 16)
        dsem_dbg.cnt = 16
        for d_ in DSem.ALL:
            if d_.cnt > 0:
                nc.gpsimd.wait_ge(d_.sem, d_.cnt)
        raise _Stop()

    uniq = {"n": 0}

    def sb(name, shape, dt, st=top):
        uniq["n"] += 1
        t = st.enter_context(nc.sbuf_tensor(f"sb_{name}_{uniq['n']}", list(shape), dt))
        return t, Buf(name)

    def ps(name, shape, dt):
        t = top.enter_context(nc.psum_tensor("ps_" + name, list(shape), dt))
        return t, Buf(name)

    x_res, _ = sb("x_res", [P, NS, D], F32)
    xb = [Buf(f"x{s}") for s in range(NS)]
    actT, _ = sb("actT", [P, 16, TG], BF16)
    actb = [Buf(f"actT{k}") for k in range(16)]
    R_hid, _ = sb("hidT", [P, 44, TG], BF16)
    hidb = [Buf(f"hid{k}") for k in range(44)]
    mixT = R_hid
    WS = 3
    wslots = [sb(f"wslot{i}", [P, 16, 512], BF16) for i in range(WS)]
    wsems = [DSem(nc, f"d_w{i}") for i in range(WS)]
    Sst, Sst_b = sb("Sst", [P, 2, NHH, P], F32)
    kTb, kTb_b = sb("kTb", [P, 2, 2, NKV, 2, P], BF16)
    Vb, Vb_b = sb("Vb", [P, 2, 2, NKV, HD], BF16)
    cs_t, cs_b = sb("cs", [P, 2, NS, 32], F32)
    amask, amask_b = sb("amask", [P, 4, 256], F32)
    amask1_b = Buf("amask1")
    if split:
        amask1, amask1_b = sb("amask1", [P, 256], F32)
        flg, flg_b = sb("flg", [P, 2], F32)
        posi1, posi1_b = sb("posi1", [P, NB // 2], I32)
        posf1, posf1_b = sb("posf1", [P, NB // 2], F32)
    hmask, hmask_b = sb("hmask", [P, P], F32)
    vec, vec_b = sb("vec", [P, 2 * NVEC], F32)
    ident, ident_b = sb("ident", [P, P], BF16)
    ones_f, ones_b = sb("ones_f", [P, P], F32)
    sinkt, sink_b = sb("sinkt", [P, 2, 32], F32)
    invf, invf_b = sb("invf", [P, 32], F32)
    posi, posi_b = sb("posi", [P, NB], I32)
    posf, posf_b = sb("posf", [P, NB], F32)
    lbt, lbt_b = sb("lbt", [P, 2, 2, NHH], F32)
    smallc, smallc_b = sb("smallc", [P, 8], F32)
    pTt, pT_b = sb("pTt", [P, 2, TG], BF16)

    pbank = [ps(f"pb{i}", [P, 512], F32) for i in range(5)]
    pdS = ps("pdS", [P, 512], F32)
    ptb = [ps(f"ptb{i}", [P, 1024], BF16) for i in range(2)]
    rot = {"pb": 0, "pt": 0}

    def next_bank():
        i = rot["pb"]
        rot["pb"] = (i + 1) % 4
        return pbank[i]

    def next_pt():
        i = rot["pt"]
        rot["pt"] = (i + 1) % 2
        return ptb[i]

    flip = {"e": 0}

    def evac_engine():
        flip["e"] ^= 1
        return "act" if flip["e"] else "dve"

    def copy_op(e, out, in_, reads, writes):
        if e == "act":
            return ctx.op("act", lambda: A.copy(out=out, in_=in_), reads, writes)
        return ctx.op("dve", lambda: V.tensor_copy(out=out, in_=in_), reads, writes)

    def scale_op(e, out, in_, sc_ap, reads, writes):
        if e == "act":
            return ctx.op("act", lambda: A.activation(out=out, in_=in_, func=AF.Identity, scale=sc_ap), reads, writes)
        return ctx.op("dve", lambda: V.tensor_scalar(out=out, in0=in_, scalar1=sc_ap, scalar2=None, op0=ALU.mult), reads, writes)

    def setup_load(name, out, in_, buf, q="sp"):
        ctx.dma(q, out, in_, DSem(nc, "d_" + name), reads=(), writes=(buf,))

    setup_load("vec", vec[:], vec_d[:, :], vec_b)
    setup_load("amask", amask[:], amask_d.rearrange("v p k -> p v k"), amask_b)
    setup_load("hmask", hmask[:], hmask_d[:, :], hmask_b)
    setup_load("sink", sinkt[:, 0, :], sink_d.partition_broadcast(P), sink_b)
    setup_load("invf", invf[:], invf_d.partition_broadcast(P), invf_b)
    setup_load("pos", posi[:], pos_d[:, :], posi_b)
    if split:
        setup_load("amask1", amask1[:], amask1_d[:, :], amask1_b)
        setup_load("flg", flg[:], flags_d.partition_broadcast(P), flg_b)
        setup_load("pos1", posi1[:], pos1_d[:, :], posi1_b)
        ctx.op("dve", lambda: V.tensor_copy(out=posf1[:], in_=posi1[:]), (posi1_b,), (posf1_b,))

    ctx.op("dve", lambda: V.memset(ones_f[:], 1.0), (), (ones_b,))
    ctx.op("dve", lambda: V.memset(ident[:], 1.0), (), (ident_b,))
    ctx.op("pool", lambda: G.affine_select(out=ident[:], in_=ident[:], pattern=[[-1, P]], compare_op=ALU.is_equal,
                                           fill=0.0, base=0, channel_multiplier=1), (ident_b,), (ident_b,))
    ctx.op("dve", lambda: V.memset(Sst[:], 0.0), (), (Sst_b,))
    ctx.op("dve", lambda: V.memset(kTb[:], 0.0), (), (kTb_b,))
    ctx.op("dve", lambda: V.memset(Vb[:], 0.0), (), (Vb_b,))
    ctx.op("dve", lambda: V.tensor_copy(out=posf[:], in_=posi[:]), (posi_b,), (posf_b,))
    ctx.op("dve", lambda: V.tensor_scalar(out=sinkt[:, 1, :], in0=sinkt[:, 0, :], scalar1=-1.0, scalar2=None, op0=ALU.mult),
           (sink_b,), (sink_b,))
    LB0 = 64
    l0 = vec[:, LB0:LB0 + 8]
    l1 = vec[:, NVEC + LB0:NVEC + LB0 + 8]
    with ExitStack() as st:
        tm, tm_b = sb("lb_m", [P, 8], F32, st)
        e0, e0_b = sb("lb_e0", [P, 8], F32, st)
        e1, e1_b = sb("lb_e1", [P, 8], F32, st)
        ctx.op("dve", lambda: V.tensor_tensor(out=tm[:], in0=l0, in1=l1, op=ALU.max), (vec_b,), (tm_b,))
        ctx.op("dve", lambda: V.tensor_tensor(out=e0[:], in0=l0, in1=tm[:], op=ALU.subtract), (vec_b, tm_b), (e0_b,))
        ctx.op("dve", lambda: V.tensor_tensor(out=e1[:], in0=l1, in1=tm[:], op=ALU.subtract), (vec_b, tm_b), (e1_b,))
        ctx.op("act", lambda: A.activation(out=e0[:], in_=e0[:], func=AF.Exp), (e0_b,), (e0_b,))
        ctx.op("act", lambda: A.activation(out=e1[:], in_=e1[:], func=AF.Exp), (e1_b,), (e1_b,))
        ctx.op("dve", lambda: V.tensor_tensor(out=tm[:], in0=e0[:], in1=e1[:], op=ALU.add), (e0_b, e1_b), (tm_b,))
        ctx.op("dve", lambda: V.reciprocal(out=tm[:], in_=tm[:]), (tm_b,), (tm_b,))
        ctx.op("dve", lambda: V.memset(lbt[:, 0, 0, :], 0.0), (), (lbt_b,))
        ctx.op("dve", lambda: V.tensor_tensor(out=lbt[:, 1, 0, :], in0=e1[:], in1=tm[:], op=ALU.mult), (e1_b, tm_b), (lbt_b,))
        ctx.op("dve", lambda: V.tensor_scalar(out=lbt[:, :, 1, :], in0=lbt[:, :, 0, :], scalar1=-1.0, scalar2=1.0,
                                              op0=ALU.mult, op1=ALU.add), (lbt_b,), (lbt_b,))
        ctx.barrier()

    def layer_tiles(l):
        for t in range(11):
            yield ("in", w_in_d[l, :, t * 512:(t + 1) * 512].rearrange("(kc p) n -> p kc n", p=P), 16, 512, (l, "in", t))
        for fb in range(4):
            yield ("out", w_out_d[l, :, fb * 512:(fb + 1) * 512].rearrange("(kc p) n -> p kc n", p=P), 16, 512, (l, "out", fb))
        for t in range(11):
            yield ("gate", w_g_d[l, :, t * 512:(t + 1) * 512].rearrange("(kc p) n -> p kc n", p=P), 16, 512, (l, "gate", t))
            yield ("up", w_u_d[l, :, t * 512:(t + 1) * 512].rearrange("(kc p) n -> p kc n", p=P), 16, 512, (l, "up", t))
        for fb in range(4):
            for kp in range(4):
                yield ("down", w_d_d[l, kp * 1408:(kp + 1) * 1408, fb * 512:(fb + 1) * 512]
                       .rearrange("(kc p) n -> p kc n", p=P), 11, 512, (l, "down", fb * 4 + kp))
        for fb in range(4):
            yield ("pg", w_pg_d[l, :, fb * 512:(fb + 1) * 512].rearrange("(kc p) n -> p kc n", p=P), 16, 512, (l, "pg", fb))
            yield ("pp", w_pp_d[l, :, fb * 512:(fb + 1) * 512].rearrange("(kc p) n -> p kc n", p=P), 2, 512, (l, "pp", fb))

    def weight_plan():
        if not split:
            for g in range(NG):
                for l in range(NL):
                    yield from layer_tiles(l)
            return
        for g in range(NG):
            yield from layer_tiles(0)
        for g in range(NGH):
            if g == NGH - 1:
                yield ("in", w_in_d[1, :, 1024:1536].rearrange("(kc p) n -> p kc n", p=P), 16, 512, (1, "in", 2))
            for h in range(NHH):
                c0 = 1536 + h * 512 + 128
                yield ("inp", w_in_d[1, :, c0:c0 + 256].rearrange("(kc p) n -> p kc n", p=P), 16, 256, (1, "inp", h))
        for j in range(NGH):
            yield from layer_tiles(1)

    wplan = list(weight_plan())
    wst = {"issued": 0, "consumed": 0}
    tile_ids = {}
    first_use = []
    for e_ in wplan:
        first_use.append(e_[4] not in tile_ids)
        tile_ids.setdefault(e_[4], len(tile_ids))
    n_reuse = sum(1 for f_ in first_use if not f_)
    use_cache = n_reuse > 0
    if use_cache:
        TPS = 64
        wscr_ds = [nc.dram_tensor(f"wscr{k}", [TPS, P, 16 * 512], BF16).ap() for k in range((len(tile_ids) + TPS - 1) // TPS)]
        scr_b = [Buf(f"wscr{i}") for i in range(len(tile_ids))]
        wbsems = [DSem(nc, f"d_wb{i}") for i in range(WS)]
        wsems_hw = [DSem(nc, f"d_wh{i}") for i in range(WS)]

    def w_issue(upto):
        while wst["issued"] < min(upto, len(wplan)):
            i = wst["issued"]
            tag, src, kc, ncol, key = wplan[i]
            t, b = wslots[i % WS]
            if not use_cache:
                ctx.dma("pool", t[:, 0:kc, 0:ncol], src, wsems[i % WS], reads=(), writes=(b,))
            else:
                tid = tile_ids[key]
                scr = wscr_ds[tid // TPS][tid % TPS, :, 0:kc * ncol].rearrange("p (k n) -> p k n", n=ncol)
                if first_use[i]:
                    ctx.dma("pool", t[:, 0:kc, 0:ncol], src, wsems[i % WS], reads=(), writes=(b,))
                    ctx.dma("sp", scr, t[:, 0:kc, 0:ncol], wbsems[i % WS], reads=(b,), writes=(scr_b[tid],))
                else:
                    ctx.dma("sp", t[:, 0:kc, 0:ncol], scr, wsems_hw[i % WS], reads=(scr_b[tid],), writes=(b,))
            wst["issued"] += 1

    def w_get(tag, held=0):
        i = wst["consumed"]
        assert wplan[i][0] == tag, (wplan[i][0], tag)
        w_issue(i + WS - held)
        wst["consumed"] += 1
        return wslots[i % WS]

    xsem = DSem(nc, "d_x")
    ysem = DSem(nc, "d_y")
    psem = DSem(nc, "d_p")
    gsem = DSem(nc, "d_g")

    def rstd_from_ss(ss, ss_b, n, width):
        ctx.op("act", lambda: A.activation(out=ss[:, 0:n], in_=ss[:, 0:n], func=AF.Ln, scale=1.0 / width, bias=eps_ap),
               (ss_b, smallc_b), (ss_b,))
        ctx.op("act", lambda: A.activation(out=ss[:, 0:n], in_=ss[:, 0:n], func=AF.Exp, scale=-0.5), (ss_b,), (ss_b,))

    ctx.op("dve", lambda: V.memset(smallc[:, 0:1], EPS), (), (smallc_b,))
    eps_ap = smallc[:, 0:1]

    def norm_to_actT(gcol):
        with ExitStack() as st:
            hb4, hb4_b = sb("hb4", [P, NS, D], BF16, st)
            ss, ss_b = sb("nss", [P, NS], F32, st)
            hbs = [Buf(f"hb{s}") for s in range(NS)]
            for s in range(NS):
                ctx.op("act", lambda: A.activation(out=hb4[:, s, :], in_=x_res[:, s, :], func=AF.Square,
                                                   accum_out=ss[:, s:s + 1]), (xb[s],), (hbs[s], ss_b))
            rstd_from_ss(ss, ss_b, NS, D)
            for s in range(NS):
                scale_op("dve" if s % 2 else "act", hb4[:, s, :], x_res[:, s, :], ss[:, s:s + 1], (xb[s], ss_b), (hbs[s],))
            for kc in range(16):
                pt, pt_b = next_pt()

                def tr():
                    for s in range(NS):
                        i = T.transpose(out=pt[:, s * P:(s + 1) * P], in_=hb4[:, s, kc * P:(kc + 1) * P], identity=ident[:])
                    return i
                ctx.op("pe", tr, hbs + [ident_b], (pt_b,))
                scale_op(evac_engine(), actT[:, kc, :], pt[:, 0:TG], vec[:, gcol + kc:gcol + kc + 1], (pt_b, vec_b), (actb[kc],))
            ctx.barrier()

    def res_begin(gain_d_row, st):
        rc = {}
        rc["junk"] = actT[:, 0, :]
        rc["junk_bs"] = (actb[0],)
        rc["gbc"] = actT[:, 4:12, :].rearrange("p a b -> p (a b)").bitcast(F32)
        rc["gbc_bs"] = tuple(actb[4:12])
        rc["ssp"], rc["ssp_b"] = sb("rssp", [P, NS, 4], F32, st)
        rc["ss"], rc["ss_b"] = sb("rss", [P, NS], F32, st)
        ctx.dma("sp", rc["gbc"], gain_d_row.partition_broadcast(P), gsem, reads=(), writes=rc["gbc_bs"])
        return rc

    def res_evac(rc, bk, bk_b, ybuf, yb, s, fb):
        ctx.op("act", lambda: A.activation(out=rc["junk"], in_=bk[:], func=AF.Square, accum_out=rc["ssp"][:, s, fb:fb + 1]),
               (bk_b,), rc["junk_bs"] + (rc["ssp_b"],))
        ctx.op("dve", lambda: V.tensor_tensor(out=ybuf[:, s, fb * 512:(fb + 1) * 512], in0=bk[:], in1=rc["gbc"][:, fb * 512:(fb + 1) * 512],
                                              op=ALU.mult), (bk_b,) + rc["gbc_bs"] + rc["junk_bs"], (yb[s],))

    def res_finish(rc, ybuf, yb):
        ss, ss_b = rc["ss"], rc["ss_b"]
        ctx.op("dve", lambda: V.tensor_reduce(out=ss[:], in_=rc["ssp"][:], axis=AX.X, op=ALU.add), (rc["ssp_b"],), (ss_b,))
        rstd_from_ss(ss, ss_b, NS, D)
        for s in range(NS):
            ctx.op("dve", lambda: V.scalar_tensor_tensor(out=x_res[:, s, :], in0=ybuf[:, s, :], scalar=ss[:, s:s + 1], in1=x_res[:, s, :],
                                                         op0=ALU.mult, op1=ALU.add), (yb[s], ss_b, xb[s]), (xb[s],))
        ctx.barrier()

    def tokmajor_proj(tag, srcT, srcb, nk, ybuf, yb, fb, rc):
        wt, wb = w_get(tag)
        for s in range(NS):
            bk, bk_b = next_bank()

            def mm():
                for kc in range(nk):
                    i = T.matmul(bk[:], lhsT=srcT[:, kc, s * P:(s + 1) * P], rhs=wt[:, kc, :], start=(kc == 0), stop=(kc == nk - 1))
                return i
            ctx.op("pe", mm, list(srcb[0:nk]) + [wb], (bk_b,))
            res_evac(rc, bk, bk_b, ybuf, yb, s, fb)

    def rope_tables(pos_ap):
        with ExitStack() as st:
            ang, ang_b = sb("ang", [P, NS, 32], F32, st)
            kk, kk_b = sb("kk", [P, NS, 32], F32, st)
            ki, ki_b = sb("ki", [P, NS, 32], I32, st)
            yy, yy_b = sb("yy", [P, NS, 32], F32, st)
            mk, mk_b = sb("mk", [P, NS, 32], F32, st)
            C1 = 6.28125
            C2 = 2.0 * math.pi - C1
            ctx.op("dve", lambda: V.tensor_tensor(out=ang[:], in0=invf[:].unsqueeze(1).to_broadcast([P, NS, 32]),
                                                  in1=pos_ap.unsqueeze(2).to_broadcast([P, NS, 32]),
                                                  op=ALU.mult), (invf_b, posf_b) + ((posf1_b,) if split else ()), (ang_b,))
            for which, shift in ((1, 0.0), (0, math.pi / 2)):
                ctx.op("dve", lambda: V.tensor_scalar(out=kk[:], in0=ang[:], scalar1=shift, scalar2=1.0 / (2 * math.pi),
                                                      op0=ALU.add, op1=ALU.mult), (ang_b,), (kk_b,))
                ctx.op("dve", lambda: V.tensor_copy(out=ki[:], in_=kk[:]), (kk_b,), (ki_b,))
                ctx.op("dve", lambda: V.tensor_copy(out=kk[:], in_=ki[:]), (ki_b,), (kk_b,))
                ctx.op("dve", lambda: V.scalar_tensor_tensor(out=yy[:], in0=kk[:], scalar=-C1, in1=ang[:], op0=ALU.mult, op1=ALU.add),
                       (kk_b, ang_b), (yy_b,))
                ctx.op("dve", lambda: V.scalar_tensor_tensor(out=yy[:], in0=kk[:], scalar=-C2, in1=yy[:], op0=ALU.mult, op1=ALU.add),
                       (kk_b, yy_b), (yy_b,))
                if shift != 0.0:
                    ctx.op("dve", lambda: V.tensor_scalar(out=yy[:], in0=yy[:], scalar1=shift, scalar2=None, op0=ALU.add), (yy_b,), (yy_b,))
                ctx.op("dve", lambda: V.tensor_scalar(out=mk[:], in0=yy[:], scalar1=math.pi, scalar2=-2 * math.pi, op0=ALU.is_gt, op1=ALU.mult),
                       (yy_b,), (mk_b,))
                ctx.op("dve", lambda: V.tensor_tensor(out=yy[:], in0=yy[:], in1=mk[:], op=ALU.add), (yy_b, mk_b), (yy_b,))
                ctx.op("dve", lambda: V.tensor_scalar(out=mk[:], in0=yy[:], scalar1=-math.pi, scalar2=2 * math.pi, op0=ALU.is_lt, op1=ALU.mult),
                       (yy_b,), (mk_b,))
                ctx.op("dve", lambda: V.tensor_tensor(out=yy[:], in0=yy[:], in1=mk[:], op=ALU.add), (yy_b, mk_b), (yy_b,))
                ctx.op("dve", lambda: V.tensor_scalar(out=yy[:], in0=yy[:], scalar1=3.1415925, scalar2=-3.1415925, op0=ALU.min, op1=ALU.max),
                       (yy_b,), (yy_b,))
                ctx.op("act", lambda: A.activation(out=cs_t[:, which, :, :], in_=yy[:], func=AF.Sin), (yy_b,), (cs_b,))
            ctx.barrier()


    def layer_body(l, first_mask, pT_src, state_only=False, kv_tail=False):
        vc = l * NVEC

        norm_to_actT(vc + 0)
        dump("actT", actT[:].rearrange("p a b -> p (a b)"), 16 * TG)
        if state_only and kv_tail:
            with ExitStack() as st:
                kvt, kvt_b = sb("kvt", [P, 512], F32, st)
                t1, t1_b = sb("kt1", [P, NKV, 32], F32, st)
                t2, t2_b = sb("kt2", [P, NKV, 32], F32, st)
                kdup, kdup_b = sb("kdup2", [P, NKV, 2, HD], BF16, st)
                wt, wb = w_get("in")
                bk, bk_b = next_bank()

                def mmkv():
                    for kc in range(16):
                        i = T.matmul(bk[:], lhsT=actT[:, kc, 3 * P:4 * P], rhs=wt[:, kc, :], start=(kc == 0), stop=(kc == 15))
                    return i
                ctx.op("pe", mmkv, actb + [wb], (bk_b,))
                copy_op("act", kvt[:], bk[:], (bk_b,), (kvt_b,))
                cosb = cs_t[:, 0, 3, :].unsqueeze(1).to_broadcast([P, NKV, 32])
                sinb = cs_t[:, 1, 3, :].unsqueeze(1).to_broadcast([P, NKV, 32])
                kk4 = kvt[:, 0:256].rearrange("p (h d) -> p h d", d=HD)
                x1 = kk4[:, :, 0:32]
                x2 = kk4[:, :, 32:64]
                ctx.op("dve", lambda: V.tensor_tensor(out=t1[:], in0=x1, in1=cosb, op=ALU.mult), (kvt_b, cs_b), (t1_b,))
                ctx.op("dve", lambda: V.tensor_tensor(out=t2[:], in0=x2, in1=sinb, op=ALU.mult), (kvt_b, cs_b), (t2_b,))
                for dd in range(2):
                    ctx.op("dve", lambda: V.tensor_tensor(out=kdup[:, :, dd, 0:32], in0=t1[:], in1=t2[:], op=ALU.subtract), (t1_b, t2_b), (kdup_b,))
                ctx.op("dve", lambda: V.tensor_tensor(out=t1[:], in0=x2, in1=cosb, op=ALU.mult), (kvt_b, cs_b), (t1_b,))
                ctx.op("dve", lambda: V.tensor_tensor(out=t2[:], in0=x1, in1=sinb, op=ALU.mult), (kvt_b, cs_b), (t2_b,))
                for dd in range(2):
                    ctx.op("dve", lambda: V.tensor_tensor(out=kdup[:, :, dd, 32:64], in0=t1[:], in1=t2[:], op=ALU.add), (t1_b, t2_b), (kdup_b,))
                ctx.op("act", lambda: A.copy(out=Vb[:, l, 1, :, :], in_=kvt[:, 256:512].rearrange("p (g d) -> p g d", d=HD)), (kvt_b,), (Vb_b,))
                pt, pt_b = next_pt()
                kdf = kdup[:].rearrange("p g t d -> p g (t d)")

                def trk():
                    for gg in range(NKV):
                        i = T.transpose(out=pt[:, gg * P:(gg + 1) * P], in_=kdf[:, gg, :], identity=ident[:])
                    return i
                ctx.op("pe", trk, (kdup_b, ident_b), (pt_b,))
                copy_op("act", kTb[0:64, l, 0, :, 1, :], pt[0:64, 0:512].rearrange("p (j t) -> p j t", t=P), (pt_b,), (kTb_b,))
                copy_op("dve", kTb[64:128, l, 1, :, 1, :], pt[64:128, 0:512].rearrange("p (j t) -> p j t", t=P), (pt_b,), (kTb_b,))
                ctx.barrier()
        if not state_only:
            with ExitStack() as st:
                qkv = R_hid[:, 16:40, :].rearrange("p a b -> p (a b)").bitcast(F32).rearrange("p (s c) -> p s c", c=1536)
                qkvb = [Buf(f"qkv{s}") for s in range(NS)]
                an4, _ = sb("an4", [P, NS, 1024], BF16, st)
                an4b = [Buf(f"an4_{s}") for s in range(NS)]
                for t in range(3):
                    wt, wb = w_get("in")
                    for s in range(NS):
                        bk, bk_b = next_bank()

                        def mm():
                            for kc in range(16):
                                i = T.matmul(bk[:], lhsT=actT[:, kc, s * P:(s + 1) * P], rhs=wt[:, kc, :], start=(kc == 0), stop=(kc == 15))
                            return i
                        ctx.op("pe", mm, actb + [wb], (bk_b,))
                        copy_op(evac_engine(), qkv[:, s, t * 512:(t + 1) * 512], bk[:], (bk_b,), (qkvb[s],))

                with ExitStack() as st2:
                    t1, t1_b = sb("rt1", [P, 20, 32], F32, st2)
                    t2, t2_b = sb("rt2", [P, 20, 32], F32, st2)
                    qr, qr_b = sb("qr", [P, NQH, HD], BF16, st2)
                    kdup, kdup_b = sb("kdup", [P, NKV, 2, HD], BF16, st2)
                    qT, qT_b = sb("qT", [P, 8, P], BF16, st2)
                    NR = 3
                    Ssb = [sb(f"Ssb{i}", [P, 2, 256], F32, st2) for i in range(NR)]
                    eb = [sb(f"eb{i}", [P, 2, 256], BF16, st2) for i in range(NR)]
                    eT = [sb(f"eT{i}", [P, 4, P], BF16, st2) for i in range(NR)]
                    stat, stat_b = sb("stat", [P, 8, NQH], F32, st2)
                    atok, atok_b = sb("atok", [P, 1024], F32, st2)
                    ajunk, ajunk_b = sb("ajunk", [P, 1024], BF16, st2)
                    ass, ass_b = sb("ass", [P, NS], F32, st2)

                    for s in range(NS):
                        slot = s % 2
                        mask_ap = first_mask if (s == 0 and first_mask is not None) else amask[:, slot, :]
                        cosb = cs_t[:, 0, s, :].unsqueeze(1).to_broadcast([P, 20, 32])
                        sinb = cs_t[:, 1, s, :].unsqueeze(1).to_broadcast([P, 20, 32])
                        qk = qkv[:, s, 0:1280].rearrange("p (h d) -> p h d", d=HD)
                        x1 = qk[:, :, 0:32]
                        x2 = qk[:, :, 32:64]
                        ctx.op("dve", lambda: V.tensor_tensor(out=t1[:], in0=x1, in1=cosb, op=ALU.mult), (qkvb[s], cs_b), (t1_b,))
                        ctx.op("dve", lambda: V.tensor_tensor(out=t2[:], in0=x2, in1=sinb, op=ALU.mult), (qkvb[s], cs_b), (t2_b,))
                        ctx.op("dve", lambda: V.tensor_tensor(out=qr[:, :, 0:32], in0=t1[:, 0:16, :], in1=t2[:, 0:16, :], op=ALU.subtract),
                               (t1_b, t2_b), (qr_b,))
                        for dd in range(2):
                            ctx.op("dve", lambda: V.tensor_tensor(out=kdup[:, :, dd, 0:32], in0=t1[:, 16:20, :], in1=t2[:, 16:20, :], op=ALU.subtract),
                                   (t1_b, t2_b), (kdup_b,))
                        ctx.op("dve", lambda: V.tensor_tensor(out=t1[:], in0=x2, in1=cosb, op=ALU.mult), (qkvb[s], cs_b), (t1_b,))
                        ctx.op("dve", lambda: V.tensor_tensor(out=t2[:], in0=x1, in1=sinb, op=ALU.mult), (qkvb[s], cs_b), (t2_b,))
                        ctx.op("dve", lambda: V.tensor_tensor(out=qr[:, :, 32:64], in0=t1[:, 0:16, :], in1=t2[:, 0:16, :], op=ALU.add),
                               (t1_b, t2_b), (qr_b,))
                        for dd in range(2):
                            ctx.op("dve", lambda: V.tensor_tensor(out=kdup[:, :, dd, 32:64], in0=t1[:, 16:20, :], in1=t2[:, 16:20, :], op=ALU.add),
                                   (t1_b, t2_b), (kdup_b,))
                        ctx.op("act", lambda: A.copy(out=Vb[:, l, slot, :, :], in_=qkv[:, s, 1280:1536].rearrange("p (g d) -> p g d", d=HD)),
                               (qkvb[s],), (Vb_b,))
                        qrf = qr[:].rearrange("p h d -> p (h d)")
                        for half in range(2):
                            pt, pt_b = next_pt()

                            def trq():
                                for j in range(4):
                                    jj = half * 4 + j
                                    i = T.transpose(out=pt[:, j * P:(j + 1) * P], in_=qrf[:, jj * P:(jj + 1) * P], identity=ident[:])
                                return i
                            ctx.op("pe", trq, (qr_b, ident_b), (pt_b,))
                            copy_op(evac_engine(), qT[:, half * 4:(half + 1) * 4, :], pt[:, 0:512].rearrange("p (j t) -> p j t", t=P),
                                    (pt_b,), (qT_b,))
                        pt, pt_b = next_pt()
                        kdf = kdup[:].rearrange("p g t d -> p g (t d)")

                        def trk():
                            for gg in range(NKV):
                                i = T.transpose(out=pt[:, gg * P:(gg + 1) * P], in_=kdf[:, gg, :], identity=ident[:])
                            return i
                        ctx.op("pe", trk, (kdup_b, ident_b), (pt_b,))
                        copy_op("act", kTb[0:64, l, 0, :, slot, :], pt[0:64, 0:512].rearrange("p (j t) -> p j t", t=P), (pt_b,), (kTb_b,))
                        copy_op("dve", kTb[64:128, l, 1, :, slot, :], pt[64:128, 0:512].rearrange("p (j t) -> p j t", t=P), (pt_b,), (kTb_b,))

                        obk = [pbank[4], pdS]

                        def stage1(j):
                            gq = j // 2
                            bk, bk_b = next_bank()
                            S_t, S_b = Ssb[j % NR]
                            e_t, e_b = eb[j % NR]

                            def mm():
                                for i2 in range(2):
                                    i = T.matmul(bk[:, i2 * 256:(i2 + 1) * 256], lhsT=qT[:, j, :],
                                                 rhs=kTb[:, l, i2, gq, :, :].rearrange("p s t -> p (s t)"), start=True, stop=True)
                                return i
                            ctx.op("pe", mm, (qT_b, kTb_b), (bk_b,))
                            ctx.op("dve", lambda: V.scalar_tensor_tensor(out=S_t[:], in0=bk[:].rearrange("p (i k) -> p i k", k=256), scalar=0.125,
                                                                         in1=mask_ap.unsqueeze(1).to_broadcast([P, 2, 256]), op0=ALU.mult, op1=ALU.add),
                                   (bk_b, amask_b, amask1_b), (S_b,))
                            h0 = 2 * j
                            ctx.op("dve", lambda: V.tensor_reduce(out=stat[:, 0, h0:h0 + 2], in_=S_t[:], axis=AX.X, op=ALU.max, negate=True),
                                   (S_b,), (stat_b,))
                            ctx.op("dve", lambda: V.tensor_tensor(out=stat[:, 0, h0:h0 + 2], in0=stat[:, 0, h0:h0 + 2],
                                                                  in1=sinkt[:, 1, l * 16 + h0:l * 16 + h0 + 2], op=ALU.min), (stat_b, sink_b), (stat_b,))
                            for i2 in range(2):
                                ctx.op("act", lambda: A.activation(out=e_t[:, i2, :], in_=S_t[:, i2, :], func=AF.Exp,
                                                                   bias=stat[:, 0, h0 + i2:h0 + i2 + 1], scale=1.0,
                                                                   accum_out=stat[:, 1, h0 + i2:h0 + i2 + 1]), (S_b, stat_b), (e_b, stat_b))

                        def stage2(j):
                            e_t, e_b = e_bufs = eb[j % NR]
                            eT_t, eT_b = eT[j % NR]
                            pt, pt_b = next_pt()

                            def tr():
                                for i2 in range(2):
                                    for sl in range(2):
                                        i = T.transpose(out=pt[:, (i2 * 2 + sl) * P:(i2 * 2 + sl + 1) * P],
                                                        in_=e_t[:, i2, sl * P:(sl + 1) * P], identity=ident[:])
                                return i
                            ctx.op("pe", tr, (e_b, ident_b), (pt_b,))
                            copy_op("act", eT_t[:], pt[:, 0:512].rearrange("p (j t) -> p j t", t=P), (pt_b,), (eT_b,))

                        def stage3(j):
                            gq = j // 2
                            eT_t, eT_b = eT[j % NR]
                            for i2 in range(2):
                                h = 2 * j + i2
                                ob, ob_b = obk[h // 8]
                                col = (h % 8) * HD

                                def mm():
                                    T.matmul(ob[:, col:col + HD], lhsT=eT_t[:, i2 * 2 + 0, :], rhs=Vb[:, l, 0, gq, :], start=True, stop=False)
                                    return T.matmul(ob[:, col:col + HD], lhsT=eT_t[:, i2 * 2 + 1, :], rhs=Vb[:, l, 1, gq, :], start=False, stop=True)
                                ctx.op("pe", mm, (eT_b, Vb_b), (ob_b,))

                        for step in range(8 + 2):
                            if step < 8:
                                stage1(step)
                            if 0 <= step - 1 < 8:
                                stage2(step - 1)
                            if 0 <= step - 2 < 8:
                                stage3(step - 2)
                        ctx.op("dve", lambda: V.tensor_tensor(out=stat[:, 2, :], in0=stat[:, 0, :], in1=sinkt[:, 0, l * 16:(l + 1) * 16], op=ALU.add),
                               (stat_b, sink_b), (stat_b,))
                        ctx.op("act", lambda: A.activation(out=stat[:, 3, :], in_=stat[:, 2, :], func=AF.Exp), (stat_b,), (stat_b,))
                        ctx.op("dve", lambda: V.tensor_tensor(out=stat[:, 3, :], in0=stat[:, 3, :], in1=stat[:, 1, :], op=ALU.add), (stat_b,), (stat_b,))
                        ctx.op("dve", lambda: V.reciprocal(out=stat[:, 4, :], in_=stat[:, 3, :]), (stat_b,), (stat_b,))
                        for hb_ in range(2):
                            ob, ob_b = obk[hb_]
                            ctx.op("dve", lambda: V.tensor_tensor(out=atok[:, hb_ * 512:(hb_ + 1) * 512].rearrange("p (h d) -> p h d", d=HD),
                                                                  in0=ob[:].rearrange("p (h d) -> p h d", d=HD),
                                                                  in1=stat[:, 4, hb_ * 8:(hb_ + 1) * 8].unsqueeze(2).to_broadcast([P, 8, HD]),
                                                                  op=ALU.mult), (ob_b, stat_b), (atok_b,))
                        ctx.op("act", lambda: A.activation(out=ajunk[:], in_=atok[:], func=AF.Square, accum_out=ass[:, s:s + 1]),
                               (atok_b,), (ajunk_b, ass_b))
                        rstd_from_ss(ass[:, s:s + 1], ass_b, 1, 1024)
                        scale_op("act", an4[:, s, :], atok[:], ass[:, s:s + 1], (atok_b, ass_b), (an4b[s],))
                    ctx.barrier()
                for j in range(8):
                    pt, pt_b = next_pt()

                    def tr():
                        for s in range(NS):
                            i = T.transpose(out=pt[:, s * P:(s + 1) * P], in_=an4[:, s, j * P:(j + 1) * P], identity=ident[:])
                        return i
                    ctx.op("pe", tr, an4b + [ident_b], (pt_b,))
                    scale_op(evac_engine(), mixT[:, j, :], pt[:, 0:TG], vec[:, vc + 48 + j:vc + 48 + j + 1], (pt_b, vec_b), (hidb[j],))
                ctx.barrier()

        dump("attn", mixT[:, 0:8, :].rearrange("p a b -> p (a b)"), 8 * TG)
        with ExitStack() as st:
            def f32t(name, shape=(P, TG)):
                return sb(name, list(shape), F32, st)
            ff, ff_b = f32t("h_f")
            qs, qs_b = f32t("h_qs")
            sgate, sgate_b = f32t("h_sgate")
            vT, vT_b = sb("h_vT", [P, TG], BF16, st)
            bT, bT_b = f32t("h_b")
            dd_, dd_b = f32t("h_d")
            E1, E1_b = f32t("h_E1")
            kin, kin_b = f32t("h_kin")
            qt, qt_b = sb("h_qt", [P, TG], BF16, st)
            kt, kt_b = sb("h_kt", [P, TG], BF16, st)
            vtok, vtok_b = sb("h_vtok", [P, 4, P], BF16, st)
            ktokA, ktokA_b = sb("h_ktokA", [P, 4, P], BF16, st)
            ktokB, ktokB_b = sb("h_ktokB", [P, 4, P], BF16, st)
            AT, AT_b = sb("h_AT", [P, 4, P], BF16, st)
            T3, T3_b = f32t("h_T3", (P, P, 9))
            D3, D3_b = f32t("h_D3", (P, P, 9))
            S3, S3_b = f32t("h_S3", (P, P, 9))
            Sra, Sra_b = sb("h_Sra", [P, 8, P], BF16, st)
            sq, sq_b = dd_, dd_b
            rs_, rs_b = bT, bT_b
            sc, sc_b = f32t("h_sc", (P, 4, 8))
            rmask, rmask_b = f32t("h_rmask")
            ctx.op("dve", lambda: V.memset(ktokA[:], 0.0), (), (ktokA_b,))
            ctx.op("dve", lambda: V.memset(ktokB[:], 0.0), (), (ktokB_b,))
            ctx.op("dve", lambda: V.memset(D3[:], 0.0), (), (D3_b,))
            ctx.op("dve", lambda: V.memset(rmask[:], 1.0), (), (rmask_b,))
            ctx.op("dve", lambda: V.memset(rmask[:].rearrange("p (c t) -> p c t", t=64)[:, :, 0:1], 0.0), (), (rmask_b,))

            NCG = 2 if state_only else 4
            hinfo = {}
            pend = []
            ptf = [ptb[0][0][:, :].bitcast(F32), ptb[1][0][:, :].bitcast(F32)]

            def inproj_start(hh):
                hinfo[hh] = {"w": w_get("inp" if state_only else "in"), "banks": []}
                pend[:] = [(hh, c) for c in range(NCG)]

            def fill(n):
                for _ in range(n):
                    if not pend:
                        return
                    hh, c = pend.pop(0)
                    wt, wb = hinfo[hh]["w"]
                    bk, bk_b = next_bank()

                    def mm():
                        for kc in range(16):
                            i = T.matmul(bk[:], lhsT=wt[:, kc, c * P:(c + 1) * P], rhs=actT[:, kc, :], start=(kc == 0), stop=(kc == 15))
                        return i
                    ctx.op("pe", mm, actb + [wb], (bk_b,))
                    hinfo[hh]["banks"].append((bk, bk_b))

            def evac(hh):
                banks = hinfo[hh]["banks"]
                if state_only:
                    (bf_, bf_b), (bi_, bi_b) = banks
                else:
                    (bq, bq_b), (bf_, bf_b), (bi_, bi_b), (bg, bg_b) = banks
                ctx.op("act", lambda: A.activation(out=ff[:], in_=bf_[:], func=AF.Sigmoid), (bf_b,), (ff_b,))
                if not state_only:
                    ctx.op("act", lambda: A.activation(out=qs[:], in_=bq[:], func=AF.Silu), (bq_b,), (qs_b,))
                ctx.op("act", lambda: A.copy(out=vT[:], in_=bi_[:]), (bi_b,), (vT_b,))
                if not state_only:
                    ctx.op("act", lambda: A.activation(out=sgate[:], in_=bg[:], func=AF.Silu), (bg_b,), (sgate_b,))

            inproj_start(0)
            fill(NCG)
            evac(0)
            for h in range(NHH):
                if h + 1 < NHH:
                    inproj_start(h + 1)
                    fill(NCG // 2)
                ctx.op("dve", lambda: V.tensor_scalar(out=ff[:], in0=ff[:], scalar1=lbt[:, l, 1, h:h + 1], scalar2=lbt[:, l, 0, h:h + 1],
                                                      op0=ALU.mult, op1=ALU.add), (ff_b, lbt_b), (ff_b,))
                ctx.op("dve", lambda: V.tensor_scalar(out=kin[:], in0=ff[:], scalar1=-1.0, scalar2=1.0, op0=ALU.mult, op1=ALU.add),
                       (ff_b,), (kin_b,))
                ctx.op("act", lambda: A.activation(out=ff[:], in_=ff[:], func=AF.Ln), (ff_b,), (ff_b,))
                ctx.op("dve", lambda: V.tensor_tensor_scan(out=bT[:], data0=rmask[:], data1=ff[:], initial=0.0, op0=ALU.mult, op1=ALU.add),
                       (rmask_b, ff_b), (bT_b,))
                b3 = bT[:].rearrange("p (c t) -> p c t", t=64)
                ctx.op("dve", lambda: V.tensor_tensor(out=dd_[:].rearrange("p (c t) -> p c t", t=64), in0=b3,
                                                      in1=b3[:, :, 31:32].to_broadcast([P, 8, 64]), op=ALU.subtract), (bT_b,), (dd_b,))
                if not state_only:
                    ctx.op("act", lambda: A.activation(out=E1[:], in_=dd_[:], func=AF.Exp), (dd_b,), (E1_b,))
                    ctx.op("dve", lambda: V.tensor_tensor(out=qt[:], in0=qs[:], in1=E1[:], op=ALU.mult), (qs_b, E1_b), (qt_b,))
                ctx.op("act", lambda: A.activation(out=E1[:], in_=dd_[:], func=AF.Exp, scale=-1.0), (dd_b,), (E1_b,))
                ctx.op("dve", lambda: V.tensor_tensor(out=kt[:], in0=kin[:], in1=E1[:], op=ALU.mult), (kin_b, E1_b), (kt_b,))
                ctx.op("act", lambda: A.activation(out=sc[:, 0, :], in_=b3[:, :, 31], func=AF.Exp), (bT_b,), (sc_b,))
                ctx.op("act", lambda: A.activation(out=sc[:, 1, :], in_=b3[:, :, 63], func=AF.Exp), (bT_b,), (sc_b,))
                ctx.op("dve", lambda: V.tensor_tensor(out=sc[:, 3, :], in0=b3[:, :, 63], in1=b3[:, :, 31], op=ALU.subtract), (bT_b,), (sc_b,))
                ctx.op("act", lambda: A.activation(out=sc[:, 2, :], in_=sc[:, 3, :], func=AF.Exp), (sc_b,), (sc_b,))
                ptA, ptA_b = ptb[0]
                ptB, ptB_b = ptb[1]

                def trv():
                    for pc in range(4):
                        i = T.transpose(out=ptA[:, pc * P:(pc + 1) * P], in_=vT[:, pc * P:(pc + 1) * P], identity=ident[:])
                    return i
                ctx.op("pe", trv, (vT_b, ident_b), (ptA_b,))
                copy_op("act", vtok[:], ptA[:, 0:512].rearrange("p (c k) -> p c k", k=P), (ptA_b,), (vtok_b,))

                def trk2():
                    for pc in range(4):
                        i = T.transpose(out=ptB[:, pc * P:(pc + 1) * P], in_=kt[:, pc * P:(pc + 1) * P], identity=ident[:])
                    return i
                ctx.op("pe", trk2, (kt_b, ident_b), (ptB_b,))
                copy_op("act", ktokA[0:64, :, :], ptB[0:64, 0:512].rearrange("p (c k) -> p c k", k=P), (ptB_b,), (ktokA_b,))
                copy_op("dve", ktokB[64:128, :, :], ptB[64:128, 0:512].rearrange("p (c k) -> p c k", k=P), (ptB_b,), (ktokB_b,))
                fill(1 if not state_only else 1)
                for j2, (dbk, dbk_b) in enumerate(((pdS[0][:, :], pdS[1]), (ptf[0], ptA_b))):
                    def mmd():
                        for cq in range(4):
                            c = 4 * j2 + cq
                            pc = c // 2
                            ktk = ktokA if c % 2 == 0 else ktokB
                            i = T.matmul(dbk[:, cq * P:(cq + 1) * P], lhsT=ktk[:, pc, :], rhs=vtok[:, pc, :], start=True, stop=True)
                        return i
                    ctx.op("pe", mmd, (ktokA_b, ktokB_b, vtok_b), (dbk_b,))
                    ctx.op("dve", lambda: V.tensor_tensor(out=T3[:, :, 1 + 4 * j2:5 + 4 * j2].rearrange("p v c -> p c v"),
                                                          in0=dbk.rearrange("p (c v) -> p c v", v=P),
                                                          in1=sc[:, 2, 4 * j2:4 * j2 + 4].unsqueeze(2).to_broadcast([P, 4, P]), op=ALU.mult),
                           (dbk_b, sc_b), (T3_b,))
                Sh = Sst[:, l, h, :]
                ctx.op("act", lambda: A.copy(out=T3[:, :, 0], in_=Sh), (Sst_b,), (T3_b,))
                ctx.op("dve", lambda: V.tensor_copy(out=D3[:, :, 1:9], in_=sc[:, 1, :].unsqueeze(1).to_broadcast([P, P, 8])), (sc_b,), (D3_b,))
                ctx.op("dve", lambda: V.tensor_tensor_scan(out=S3[:].rearrange("p v j -> p (v j)"), data0=D3[:].rearrange("p v j -> p (v j)"),
                                                           data1=T3[:].rearrange("p v j -> p (v j)"), initial=0.0, op0=ALU.mult, op1=ALU.add),
                       (D3_b, T3_b), (S3_b,))
                if not state_only:
                    ctx.op("dve", lambda: V.tensor_tensor(out=Sra[:], in0=S3[:, :, 0:8].rearrange("p v c -> p c v"),
                                                          in1=sc[:, 0, :].unsqueeze(2).to_broadcast([P, 8, P]), op=ALU.mult), (S3_b, sc_b), (Sra_b,))
                ctx.op("act", lambda: A.copy(out=Sh, in_=S3[:, :, 8]), (S3_b,), (Sst_b,))
                fill(NCG)
                if state_only:
                    if h + 1 < NHH:
                        evac(h + 1)
                    continue
                bkA = ptf[1]

                def mmA():
                    for pc in range(4):
                        i = T.matmul(bkA[:, pc * P:(pc + 1) * P], lhsT=kt[:, pc * P:(pc + 1) * P], rhs=qt[:, pc * P:(pc + 1) * P],
                                     start=True, stop=True)
                    return i
                ctx.op("pe", mmA, (kt_b, qt_b), (ptB_b,))
                ctx.op("dve", lambda: V.tensor_tensor(out=AT[:], in0=bkA.rearrange("p (c t) -> p c t", t=P),
                                                      in1=hmask[:].unsqueeze(1).to_broadcast([P, 4, P]), op=ALU.mult),
                       (ptB_b, hmask_b), (AT_b,))
                oT, oT_b = pbank[4]

                def mmo():
                    for pc in range(4):
                        T.matmul(oT[:, pc * P:(pc + 1) * P], lhsT=vtok[:, pc, :], rhs=AT[:, pc, :], start=True, stop=False)
                        for cc in range(2):
                            c = 2 * pc + cc
                            i = T.matmul(oT[:, c * 64:(c + 1) * 64], lhsT=Sra[:, c, :], rhs=qt[:, c * 64:(c + 1) * 64],
                                         start=False, stop=(cc == 1))
                    return i
                ctx.op("pe", mmo, (vtok_b, AT_b, Sra_b, qt_b), (oT_b,))
                ctx.op("act", lambda: A.activation(out=sq[:], in_=oT[:], func=AF.Square), (oT_b,), (sq_b,))
                bk, bk_b = pdS
                ctx.op("pe", lambda: T.matmul(bk[:], lhsT=ones_f[:], rhs=sq[:], start=True, stop=True), (ones_b, sq_b), (bk_b,))
                ctx.op("act", lambda: A.activation(out=rs_[:], in_=bk[:], func=AF.Ln, scale=1.0 / P, bias=eps_ap), (bk_b, smallc_b), (rs_b,))
                ctx.op("act", lambda: A.activation(out=rs_[:], in_=rs_[:], func=AF.Exp, scale=-0.5), (rs_b,), (rs_b,))
                ctx.op("dve", lambda: V.tensor_tensor(out=sq[:], in0=oT[:], in1=rs_[:], op=ALU.mult), (oT_b, rs_b), (sq_b,))
                ctx.op("dve", lambda: V.scalar_tensor_tensor(out=mixT[:, 8 + h, :], in0=sq[:], scalar=vec[:, vc + 56 + h:vc + 56 + h + 1],
                                                             in1=sgate[:], op0=ALU.mult, op1=ALU.mult), (sq_b, vec_b, sgate_b), (hidb[8 + h],))
                if h + 1 < NHH:
                    evac(h + 1)
            ctx.barrier()
        if state_only:
            return
        dump("hgrn", mixT[:, 8:16, :].rearrange("p a b -> p (a b)"), 8 * TG)
        with ExitStack() as st:
            ybuf, _ = sb("ybuf", [P, NS, D], F32, st)
            yb = [Buf(f"y{s}") for s in range(NS)]
            rc = res_begin(pmg_d[l], st)
            for fb in range(4):
                tokmajor_proj("out", mixT, hidb, 16, ybuf, yb, fb, rc)
            res_finish(rc, ybuf, yb)

        dump("xmix", x_res[:].rearrange("p a b -> p (a b)"), 4 * D)
        norm_to_actT(vc + 16)
        with ExitStack() as st:
            sgt = [sb(f"f_sg{i}", [P, TG], F32, st) for i in range(8)]
            for t in range(11):
                wg, wg_b = w_get("gate")
                for c in range(4):
                    bg, bg_b = next_bank()

                    def mmg():
                        for kc in range(16):
                            i = T.matmul(bg[:], lhsT=wg[:, kc, c * P:(c + 1) * P], rhs=actT[:, kc, :], start=(kc == 0), stop=(kc == 15))
                        return i
                    ctx.op("pe", mmg, actb + [wg_b], (bg_b,))
                    s_t, s_b = sgt[(t % 2) * 4 + c]
                    ctx.op("act", lambda: A.activation(out=s_t[:], in_=bg[:], func=AF.Silu), (bg_b,), (s_b,))
                wu, wu_b = w_get("up")
                for c in range(4):
                    bu, bu_b = next_bank()

                    def mmu():
                        for kc in range(16):
                            i = T.matmul(bu[:], lhsT=wu[:, kc, c * P:(c + 1) * P], rhs=actT[:, kc, :], start=(kc == 0), stop=(kc == 15))
                        return i
                    ctx.op("pe", mmu, actb + [wu_b], (bu_b,))
                    s_t, s_b = sgt[(t % 2) * 4 + c]
                    hc = t * 4 + c
                    ctx.op("dve", lambda: V.tensor_tensor(out=R_hid[:, hc, :], in0=s_t[:], in1=bu[:], op=ALU.mult), (s_b, bu_b), (hidb[hc],))
            ctx.barrier()
        with ExitStack() as st:
            ybuf, _ = sb("ybuf2", [P, NS, D], F32, st)
            yb = [Buf(f"y2{s}") for s in range(NS)]
            rc = res_begin(pfg_d[l], st)
            for fb in range(4):
                for kp in range(4):
                    wt, wb = w_get("down")
                    for s in range(NS):
                        bk, bk_b = pbank[s]

                        def mm():
                            for kc in range(11):
                                hc = kp * 11 + kc
                                i = T.matmul(bk[:], lhsT=R_hid[:, hc, s * P:(s + 1) * P], rhs=wt[:, kc, :],
                                             start=(hc == 0), stop=(hc == 43))
                            return i
                        ctx.op("pe", mm, hidb[kp * 11:(kp + 1) * 11] + [wb], (bk_b,))
                for s in range(NS):
                    bk, bk_b = pbank[s]
                    res_evac(rc, bk, bk_b, ybuf, yb, s, fb)
            res_finish(rc, ybuf, yb)

        dump("xffn", x_res[:].rearrange("p a b -> p (a b)"), 4 * D)
        norm_to_actT(vc + 32)
        with ExitStack() as st:
            ctx.dma("pool", pTt[:], pT_src.rearrange("(kc p) t -> p kc t", p=P), psem, reads=(), writes=(pT_b,))
            sgs = [sb(f"p_sg{i}", [P, 512], F32, st) for i in range(2)]
            tm2 = [sb(f"p_tm{i}", [P, 512], F32, st) for i in range(2)]
            k = 0
            for fb in range(4):
                wg, wg_b = w_get("pg")
                wp, wp_b = w_get("pp", held=1)
                for s in range(NS):
                    bg, bg_b = next_bank()
                    bp, bp_b = next_bank()

                    def mmg():
                        for kc in range(16):
                            i = T.matmul(bg[:], lhsT=actT[:, kc, s * P:(s + 1) * P], rhs=wg[:, kc, :], start=(kc == 0), stop=(kc == 15))
                        return i

                    def mmp():
                        for kc in range(2):
                            i = T.matmul(bp[:], lhsT=pTt[:, kc, s * P:(s + 1) * P], rhs=wp[:, kc, :], start=(kc == 0), stop=(kc == 1))
                        return i
                    ctx.op("pe", mmg, actb + [wg_b], (bg_b,))
                    ctx.op("pe", mmp, (pT_b, wp_b), (bp_b,))
                    s_t, s_b = sgs[k % 2]
                    t_t, t_b = tm2[k % 2]
                    k += 1
                    ctx.op("act", lambda: A.activation(out=s_t[:], in_=bg[:], func=AF.Sigmoid), (bg_b,), (s_b,))
                    ctx.op("dve", lambda: V.tensor_tensor(out=t_t[:], in0=s_t[:], in1=bp[:], op=ALU.mult), (s_b, bp_b), (t_b,))
                    xs = x_res[:, s, fb * 512:(fb + 1) * 512]
                    ctx.op("dve", lambda: V.tensor_tensor(out=xs, in0=xs, in1=t_t[:], op=ALU.add), (xb[s], t_b), (xb[s],))
            ctx.barrier()


    def main_body(NG):
        for g in range(NG):
            ctx.dma("sp", x_res[:], x_d[g * TG:(g + 1) * TG, :].rearrange("(s p) d -> p s d", p=P), xsem, reads=(), writes=tuple(xb))
            rope_tables(posf[:, g * NS:(g + 1) * NS])
            for l in range(NL):
                layer_body(l, amask[:, 2, :] if g == 0 else None, pT_d[l, :, g * TG:(g + 1) * TG])
            ctx.dma("sp", y_d[g * TG:(g + 1) * TG, :].rearrange("(s p) d -> p s d", p=P), x_res[:], ysem, reads=tuple(xb), writes=())

    def main_split():
        x1b = [Buf(f"x1s{g}") for g in range(NG)]
        ssem = DSem(nc, "d_s")
        xsem2 = DSem(nc, "d_x2")
        grp = lambda ap, g: ap[g * TG:(g + 1) * TG, :].rearrange("(s p) d -> p s d", p=P)
        for g in range(NG):
            ctx.dma("sp", x_res[:], grp(x_d, g), xsem, reads=(), writes=tuple(xb))
            rope_tables(posf[:, g * NS:(g + 1) * NS])
            layer_body(0, amask[:, 2, :] if g == 0 else None, pT_d[0, :, g * TG:(g + 1) * TG])
            ctx.dma("sp", grp(x1s_d, g), x_res[:], ssem, reads=tuple(xb), writes=(x1b[g],))
        for g in range(NGH):
            ctx.dma("sp", x_res[:], grp(x1s_d, g), xsem, reads=(x1b[g],), writes=tuple(xb))
            if g == NGH - 1:
                rope_tables(posf[:, g * NS:(g + 1) * NS])
            layer_body(1, None, None, state_only=True, kv_tail=(g == NGH - 1))
        ctx.op("dve", lambda: V.tensor_scalar(out=Sst[:, 1, :, :], in0=Sst[:, 1, :, :], scalar1=flg[:, 1:2], scalar2=None, op0=ALU.mult),
               (Sst_b, flg_b), (Sst_b,))
        ctx.op("dve", lambda: V.tensor_scalar(out=kTb[:, 1, :, :, 1, :], in0=kTb[:, 1, :, :, 1, :], scalar1=flg[:, 1:2], scalar2=None, op0=ALU.mult),
               (kTb_b, flg_b), (kTb_b,))
        ctx.op("dve", lambda: V.tensor_scalar(out=Vb[:, 1, 1, :, :], in0=Vb[:, 1, 1, :, :], scalar1=flg[:, 1:2], scalar2=None, op0=ALU.mult),
               (Vb_b, flg_b), (Vb_b,))
        for j in range(NGH):
            with ExitStack() as st:
                xtmp, xtmp_b = sb("xtmp", [P, NS, D], F32, st)
                ctx.dma("sp", x_res[:], grp(x1s_d, j), xsem, reads=(x1b[j],), writes=tuple(xb))
                for e2 in ("pe", "act", "dve"):
                    ctx._wait("sp", (ctx.sem[e2], ctx.cnt[e2], e2))
                ctx.dma("sp", xtmp[:], grp(x1s_d, NGH + j), xsem2, reads=(x1b[NGH + j],), writes=(xtmp_b,))
                for s in range(NS):
                    ctx.op("dve", lambda: V.tensor_scalar(out=x_res[:, s, :], in0=x_res[:, s, :], scalar1=flg[:, 0:1], scalar2=None, op0=ALU.mult),
                           (xb[s], flg_b), (xb[s],))
                    ctx.op("dve", lambda: V.scalar_tensor_tensor(out=x_res[:, s, :], in0=xtmp[:, s, :], scalar=flg[:, 1:2], in1=x_res[:, s, :],
                                                                 op0=ALU.mult, op1=ALU.add), (xtmp_b, flg_b, xb[s]), (xb[s],))
                ctx.barrier()
            rope_tables(posf1[:, j * NS:(j + 1) * NS])
            layer_body(1, amask1[:] if j == 0 else None, pT1_d[:, j * TG:(j + 1) * TG])
            ctx.dma("sp", grp(y_d, j), x_res[:], ysem, reads=tuple(xb), writes=())
        ctx._wait("sp", (ssem.sem, ssem.cnt, ssem.key))

    try:
      if split:
          main_split()
      else:
          main_body(NG)
    except _Stop:
        top.close()
        return nc
    ctx._wait("sp", (ysem.sem, ysem.cnt, ysem.key))
    if use_cache:
        for d_ in wbsems:
            if d_.cnt > 0:
                ctx._wait("sp", (d_.sem, d_.cnt, d_.key))
    assert wst["consumed"] == len(wplan), (wst, len(wplan))
    top.close()
    print(f"[build] ops={ctx.nops} waits={ctx.nwaits} weight_tiles={len(wplan)}")
    return nc


def _fm(v):
    v = np.asarray(v, np.float32)
    return np.ascontiguousarray(v.reshape(-1, P).T)


def _host_consts():
    half = 32
    invf = (10000.0 ** (-np.arange(half, dtype=np.float32) / half)).astype(np.float32)
    tq = np.arange(P)[:, None]
    tk = np.arange(P)[None, :]
    cur = np.where(tk <= tq, 0.0, MASKV).astype(np.float32)
    prev = np.where(tk > tq, 0.0, MASKV).astype(np.float32)
    dead = np.full((P, P), MASKV, np.float32)
    am = np.stack([
        np.concatenate([cur, prev], 1),
        np.concatenate([prev, cur], 1),
        np.concatenate([cur, dead], 1),
        np.concatenate([dead, cur], 1),
    ]).astype(np.float32)
    s = np.arange(P)[:, None]
    t = np.arange(P)[None, :]
    hm = ((s // 64 == t // 64) & (s <= t)).astype(np.float32)
    return invf, am, hm


def _prep_shared(inp, NL=2):
    w_in = np.asarray(inp["w_in"], np.float32)
    cols = list(range(1536))
    for h in range(NHH):
        for sec in range(4):
            base = 1536 + sec * 1024 + h * P
            cols.extend(range(base, base + P))
    w_in_p = np.ascontiguousarray(w_in[:, :, cols])
    vec = np.zeros((P, 2 * NVEC), np.float32)
    for l in range(2):
        b = l * NVEC
        vec[:, b + 0:b + 16] = _fm(inp["pre_mix_gain"][l])
        vec[:, b + 16:b + 32] = _fm(inp["pre_ffn_gain"][l])
        vec[:, b + 32:b + 48] = _fm(inp["ple_gain"][l])
        vec[:, b + 48:b + 56] = _fm(inp["attn_out_gain"][l])
        vec[:, b + 56:b + 64] = _fm(inp["hgrn_out_gain"][l])
        vec[:, b + 64:b + 72] = _fm(inp["hgrn_lb_logits"][l])
    invf, am, hm = _host_consts()
    f = lambda k: np.ascontiguousarray(np.asarray(inp[k], np.float32))
    return {
        "w_in": w_in_p, "w_out": f("w_out"), "w_gate": f("w_ffn_gate"), "w_up": f("w_ffn_up"), "w_down": f("w_ffn_down"),
        "w_pg": f("w_ple_gate"), "w_pp": f("w_ple_proj"), "vec_fm": vec,
        "post_mix_gain": f("post_mix_gain"), "post_ffn_gain": f("post_ffn_gain"),
        "sinks": np.ascontiguousarray(np.asarray(inp["attn_sinks"], np.float32).reshape(32)),
        "invf": invf, "amask": am, "hmask": hm,
    }


def _prep_core(inp, b, NTOK):
    x = np.ascontiguousarray(np.asarray(inp["x"], np.float32)[b, :NTOK])
    pT = np.ascontiguousarray(np.transpose(np.asarray(inp["p"], np.float32)[:, b, :NTOK, :], (0, 2, 1)))
    pos = np.asarray(inp["positions"])[b, :NTOK].astype(np.int32)
    posT = np.ascontiguousarray(pos.reshape(-1, P).T)
    return {"x": x, "pT": pT, "posT": posT}


def _prep_core_split(inp, b, half, S):
    H = S // 2
    x = np.ascontiguousarray(np.asarray(inp["x"], np.float32)[b])
    p = np.asarray(inp["p"], np.float32)
    pT = np.ascontiguousarray(np.transpose(p[:, b], (0, 2, 1)))
    pT1 = np.ascontiguousarray(pT[1][:, half * H:(half + 1) * H])
    pos = np.asarray(inp["positions"])[b].astype(np.int32)
    posT = np.ascontiguousarray(pos.reshape(-1, P).T)
    posT1 = np.ascontiguousarray(pos[half * H:(half + 1) * H].reshape(-1, P).T)
    _, am, _ = _host_consts()
    flags = np.array([1.0, 0.0] if half == 0 else [0.0, 1.0], np.float32)
    amask1 = np.ascontiguousarray(am[2] if half == 0 else am[0])
    return {"x": x, "pT": pT, "posT": posT, "pT1": pT1, "posT1": posT1, "flags": flags, "amask1": amask1}


def kernel(**inputs):
    B, S = 4, 4096
    nc = build(S, 2, split=True)
    shared = _prep_shared(inputs)
    in_maps = []
    for c in range(2 * B):
        m = dict(shared)
        m.update(_prep_core_split(inputs, c // 2, c % 2, S))
        in_maps.append(m)
    res = run_bass_kernel_spmd(nc, in_maps, core_ids=list(range(2 * B)))
    H = S // 2
    out = np.empty((B, S, D), np.float32)
    for c in range(2 * B):
        out[c // 2, (c % 2) * H:(c % 2 + 1) * H] = np.asarray(res.results[c]["y"], np.float32)
    return out
```
